# Optimizing a Trainium2 kernel written in Bass

```python
import math
import jax, jax.numpy as jnp
from jax import lax
import numpy as np

D_MODEL = 1024
BATCH = 32
SEQ = 2048
DEPTH = 4

N_META = 16
BLK = 128
WINDOW = 128
HEAD_DIM = 64
A_HEADS = 8
A_KV_HEADS = 2
B_HEADS = 8
C_HEADS = 8
C_Q_RANK = 256
C_KV_RANK = 128
C_NOPE = 64
C_ROPE = 32
C_V = 64
ROPE_THETA = 10000.0
N_BRANCH = 3
BRANCH_W = 512
EPS = 1e-6
NEG = -1e30

A_Q = A_HEADS * HEAD_DIM
A_KV = A_KV_HEADS * HEAD_DIM
B_QKV = B_HEADS * HEAD_DIM
SPLITS = (A_Q, A_KV, A_KV, B_QKV, B_QKV, B_QKV, B_HEADS,
          C_Q_RANK, C_KV_RANK, C_ROPE, N_BRANCH * BRANCH_W, N_BRANCH * D_MODEL)
D_IN = sum(SPLITS)
SPLIT_IDX = tuple(int(v) for v in np.cumsum(SPLITS)[:-1])

kernel_name = 'hybrid_swa_fox_mla_gated_trunk'


def rmsnorm(x, g):
    xf = x.astype(jnp.float32)
    y = xf * lax.rsqrt(jnp.mean(xf * xf, axis=-1, keepdims=True) + EPS)
    return (y * g.astype(jnp.float32)).astype(x.dtype)


def alibi_slopes(n):
    return 2.0 ** (-8.0 * (jnp.arange(n, dtype=jnp.float32) + 1.0) / n)


def rope(x, pos):
    half = x.shape[-1] // 2
    inv = ROPE_THETA ** (-jnp.arange(half, dtype=jnp.float32) / half)
    ang = pos[:, None] * inv[None, :]
    cos = jnp.cos(ang)[None, :, None, :]
    sin = jnp.sin(ang)[None, :, None, :]
    xf = x.astype(jnp.float32)
    x1, x2 = xf[..., :half], xf[..., half:]
    return jnp.concatenate([x1 * cos - x2 * sin, x2 * cos + x1 * sin], axis=-1).astype(x.dtype)


def swa_sink_attention(q, k, v, sinks, valid):
    b, L, hq, d = q.shape
    hkv = k.shape[2]
    grp = hq // hkv
    nb = L // BLK
    qr = q.reshape(b, nb, BLK, hkv, grp, d)

    def band(t):
        tb = t.reshape(b, nb, BLK, hkv, t.shape[-1])
        prev = jnp.concatenate([jnp.zeros_like(tb[:, :1]), tb[:, :-1]], axis=1)
        return jnp.concatenate([prev, tb], axis=2)

    kw, vw = band(k), band(v)
    vb = valid.reshape(nb, BLK)
    vprev = jnp.concatenate([jnp.zeros((1, BLK), dtype=bool), vb[:-1]], axis=0)
    validk = jnp.concatenate([vprev, vb], axis=1)
    dist = BLK + jnp.arange(BLK)[:, None] - jnp.arange(2 * BLK)[None, :]
    mask = ((dist >= 0) & (dist < WINDOW))[None] & validk[:, None, :]
    slopes = alibi_slopes(hq).reshape(hkv, grp)
    s = jnp.einsum('bnqhgd,bnshd->bnhgqs', qr, kw).astype(jnp.float32) * (d ** -0.5)
    s = s - slopes[:, :, None, None] * dist.astype(jnp.float32)
    s = jnp.where(mask[None, :, None, None], s, NEG)
    sink = jnp.broadcast_to(sinks.astype(jnp.float32).reshape(hkv, grp)[None, None, :, :, None, None],
                            s.shape[:-1] + (1,))
    p = jax.nn.softmax(jnp.concatenate([s, sink], axis=-1), axis=-1)[..., :-1].astype(v.dtype)
    o = jnp.einsum('bnhgqs,bnshd->bnqhgd', p, vw)
    return o.reshape(b, L, hq * d)


def dense_causal_attention(q, k, v, valid, scale, log_cum=None):
    b, L, h, _ = q.shape
    nb = L // BLK
    idx = jnp.arange(L)
    fh = None if log_cum is None else jnp.transpose(log_cum, (0, 2, 1))
    outs = []
    for i in range(nb):
        lo, hi = i * BLK, (i + 1) * BLK
        s = jnp.einsum('bqhd,bkhd->bhqk', q[:, lo:hi], k[:, :hi]).astype(jnp.float32) * scale
        if fh is not None:
            s = s + (fh[:, :, lo:hi, None] - fh[:, :, None, :hi])
        mask = (idx[None, :hi] <= idx[lo:hi, None]) & valid[None, :hi]
        s = jnp.where(mask[None, None], s, NEG)
        p = jax.nn.softmax(s, axis=-1).astype(v.dtype)
        outs.append(jnp.einsum('bhqk,bkhd->bqhd', p, v[:, :hi]))
    return jnp.concatenate(outs, axis=1).reshape(b, L, -1)


def hybrid_layer(x, pos, valid, norm_g, w_in, b_f, sinks, q_norm_g, kv_norm_g, w_uq, w_ukv, w_br, w_out):
    b, L, _ = x.shape
    h = rmsnorm(x, norm_g)
    (aq, ak, av, bq, bk, bv, bfl, cq, ckv, ckr, z, g) = jnp.split(h @ w_in, SPLIT_IDX, axis=-1)

    oa = swa_sink_attention(aq.reshape(b, L, A_HEADS, HEAD_DIM),
                            ak.reshape(b, L, A_KV_HEADS, HEAD_DIM),
                            av.reshape(b, L, A_KV_HEADS, HEAD_DIM), sinks, valid)

    logf = jax.nn.log_sigmoid((bfl + b_f).astype(jnp.float32))
    logf = jnp.where(valid[None, :, None], logf, 0.0)
    fcum = jnp.cumsum(logf, axis=1)
    ob = dense_causal_attention(bq.reshape(b, L, B_HEADS, HEAD_DIM),
                                bk.reshape(b, L, B_HEADS, HEAD_DIM),
                                bv.reshape(b, L, B_HEADS, HEAD_DIM),
                                valid, HEAD_DIM ** -0.5, fcum)

    qc = (rmsnorm(cq, q_norm_g) @ w_uq).reshape(b, L, C_HEADS, C_NOPE + C_ROPE)
    kvc = (rmsnorm(ckv, kv_norm_g) @ w_ukv).reshape(b, L, C_HEADS, C_NOPE + C_V)
    k_rope = rope(ckr[:, :, None, :], pos)
    qc = jnp.concatenate([qc[..., :C_NOPE], rope(qc[..., C_NOPE:], pos)], axis=-1)
    kc = jnp.concatenate([kvc[..., :C_NOPE],
                          jnp.broadcast_to(k_rope, (b, L, C_HEADS, C_ROPE))], axis=-1)
    oc = dense_causal_attention(qc, kc, kvc[..., C_NOPE:], valid, (C_NOPE + C_ROPE) ** -0.5)

    zs = jnp.split(z, N_BRANCH, axis=-1)
    gs = jnp.split(g, N_BRANCH, axis=-1)
    branches = (oa, ob, oc)
    y = jax.nn.sigmoid(gs[0]) * ((branches[0] * jax.nn.silu(zs[0])) @ w_br[0])
    for i in range(1, N_BRANCH):
        y = y + jax.nn.sigmoid(gs[i]) * ((branches[i] * jax.nn.silu(zs[i])) @ w_br[i])
    return x + y @ w_out


def setup_inputs(seed: int = 0) -> dict:
    key = jax.random.key(seed)
    ks = jax.random.split(key, 13)
    f32 = jnp.float32
    x = jax.random.normal(ks[0], (BATCH, SEQ, D_MODEL), f32)
    meta_tokens = jax.random.normal(ks[1], (N_META, D_MODEL), f32)
    norm_g = 1.0 + 0.05 * jax.random.normal(ks[2], (DEPTH, D_MODEL), f32)
    w_in = jax.random.normal(ks[3], (DEPTH, D_MODEL, D_IN), f32) * D_MODEL ** -0.5
    b_f = 1.0 + 3.0 * jax.random.uniform(ks[4], (DEPTH, B_HEADS), f32)
    sinks = 0.5 * jax.random.normal(ks[5], (DEPTH, A_HEADS), f32)
    q_norm_g = 1.0 + 0.05 * jax.random.normal(ks[6], (DEPTH, C_Q_RANK), f32)
    kv_norm_g = 1.0 + 0.05 * jax.random.normal(ks[7], (DEPTH, C_KV_RANK), f32)
    w_uq = jax.random.normal(ks[8], (DEPTH, C_Q_RANK, C_HEADS * (C_NOPE + C_ROPE)), f32) * C_Q_RANK ** -0.5
    w_ukv = jax.random.normal(ks[9], (DEPTH, C_KV_RANK, C_HEADS * (C_NOPE + C_V)), f32) * C_KV_RANK ** -0.5
    w_br = jax.random.normal(ks[10], (DEPTH, N_BRANCH, BRANCH_W, D_MODEL), f32) * BRANCH_W ** -0.5
    w_out = jax.random.normal(ks[11], (DEPTH, D_MODEL, D_MODEL), f32) * D_MODEL ** -0.5
    final_norm_g = 1.0 + 0.05 * jax.random.normal(ks[12], (D_MODEL,), f32)
    return {'x': x, 'meta_tokens': meta_tokens, 'norm_g': norm_g, 'w_in': w_in, 'b_f': b_f,
            'sinks': sinks, 'q_norm_g': q_norm_g, 'kv_norm_g': kv_norm_g, 'w_uq': w_uq,
            'w_ukv': w_ukv, 'w_br': w_br, 'w_out': w_out, 'final_norm_g': final_norm_g}


def reference(x, meta_tokens, norm_g, w_in, b_f, sinks, q_norm_g, kv_norm_g, w_uq, w_ukv, w_br, w_out, final_norm_g):
    b = x.shape[0]
    pad = BLK - N_META
    h = jnp.concatenate([jnp.zeros((b, pad, D_MODEL), x.dtype),
                         jnp.broadcast_to(meta_tokens[None].astype(x.dtype), (b, N_META, D_MODEL)),
                         x], axis=1)
    L = h.shape[1]
    idx = jnp.arange(L)
    pos = (idx - pad).astype(jnp.float32)
    valid = idx >= pad
    for l in range(DEPTH):
        h = hybrid_layer(h, pos, valid, norm_g[l], w_in[l], b_f[l], sinks[l], q_norm_g[l],
                         kv_norm_g[l], w_uq[l], w_ukv[l], w_br[l], w_out[l])
    return rmsnorm(h[:, BLK:], final_norm_g)
```

```python
import contextlib
import math
import numpy as np
import concourse.bass as bass
import concourse.mybir as mybir
from concourse.bass_utils import run_bass_kernel_spmd

F32 = mybir.dt.float32
BF16 = mybir.dt.bfloat16
AF = mybir.ActivationFunctionType
ALU = mybir.AluOpType

D = 1024
DEPTH = 4
N_META = 16
HD = 64
D_IN = 7336
AQ, AK, AV, BQ, BK, BV, BFO, CQ, CKV, CKR, ZO, GO = 0, 512, 640, 768, 1280, 1792, 2304, 2312, 2568, 2696, 2728, 4264
EPS = 1e-6
MASKV = -240000.0
ENGINES = ("pe", "act", "dve", "pool", "sp")


class Op:
    __slots__ = ("eng", "fn", "is_dma", "deps", "needs_inc", "tok", "ring")

    def __init__(self, eng, fn, is_dma):
        self.eng = eng
        self.fn = fn
        self.is_dma = is_dma
        self.deps = []
        self.needs_inc = False
        self.tok = None
        self.ring = None


class Sched:
    def __init__(self, nc):
        self.nc = nc
        self.ops = {e: [] for e in ENGINES}
        self.last_w = {}
        self.readers = {}
        self.dma_ring = {"sp": 24, "pool": 12, "act": 4}

    def _add(self, eng, fn, reads, writes, is_dma):
        op = Op(eng, fn, is_dma)
        deps = op.deps
        lw = self.last_w
        rd = self.readers
        for r in reads:
            w = lw.get(r)
            if w is not None:
                deps.append((w, 0))
        for k in writes:
            w = lw.get(k)
            if w is not None:
                deps.append((w, 1))
            lst = rd.get(k)
            if lst:
                for x in lst:
                    deps.append((x, 2))
        for r in reads:
            lst = rd.get(r)
            if lst is None:
                rd[r] = [op]
            else:
                lst.append(op)
        for k in writes:
            lw[k] = op
            rd[k] = []
        self.ops[eng].append(op)
        return op

    def op(self, eng, fn, reads=(), writes=()):
        return self._add(eng, fn, reads, writes, False)

    def dma(self, eng, fn, reads=(), writes=()):
        return self._add(eng, fn, reads, writes, True)

    def emit(self):
        nc = self.nc
        for e in ENGINES:
            for op in self.ops[e]:
                kept = []
                seen = set()
                for (p, kind) in op.deps:
                    if p is op or id(p) in seen:
                        continue
                    if (not p.is_dma) and (not op.is_dma) and p.eng == op.eng:
                        if p.eng == "pe":
                            continue
                    seen.add(id(p))
                    p.needs_inc = True
                    kept.append(p)
                op.deps = kept
        stack = contextlib.ExitStack()
        eng_sem = {e: stack.enter_context(nc.semaphore("c_" + e)) for e in ENGINES}
        ring_sems = {e: [stack.enter_context(nc.semaphore("d_%s%d" % (e, i))) for i in range(n)]
                     for e, n in self.dma_ring.items()}
        final_vals = {}
        for e in ENGINES:
            cnt = 0
            dcnt = 0
            for op in self.ops[e]:
                if op.is_dma:
                    n = self.dma_ring[e]
                    slot, k = dcnt % n, dcnt // n
                    op.ring = (ring_sems[e][slot], 16 * k)
                    op.tok = (ring_sems[e][slot], 16 * (k + 1))
                    final_vals[id(op.tok[0])] = op.tok
                    dcnt += 1
                elif op.needs_inc:
                    cnt += 1
                    op.tok = (eng_sem[e], cnt)
        self.stats = {e: len(self.ops[e]) for e in ENGINES}
        ops = self.ops

        def body(e):
            def run(engh):
                waited = {}
                for op in ops[e]:
                    need = {}
                    if op.is_dma and op.ring[1] > 0:
                        need[id(op.ring[0])] = op.ring
                    for p in op.deps:
                        s, v = p.tok
                        cur = need.get(id(s))
                        if cur is None or cur[1] < v:
                            need[id(s)] = (s, v)
                    for sid, (s, v) in need.items():
                        if waited.get(sid, 0) >= v:
                            continue
                        engh.wait_ge(s, v)
                        waited[sid] = v
                    ins = op.fn(engh)
                    if op.is_dma:
                        ins.then_inc(op.tok[0], 16)
                    elif op.needs_inc:
                        ins.then_inc(op.tok[0], 1)
                if e == "sp":
                    for (s, v) in final_vals.values():
                        if waited.get(id(s), 0) < v:
                            engh.wait_ge(s, v)
                            waited[id(s)] = v
            return run

        with nc.Block() as block:
            block.tensor(body("pe"))
            block.scalar(body("act"))
            block.vector(body("dve"))
            block.gpsimd(body("pool"))
            block.sync(body("sp"))
        stack.close()


def make_consts(NB):
    L = NB * 128
    pad = 128 - N_META
    ident = np.eye(128, dtype=np.float32)
    kk = np.arange(128)[:, None]
    qq = np.arange(128)[None, :]
    cmask = np.where(kk <= qq, 0.0, MASKV).astype(np.float32)
    slopes = 2.0 ** (-8.0 * (np.arange(8) + 1.0) / 8)
    abias = np.zeros((128, 8, 2, 128), np.float32)
    for h in range(8):
        dprev = 128 + qq - kk
        dcur = qq - kk
        abias[:, h, 0, :] = np.where(dprev < 128, -8.0 * slopes[h] * dprev, MASKV)
        abias[:, h, 1, :] = np.where(dcur >= 0, -8.0 * slopes[h] * dcur, MASKV)
    pos = (np.arange(L) - pad).astype(np.float32)
    inv = (10000.0 ** (-np.arange(16, dtype=np.float32) / 16)).astype(np.float32)
    ang = pos[None, :] * inv[:, None]
    cos = np.cos(ang).astype(np.float32)
    sin = np.sin(ang).astype(np.float32)
    rope = np.zeros((128, 2, L), np.float32)
    rope[64:80, 0] = cos
    rope[80:96, 0] = cos
    rope[64:80, 1] = -sin
    rope[80:96, 1] = sin
    return {"c_ident": ident, "c_cmask": cmask, "c_abias": abias, "c_rope": rope}


def build(NSEQ, NLAYER, NB):
    L = NB * 128
    TCH = [(s, min(512, L - s)) for s in range(0, L, 512)]
    nc = bass.Bass("TRN2", target_bir_lowering=False)

    def din(name, shape):
        return nc.dram_tensor(name, list(shape), F32, kind="ExternalInput").ap()

    xin = din("xin", [NSEQ, L, D])
    norm_g = din("norm_g", [DEPTH, D])
    w_in = din("w_in", [DEPTH, D, D_IN])
    b_f = din("b_f", [DEPTH, 8])
    sinks = din("sinks", [DEPTH, 8])
    q_norm_g = din("q_norm_g", [DEPTH, 256])
    kv_norm_g = din("kv_norm_g", [DEPTH, 128])
    w_uq = din("w_uq", [DEPTH, 256, 768])
    w_ukv = din("w_ukv", [DEPTH, 128, 1024])
    w_br = din("w_br", [DEPTH, 3, 512, D])
    w_out = din("w_out", [DEPTH, D, D])
    final_g = din("final_norm_g", [1, D])
    c_ident = din("c_ident", [128, 128])
    c_cmask = din("c_cmask", [128, 128])
    c_abias = din("c_abias", [128, 8, 2, 128])
    c_rope = din("c_rope", [128, 2, L])
    out = nc.dram_tensor("out", [NSEQ, L - 128, D], F32, kind="ExternalOutput").ap()
    xs = nc.dram_tensor("xs_scr", [L, D], F32, kind="Internal").ap()
    ys = nc.dram_tensor("ys_scr", [8, 128, L], F32, kind="Internal").ap()

    st = contextlib.ExitStack()

    def sb(name, shape, dt):
        return st.enter_context(nc.sbuf_tensor(name, list(shape), dt))

    def pst(name, shape, dt):
        return st.enter_context(nc.psum_tensor(name, list(shape), dt))

    hT = sb("hT", [128, 8, L], BF16)
    uT = sb("uT", [128, 4, L], BF16)
    qk = [sb("qk%d" % i, [128, L], BF16) for i in range(4)]
    vp = sb("vp", [128, NB, 2, 65], BF16)
    opair = sb("opair", [128, NB, 128], BF16)
    NPT = 4
    PT = [sb("pt%d" % i, [128, 512], BF16) for i in range(NPT)]
    NWR = 6
    wring = [sb("wr%d" % i, [128, 8, 128], BF16) for i in range(NWR)]
    wbig = sb("wbig", [128, 8, 1024], BF16)
    wuq_t = sb("wuq", [128, 2, 800], BF16)
    wuqrot = sb("wuqrot", [128, 2, 8, 128], BF16)
    wukv_t = sb("wukv", [128, 1024], BF16)
    wkr = sb("wkr", [128, 8, 2, 128], BF16)
    wbf = sb("wbf", [128, 8, 8], BF16)
    S1 = sb("S1", [128, L], F32)
    S2 = sb("S2", [128, L], F32)
    S3 = sb("S3", [128, max(L, 2048)], F32)
    S1b = S1.bitcast(BF16)
    S2b = S2.bitcast(BF16)
    S3b = S3.bitcast(BF16)
    rstdb = sb("rstdb", [128, 512], F32)
    ropeT = sb("ropeT", [128, 2, L], BF16)
    abias = sb("abias", [128, 8, 2, 128], BF16)
    ident = sb("ident", [128, 128], BF16)
    identf = sb("identf", [128, 128], F32)
    onesf = sb("onesf", [128, 128], F32)
    cmask = sb("cmask", [128, 128], BF16)
    gbuf = [sb("gb%d" % i, [128, D], F32) for i in range(2)]
    negbf = sb("negbf", [8, DEPTH], F32)
    esink = sb("esink", [128, DEPTH * 8], F32)
    gq = sb("gq", [128, DEPTH, 2], F32)
    gkv = sb("gkv", [128, DEPTH], F32)
    one_c = sb("one_c", [128, 1], F32)
    xt = [sb("xt%d" % i, [128, D], F32) for i in range(2)]
    hb = [sb("hb%d" % i, [128, D], BF16) for i in range(3)]
    st_ss = [sb("ss%d" % i, [128, 1], F32) for i in range(3)]
    st_ms = [sb("ms%d" % i, [128, 1], F32) for i in range(3)]
    st_rs = [sb("rs%d" % i, [128, 1], F32) for i in range(3)]
    tnh = [sb("tnh%d" % i, [128, 512], BF16) for i in range(2)]
    tmpf = [sb("tmpf%d" % i, [128, 512], F32) for i in range(2)]
    yold = [sb("yold%d" % i, [128, 512], F32) for i in range(2)]
    yblk = [sb("yblk%d" % i, [128, 8, 128], BF16) for i in range(2)]
    GT = sb("GT", [128, NB * 8], F32)
    den = [sb("den%d" % i, [128, 1], F32) for i in range(4)]
    den8 = [sb("den8_%d" % i, [128, 8], F32) for i in range(4)]
    eps_c = sb("eps_c", [128, 1], F32)
    osb = [sb("osb%d" % i, [128, 455], F32) for i in range(3)]

    pj = [pst("pj%d" % i, [128, 512], F32) for i in range(2)]
    stp = [pst("st%d" % i, [128, 512], F32) for i in range(2)]
    ob = [pst("ob%d" % i, [128, 512], F32) for i in range(3)]
    trb = pst("trb", [128, 1024], BF16)
    trf = trb.bitcast(F32)

    S = Sched(nc)
    cnt = {"pj": 0, "st": 0, "pt": 0, "wr": 0, "tnh": 0, "tmpf": 0, "yold": 0, "yblk": 0, "xt": 0, "den": 0,
           "rtm": 0, "evac": 0, "trs": 0, "st4": 0}

    def nxt(name, n):
        v = cnt[name] % n
        cnt[name] += 1
        return v

    def evac_eng():
        cnt["evac"] += 1
        return "act" if cnt["evac"] % 2 == 0 else "dve"

    def copy_op(eng, out_ap, in_ap, reads, writes):
        if eng == "act":
            S.op("act", lambda e: e.copy(out=out_ap, in_=in_ap), reads, writes)
        else:
            S.op("dve", lambda e: e.tensor_copy(out=out_ap, in_=in_ap), reads, writes)

    def mm(out_ap, lhsT, rhs, start, stop, reads, writes):
        S.op("pe", lambda e: e.matmul(out_ap, lhsT=lhsT, rhs=rhs, start=start, stop=stop, skip_group_check=True),
             reads, writes)

    TRBK = ["trb"]

    def blocks_of(s0, n):
        return range(s0 // 128, (s0 + n) // 128)

    S.dma("pool", lambda e: e.dma_start(out=ident[:], in_=c_ident), writes=["ident"])
    S.dma("pool", lambda e: e.dma_start(out=cmask[:], in_=c_cmask), writes=["cmask"])
    S.dma("pool", lambda e: e.dma_start(out=abias[:], in_=c_abias), writes=["abias"])
    S.dma("pool", lambda e: e.dma_start(out=ropeT[:], in_=c_rope), writes=["ropeT"])
    S.dma("sp", lambda e: e.dma_start(out=identf[:], in_=c_ident), writes=["identf"])
    S.op("pool", lambda e: e.memset(onesf[:], 1.0), writes=["onesf"])
    S.op("pool", lambda e: e.memset(one_c[:], 1.0), writes=["one_c"])
    S.op("pool", lambda e: e.memset(eps_c[:], EPS), writes=["eps_c"])
    S.op("pool", lambda e: e.memset(vp[:, :, :, 64:65], 1.0), writes=["vp_ones"])
    S.op("pool", lambda e: e.memset(vp[0:128 - N_META, 0:1, :, 64:65], 0.0), reads=[], writes=["vp_ones"])
    for _i in range(4):
        S.op("pool", (lambda _i: lambda e: e.memset(qk[_i][:], 0.0))(_i),
             writes=[("qk", _i, b) for b in range(NB)] + [("qk", _i, "aug")])
    S.op("pool", lambda e: e.memset(wuqrot[:], 0.0), writes=["wuqrot"])
    S.op("pool", lambda e: e.memset(wuq_t[:], 0.0), writes=["wuq"])
    S.op("pool", lambda e: e.memset(wkr[:], 0.0), writes=["wkr"])
    with nc.allow_non_contiguous_dma(reason="tiny param loads"):
        S.dma("sp", lambda e: e.dma_start(out=negbf[:], in_=b_f.rearrange("l h -> h l")), writes=["negbf"])
        S.dma("sp", lambda e: e.dma_start(out=gq[:], in_=q_norm_g.rearrange("l (c p) -> p l c", p=128)),
              writes=["gq"])
        S.dma("sp", lambda e: e.dma_start(out=gkv[:], in_=kv_norm_g.rearrange("l p -> p l")), writes=["gkv"])
    S.dma("sp", lambda e: e.dma_start(out=esink[:], in_=sinks.rearrange("l h -> (l h)").partition_broadcast(128)),
          writes=["esink"])
    S.op("dve", lambda e: e.tensor_scalar(out=negbf[:], in0=negbf[:], scalar1=-1.0, scalar2=None, op0=ALU.mult),
         reads=["negbf"], writes=["negbf"])
    S.op("act", lambda e: e.activation(out=esink[:], in_=esink[:], func=AF.Exp), reads=["esink"], writes=["esink"])

    def layer_specs(l):
        sp = []
        for c in range(4):
            sp.append((l, [(ZO + 0 * 512 + c * 128, 128)]))
        for c in range(4):
            g = c // 2
            sp.append((l, [(AQ + c * 128, 128)]))
            sp.append((l, [(AK + g * 64, 64), (AK + g * 64, 64)]))
            sp.append((l, [(AV + g * 64, 64), (AV + g * 64, 64)]))
        for m in range(8):
            sp.append((l, [(GO + 0 * 1024 + m * 128, 128)]))
        for c in range(4):
            sp.append((l, [(ZO + 1 * 512 + c * 128, 128)]))
        for c in range(4):
            sp.append((l, [(BQ + c * 128, 128)]))
            sp.append((l, [(BK + c * 128, 128)]))
            sp.append((l, [(BV + c * 128, 128)]))
        for m in range(8):
            sp.append((l, [(GO + 1 * 1024 + m * 128, 128)]))
        for c in range(4):
            sp.append((l, [(ZO + 2 * 512 + c * 128, 128)]))
        for i in range(2):
            sp.append((l, [(CQ + i * 128, 128)]))
        sp.append((l, [(CKV, 128)]))
        for m in range(8):
            sp.append((l, [(GO + 2 * 1024 + m * 128, 128)]))
        return sp

    WSPECS = []
    for _s in range(NSEQ):
        for _l in range(NLAYER):
            WSPECS.extend(layer_specs(_l))
    wstate = {"issued": 0, "used": 0}
    WPF = NWR - 3

    def _issue_chunk(idx):
        l, col_runs = WSPECS[idx]
        slot = idx % NWR
        t = wring[slot]
        key = ("wr", slot)
        o = 0
        for (c0, n) in col_runs:
            src = w_in[l, :, c0:c0 + n].rearrange("(kc p) n -> p kc n", p=128)
            dst = t[:, :, o:o + n]
            S.dma("pool", (lambda dst, src: lambda e: e.dma_start(out=dst, in_=src))(dst, src), writes=[key])
            o += n

    def wload_chunk(l, col_runs):
        idx = wstate["used"]
        assert WSPECS[idx] == (l, col_runs), (idx, WSPECS[idx], l, col_runs)
        while wstate["issued"] < min(len(WSPECS), idx + 1 + WPF):
            _issue_chunk(wstate["issued"])
            wstate["issued"] += 1
        wstate["used"] += 1
        slot = idx % NWR
        return wring[slot], ("wr", slot)

    def proj_fm(wt, wkey, M, evac_fn, extra_reads=()):
        for (s0, n) in TCH:
            pi = nxt("pj", 2)
            p = pj[pi]
            pkey = ("pj", pi)
            hkeys = [("hT", b) for b in blocks_of(s0, n)]
            for k in range(8):
                mm(p[0:M, 0:n], wt[:, k, 0:M], hT[:, k, s0:s0 + n], k == 0, k == 7,
                   [wkey] + hkeys + list(extra_reads), [pkey])
            evac_fn(p, pkey, s0, n)

    def norm_stats(xtile, xkey, b, gtile, gkey, last, sidx):
        i = b % 3
        hbt = hb[i]
        S.op("act", lambda e: e.activation(out=hbt[:], in_=xtile[:], func=AF.Square, accum_out=st_ss[i][:]),
             reads=[xkey], writes=[("hb", i), ("ss", i)])
        S.op("act", lambda e: e.activation(out=st_ms[i][:], in_=st_ss[i][:], func=AF.Ln, bias=eps_c[:], scale=1.0 / D),
             reads=[("ss", i), "eps_c"], writes=[("ms", i)])
        S.op("act", lambda e: e.activation(out=st_rs[i][:], in_=st_ms[i][:], func=AF.Exp, scale=-0.5),
             reads=[("ms", i)], writes=[("rs", i)])
        if last:
            if b == 0:
                return
            S.op("dve", lambda e: e.scalar_tensor_tensor(out=xtile[:], in0=xtile[:], scalar=st_rs[i][:], in1=gtile[:],
                                                         op0=ALU.mult, op1=ALU.mult),
                 reads=[xkey, ("rs", i), gkey], writes=[xkey])
            S.dma("sp", lambda e: e.dma_start(out=out[sidx, (b - 1) * 128:b * 128, :], in_=xtile[:]), reads=[xkey])
            return
        S.op("dve", lambda e: e.scalar_tensor_tensor(out=hbt[:], in0=xtile[:], scalar=st_rs[i][:], in1=gtile[:],
                                                     op0=ALU.mult, op1=ALU.mult),
             reads=[xkey, ("rs", i), gkey], writes=[("hb", i)])

    def norm_transposes(b):
        i = b % 3
        hbt = hb[i]
        for c in range(8):
            S.op("pe", (lambda c: lambda e: e.transpose(out=trb[:, c * 128:(c + 1) * 128],
                                                        in_=hbt[:, c * 128:(c + 1) * 128], identity=ident[:]))(c),
                 reads=[("hb", i), "ident"], writes=TRBK)
        copy_op(evac_eng(), hT[:, :, b * 128:(b + 1) * 128], trb[:].rearrange("p (c t) -> p c t", c=8),
                TRBK, [("hT", b)])

    def norm_block(xtile, xkey, b, gtile, gkey, last, sidx):
        norm_stats(xtile, xkey, b, gtile, gkey, last, sidx)
        if not last:
            norm_transposes(b)

    def load_gb(which, src_row):
        S.dma("sp", lambda e: e.dma_start(out=gbuf[which][:], in_=src_row.partition_broadcast(128)),
              writes=[("gb", which)])

    SKEW = 2

    def st_next():
        i = nxt("st4", 4)
        return [(stp[0], ("st", 0)), (stp[1], ("st", 1)), (pj[0], ("pj", 0)), (pj[1], ("pj", 1))][i]

    def run_pipeline(stages, qk_stage, pv_stage):
        pend = []
        for stg in stages:
            pend.append(qk_stage(stg))
            if len(pend) > SKEW:
                pv_stage(*pend.pop(0))
        while pend:
            pv_stage(*pend.pop(0))

    def o_ap(qb):
        return ob[qb // 7][:, (qb % 7) * 65:(qb % 7) * 65 + 65], ("ob", qb // 7)

    def normalize(oap, okey, qb, j, sink_ap=None):
        di = nxt("den", 4)
        d = den[di]
        if sink_ap is None:
            S.op("dve", lambda e: e.tensor_scalar(out=d[:], in0=oap[:, 64:65], scalar1=1e-30, scalar2=None,
                                                  op0=ALU.max), reads=[okey], writes=[("den", di)])
        else:
            S.op("dve", lambda e: e.tensor_scalar(out=d[:], in0=oap[:, 64:65], scalar1=sink_ap, scalar2=None,
                                                  op0=ALU.add), reads=[okey, "esink"], writes=[("den", di)])
        S.op("dve", lambda e: e.reciprocal(out=d[:], in_=d[:]), reads=[("den", di)], writes=[("den", di)])
        S.op("dve", lambda e: e.tensor_scalar(out=opair[:, qb, j * 64:(j + 1) * 64], in0=oap[:, 0:64],
                                              scalar1=d[:], scalar2=None, op0=ALU.mult),
             reads=[okey, ("den", di)], writes=[("opair", qb, j)])

    def attn_dense(j, Kd, scale, bias_fn):
        qt, kt = qk[j], qk[2 + j]
        qkeys = lambda s0, n: [("qk", j, b) for b in blocks_of(s0, n)] + [("qk", j, "aug")]
        stages = []
        for kb in range(NB):
            q0 = kb * 128
            for ci, s0 in enumerate(range(q0, L, 512)):
                stages.append((kb, ci, s0, min(512, L - s0)))

        def qk_stage(stg):
            kb, ci, s0, n = stg
            q0 = kb * 128
            sp_, skey = st_next()
            mm(sp_[:, 0:n], kt[0:Kd, q0:q0 + 128], qt[0:Kd, s0:s0 + n], True, ci != 0,
               [("qk", 2 + j, kb), ("qk", 2 + j, "aug")] + qkeys(s0, n), [skey])
            if ci == 0:
                mm(sp_[:, 0:128], ident[:], cmask[:], False, True, ["ident", "cmask"], [skey])
            return stg, sp_, skey

        def pv_stage(stg, sp_, skey):
            kb, ci, s0, n = stg
            pi = nxt("pt", NPT)
            pt = PT[pi]
            pkey = ("pt", pi)
            bias_ap, bias_keys = bias_fn(kb)
            if bias_ap is None:
                S.op("act", (lambda pt, sp_, n: lambda e: e.activation(out=pt[:, 0:n], in_=sp_[:, 0:n],
                                                                       func=AF.Exp, scale=scale))(pt, sp_, n),
                     reads=[skey], writes=[pkey])
            else:
                S.op("act", (lambda pt, sp_, n, bias_ap: lambda e: e.activation(
                    out=pt[:, 0:n], in_=sp_[:, 0:n], func=AF.Exp, bias=bias_ap, scale=scale))(pt, sp_, n, bias_ap),
                    reads=[skey] + bias_keys, writes=[pkey])
            for qi, qb in enumerate(blocks_of(s0, n)):
                oap, okey = o_ap(qb)
                mm(oap, pt[:, qi * 128:(qi + 1) * 128], vp[:, kb, j, :], kb == 0 and qb % 7 == 0, kb == qb,
                   [pkey, ("vp", kb), "vp_ones"], [okey])

        run_pipeline(stages, qk_stage, pv_stage)
        dense_finish(j)

    def dense_finish(j):
        for bank in range((NB + 6) // 7):
            nq = min(7, NB - bank * 7)
            ncol = nq * 65
            copy_op(evac_eng(), osb[bank][:, 0:ncol], ob[bank][:, 0:ncol], [("ob", bank)], [("osb", bank)])
            ov = osb[bank][:, 0:ncol].rearrange("p (q d) -> p q d", d=65)
            di = nxt("den", 4)
            d = den8[di]
            S.op("dve", (lambda d, ov, nq: lambda e: e.tensor_scalar(out=d[:, 0:nq], in0=ov[:, :, 64], scalar1=1e-30,
                                                                    scalar2=None, op0=ALU.max))(d, ov, nq),
                 reads=[("osb", bank)], writes=[("den8", di)])
            S.op("dve", (lambda d, nq: lambda e: e.reciprocal(out=d[:, 0:nq], in_=d[:, 0:nq]))(d, nq),
                 reads=[("den8", di)], writes=[("den8", di)])
            q0 = bank * 7
            S.op("dve", (lambda d, ov, nq, q0: lambda e: e.tensor_tensor(
                out=opair[:, q0:q0 + nq, j * 64:(j + 1) * 64], in0=ov[:, :, 0:64],
                in1=d[:, 0:nq].unsqueeze(2).broadcast_to([128, nq, 64]), op=ALU.mult))(d, ov, nq, q0),
                reads=[("osb", bank), ("den8", di)], writes=[("opair", q0 + q, j) for q in range(nq)])

    def attn_swa(j, h, l):
        qt, kt = qk[j], qk[2 + j]

        def qk_stage(qb):
            sp_, skey = st_next()
            parts = []
            if qb >= 1:
                parts.append((qb - 1, 0))
            parts.append((qb, 1))
            for pi_, (kb, which) in enumerate(parts):
                col = pi_ * 128
                mm(sp_[:, col:col + 128], kt[:, kb * 128:(kb + 1) * 128], qt[:, qb * 128:(qb + 1) * 128],
                   True, False, [("qk", 2 + j, kb), ("qk", 2 + j, "aug"), ("qk", j, qb), ("qk", j, "aug")], [skey])
                mm(sp_[:, col:col + 128], ident[:], abias[:, h, which, :], False, True, ["ident", "abias"], [skey])
            return qb, parts, sp_, skey

        def pv_stage(qb, parts, sp_, skey):
            n = 128 * len(parts)
            pi = nxt("pt", NPT)
            pt = PT[pi]
            pkey = ("pt", pi)
            S.op("act", (lambda pt, sp_, n: lambda e: e.activation(out=pt[:, 0:n], in_=sp_[:, 0:n], func=AF.Exp,
                                                                   scale=0.125))(pt, sp_, n),
                 reads=[skey], writes=[pkey])
            oap, okey = ob[qb % 3][:, 0:65], ("ob", qb % 3)
            for pi_, (kb, which) in enumerate(parts):
                mm(oap, pt[:, pi_ * 128:(pi_ + 1) * 128], vp[:, kb, j, :], pi_ == 0, pi_ == len(parts) - 1,
                   [pkey, ("vp", kb), "vp_ones"], [okey])
            normalize(oap, okey, qb, j, esink[:, l * 8 + h:l * 8 + h + 1])

        run_pipeline(list(range(NB)), qk_stage, pv_stage)

    def finish_pair(c):
        banks = [(trb, "trb"), (pj[0].bitcast(BF16), ("pj", 0)), (pj[1].bitcast(BF16), ("pj", 1))]
        for g0 in range(0, NB, 8):
            gn = min(8, NB - g0)
            bi = nxt("trs", 3)
            bt, bkey = banks[bi]
            for qi in range(gn):
                qb = g0 + qi
                S.op("pe", (lambda qb, qi, bt: lambda e: e.transpose(out=bt[:, qi * 128:(qi + 1) * 128],
                                                                   in_=opair[:, qb, :], identity=ident[:]))(qb, qi, bt),
                     reads=[("opair", qb, 0), ("opair", qb, 1), "ident"], writes=[bkey])
            S.op("dve", (lambda g0, gn, bt: lambda e: e.tensor_tensor(
                out=uT[:, c, g0 * 128:(g0 + gn) * 128], in0=bt[:, 0:gn * 128],
                in1=uT[:, c, g0 * 128:(g0 + gn) * 128], op=ALU.mult))(g0, gn, bt),
                reads=[bkey] + [("uT", c, g0 + q) for q in range(gn)], writes=[("uT", c, g0 + q) for q in range(gn)])

    def z_proj(l, i):
        for c in range(4):
            wt, wkey = wload_chunk(l, [(ZO + i * 512 + c * 128, 128)])

            def ev(p, pkey, s0, n, c=c):
                ti = nxt("tnh", 2)
                t = tnh[ti]
                S.op("act", lambda e: e.activation(out=t[:, 0:n], in_=p[:, 0:n], func=AF.Tanh, scale=0.5),
                     reads=[pkey], writes=[("tnh", ti)])
                S.op("dve", lambda e: e.scalar_tensor_tensor(out=uT[:, c, s0:s0 + n], in0=t[:, 0:n], scalar=1.0,
                                                             in1=p[:, 0:n], op0=ALU.add, op1=ALU.mult),
                     reads=[("tnh", ti), pkey], writes=[("uT", c, b) for b in blocks_of(s0, n)])
            proj_fm(wt, wkey, 128, ev)

    def load_wbr(l, i):
        src = w_br[l, i].rearrange("(c p) n -> p c n", p=128)
        S.dma("pool", lambda e: e.dma_start(out=wbig[:, 0:4, :], in_=src), writes=[("wbig", 0)])

    def branch_final(l, i, first):
        its = [(m, s0, n) for m in range(8) for (s0, n) in TCH]
        yolds = {}

        def issue_yold(k):
            if first or k >= len(its):
                return
            m, s0, n = its[k]
            yi = nxt("yold", 2)
            yo = yold[yi]
            S.dma("sp", (lambda yo, m, s0, n: lambda e: e.dma_start(out=yo[:, 0:n], in_=ys[m, :, s0:s0 + n]))(yo, m, s0, n),
                  reads=[("ys", m, s0)], writes=[("yold", yi)])
            yolds[k] = (yo, yi)

        issue_yold(0)
        wt = wkey = None
        for k, (m, s0, n) in enumerate(its):
            if s0 == 0:
                wt, wkey = wload_chunk(l, [(GO + i * 1024 + m * 128, 128)])
            issue_yold(k + 1)
            pi = nxt("pj", 2)
            p = pj[pi]
            pkey = ("pj", pi)
            hkeys = [("hT", b) for b in blocks_of(s0, n)]
            for kk in range(8):
                mm(p[:, 0:n], wt[:, kk, :], hT[:, kk, s0:s0 + n], kk == 0, kk == 7, [wkey] + hkeys, [pkey])
            ti = nxt("tnh", 2)
            t = tnh[ti]
            S.op("act", (lambda t, p, n: lambda e: e.activation(out=t[:, 0:n], in_=p[:, 0:n], func=AF.Tanh,
                                                                scale=0.5))(t, p, n),
                 reads=[pkey], writes=[("tnh", ti)])
            pi2 = nxt("pj", 2)
            p2 = pj[pi2]
            pkey2 = ("pj", pi2)
            for c in range(4):
                mm(p2[:, 0:n], wbig[:, c, m * 128:(m + 1) * 128], uT[:, c, s0:s0 + n], c == 0, c == 3,
                   [("wbig", 0)] + [("uT", c, b) for b in blocks_of(s0, n)], [pkey2])
            fi = nxt("tmpf", 2)
            tf = tmpf[fi]
            S.op("dve", (lambda tf, t, p2, n: lambda e: e.scalar_tensor_tensor(
                out=tf[:, 0:n], in0=t[:, 0:n], scalar=1.0, in1=p2[:, 0:n], op0=ALU.add, op1=ALU.mult))(tf, t, p2, n),
                reads=[("tnh", ti), pkey2], writes=[("tmpf", fi)])
            ykey = ("ys", m, s0)
            ydst = ys[m, :, s0:s0 + n]
            if not first:
                yo, yi = yolds.pop(k)
                S.op("dve", (lambda tf, yo, n: lambda e: e.tensor_tensor(out=tf[:, 0:n], in0=tf[:, 0:n],
                                                                         in1=yo[:, 0:n], op=ALU.add))(tf, yo, n),
                     reads=[("tmpf", fi), ("yold", yi)], writes=[("tmpf", fi)])
            S.dma("sp", (lambda tf, ydst, n: lambda e: e.dma_start(out=ydst, in_=tf[:, 0:n]))(tf, ydst, n),
                  reads=[("tmpf", fi)], writes=[ykey])

    def load_wout_hi(l):
        src = w_out[l, 512:1024, :].rearrange("(c p) n -> p c n", p=128)
        S.dma("pool", lambda e: e.dma_start(out=wbig[:, 4:8, :], in_=src), writes=[("wbig", 1)])

    def wout_phase(l, sidx):
        last = (l == NLAYER - 1)
        src = w_out[l, 0:512, :].rearrange("(c p) n -> p c n", p=128)
        S.dma("pool", lambda e: e.dma_start(out=wbig[:, 0:4, :], in_=src), writes=[("wbig", 0)])
        gi = (l + 1) % 2
        load_gb(gi, final_g[0] if last else norm_g[l + 1])
        xsrc = xin[sidx] if l == 0 else xs
        pending = []
        for b in range(NB):
            yi = nxt("yblk", 2)
            yb = yblk[yi]
            S.dma("pool", (lambda yb, b: lambda e: e.dma_start(
                out=yb[:], in_=ys[:, :, b * 128:(b + 1) * 128].rearrange("m p t -> p m t")))(yb, b),
                reads=[("ys", m, s0) for m in range(8) for (s0, n) in TCH if s0 <= b * 128 < s0 + n],
                writes=[("yblk", yi)])
            xi = nxt("xt", 2)
            x = xt[xi]
            xkey = ("xt", xi)
            S.dma("sp", (lambda x, b: lambda e: e.dma_start(out=x[:], in_=xsrc[b * 128:(b + 1) * 128, :]))(x, b),
                  reads=[("xs", b)] if l > 0 else [], writes=[xkey])
            for half in range(2):
                pi = nxt("pj", 2)
                p = pj[pi]
                pkey = ("pj", pi)
                order = [4, 5, 6, 7, 0, 1, 2, 3]
                for oi, m in enumerate(order):
                    mm(p[:, :], yb[:, m, :], wbig[:, m, half * 512:(half + 1) * 512], oi == 0, oi == 7,
                       [("yblk", yi), ("wbig", m // 4)], [pkey])
                S.op("dve", (lambda x, p, half: lambda e: e.scalar_tensor_tensor(
                    out=x[:, half * 512:(half + 1) * 512], in0=p[:, :], scalar=0.25,
                    in1=x[:, half * 512:(half + 1) * 512], op0=ALU.mult, op1=ALU.add))(x, p, half),
                    reads=[pkey, xkey], writes=[xkey])
            if not last:
                S.dma("sp", (lambda x, b: lambda e: e.dma_start(out=xs[b * 128:(b + 1) * 128, :], in_=x[:]))(x, b),
                      reads=[xkey], writes=[("xs", b)])
            norm_stats(x, xkey, b, gbuf[gi], ("gb", gi), last, sidx)
            pending.append(b)
            if len(pending) > 2 and not last:
                norm_transposes(pending.pop(0))
        while pending and not last:
            norm_transposes(pending.pop(0))

    def first_norm(sidx):
        load_gb(0, norm_g[0])
        for b in range(NB):
            xi = nxt("xt", 2)
            x = xt[xi]
            xkey = ("xt", xi)
            S.dma("sp", (lambda x, b: lambda e: e.dma_start(out=x[:], in_=xin[sidx, b * 128:(b + 1) * 128, :]))(x, b),
                  writes=[xkey])
            norm_block(x, xkey, b, gbuf[0], ("gb", 0), False, sidx)

    S1K = ["S1"]
    S2K = ["S2"]
    S3K = ["S3"]

    def evac_pair(p, pkey, s0, n, t0, k0, t1, k1, rows=64):
        copy_op("act", t0[0:rows, s0:s0 + n], p[0:rows, 0:n], [pkey], [(k0[0], k0[1], b) for b in blocks_of(s0, n)])
        copy_op("dve", t1[0:rows, s0:s0 + n], p[64:64 + rows, 0:n], [pkey],
                [(k1[0], k1[1], b) for b in blocks_of(s0, n)])

    def v_proj_tm(wt, wkey, lhs_fn, lhs_keys_fn, nk):
        for b in range(NB):
            pi = nxt("pj", 2)
            p = pj[pi]
            pkey = ("pj", pi)
            for k in range(nk):
                mm(p[:, 0:128], lhs_fn(k, b), wt(k), k == 0, k == nk - 1, [wkey] + lhs_keys_fn(b), [pkey])
            copy_op(evac_eng(), vp[:, b, :, 0:64], p[:, 0:128].rearrange("p (j d) -> p j d", j=2), [pkey],
                    [("vp", b)])

    def mixer_A(l, first=True):
        for _i in range(2):
            S.op("pool", (lambda _i: lambda e: e.memset(qk[_i][64:128, :], 0.0))(_i), writes=[("qk", _i, "aug")])
        z_proj(l, 0)
        for c in range(4):
            g = c // 2
            wq_, wqk = wload_chunk(l, [(AQ + c * 128, 128)])
            proj_fm(wq_, wqk, 128, lambda p, pkey, s0, n: evac_pair(p, pkey, s0, n, qk[0], ("qk", 0), qk[1], ("qk", 1)))
            wk_, wkk = wload_chunk(l, [(AK + g * 64, 64), (AK + g * 64, 64)])
            proj_fm(wk_, wkk, 128, lambda p, pkey, s0, n: evac_pair(p, pkey, s0, n, qk[2], ("qk", 2), qk[3], ("qk", 3)))
            wv_, wvk = wload_chunk(l, [(AV + g * 64, 64), (AV + g * 64, 64)])
            v_proj_tm(lambda k: wv_[:, k, :], wvk, lambda k, b: hT[:, k, b * 128:(b + 1) * 128],
                      lambda b: [("hT", b)], 8)
            for j in range(2):
                attn_swa(j, 2 * c + j, l)
            finish_pair(c)
            if c == 1:
                load_wbr(l, 0)
        branch_final(l, 0, first)

    def fox_gates(l):
        src = w_in[l, :, BFO:BFO + 8].rearrange("(kc p) n -> p kc n", p=128)
        with nc.allow_non_contiguous_dma(reason="8-col gate weights"):
            S.dma("pool", lambda e: e.dma_start(out=wbf[:], in_=src), writes=["wbf"])
        for (s0, n) in TCH:
            pi = nxt("pj", 2)
            p = pj[pi]
            pkey = ("pj", pi)
            for k in range(8):
                mm(p[0:8, 0:n], wbf[:, k, :], hT[:, k, s0:s0 + n], k == 0, k == 7,
                   ["wbf"] + [("hT", b) for b in blocks_of(s0, n)], [pkey])
            S.op("act", (lambda p, s0, n: lambda e: e.activation(out=S1[0:8, s0:s0 + n], in_=p[0:8, 0:n], func=AF.Exp,
                                                                 bias=negbf[:, l:l + 1], scale=-1.0))(p, s0, n),
                 reads=[pkey, "negbf"], writes=S1K)
        S.op("act", lambda e: e.activation(out=S1[0:8, :], in_=S1[0:8, :], func=AF.Ln, bias=one_c[0:8, :], scale=1.0),
             reads=S1K + ["one_c"], writes=S1K)
        S.op("dve", lambda e: e.tensor_tensor_scan(out=S2[0:8, :], data0=S1[0:8, :], data1=S1[0:8, :], initial=0.0,
                                                   op0=ALU.add, op1=ALU.max), reads=S1K, writes=S2K)
        fq = S3b[0:8, 0:2 * L].rearrange("p (t l) -> p t l", t=2)
        S.op("dve", lambda e: e.tensor_scalar(out=fq[:, 0, :], in0=S2[0:8, :], scalar1=-8.0, scalar2=None,
                                              op0=ALU.mult), reads=S2K, writes=S3K)
        S.op("dve", lambda e: e.scalar_tensor_tensor(out=fq[:, 1, :], in0=S2[0:8, :], scalar=-8.0, in1=fq[:, 0, :],
                                                     op0=ALU.mult, op1=ALU.subtract), reads=S2K + S3K, writes=S3K)
        for b in range(NB):
            S.op("pe", (lambda b: lambda e: e.transpose(out=trf[:, b * 8:(b + 1) * 8], in_=S2[0:8, b * 128:(b + 1) * 128],
                                                        identity=identf[0:8, 0:8]))(b),
                 reads=S2K + ["identf"], writes=TRBK)
        S.op("dve", lambda e: e.tensor_copy(out=GT[:], in_=trf[:, 0:NB * 8]), reads=TRBK, writes=["GT"])
        return fq

    def mixer_B(l, first=False):
        fq = fox_gates(l)
        for _i in range(4):
            S.op("pool", (lambda _i: lambda e: e.memset(qk[_i][64:96, :], 0.0))(_i), writes=[("qk", _i, "aug")])
        z_proj(l, 1)
        for c in range(4):
            wq_, wqk = wload_chunk(l, [(BQ + c * 128, 128)])
            proj_fm(wq_, wqk, 128, lambda p, pkey, s0, n: evac_pair(p, pkey, s0, n, qk[0], ("qk", 0), qk[1], ("qk", 1)))
            wk_, wkk = wload_chunk(l, [(BK + c * 128, 128)])
            proj_fm(wk_, wkk, 128, lambda p, pkey, s0, n: evac_pair(p, pkey, s0, n, qk[2], ("qk", 2), qk[3], ("qk", 3)))
            wv_, wvk = wload_chunk(l, [(BV + c * 128, 128)])
            v_proj_tm(lambda k: wv_[:, k, :], wvk, lambda k, b: hT[:, k, b * 128:(b + 1) * 128],
                      lambda b: [("hT", b)], 8)
            for j in range(2):
                h = 2 * c + j
                for t_ in range(2):
                    S.dma("sp", (lambda j, h, t_: lambda e: e.dma_start(out=qk[j][64 + t_:65 + t_, :],
                                                                        in_=fq[h:h + 1, t_, :]))(j, h, t_),
                          reads=S3K, writes=[("qk", j, "aug")])
                S.op("pool", (lambda j: lambda e: e.memset(qk[2 + j][64:66, :], 1.0))(j), writes=[("qk", 2 + j, "aug")])
            for j in range(2):
                h = 2 * c + j
                attn_dense(j, 128, 0.125,
                           lambda kb, h=h: (GT[:, kb * 8 + h:kb * 8 + h + 1], ["GT"]))
            finish_pair(c)
            if c == 1:
                load_wbr(l, 1)
        branch_final(l, 1, first)

    def mixer_C(l, first=False):
        z_proj(l, 2)
        S.dma("pool", lambda e: e.dma_start(out=wuq_t[:, :, 0:768], in_=w_uq[l].rearrange("(c p) n -> p c n", p=128)),
              writes=["wuq"])
        S.dma("pool", lambda e: e.dma_start(out=wukv_t[:], in_=w_ukv[l]), writes=["wukv"])
        uqv = w_uq[l].rearrange("(c p) (h r) -> p c h r", p=128, r=96)
        with nc.allow_non_contiguous_dma(reason="rope column permutation"):
            for c2 in range(2):
                S.dma("pool", (lambda c2: lambda e: e.dma_start(out=wuqrot[:, c2, :, 64:80], in_=uqv[:, c2, :, 80:96]))(c2),
                      writes=["wuqrot"])
                S.dma("pool", (lambda c2: lambda e: e.dma_start(out=wuqrot[:, c2, :, 80:96], in_=uqv[:, c2, :, 64:80]))(c2),
                      writes=["wuqrot"])
            krv = w_in[l].rearrange("(kc p) n -> p kc n", p=128)
            S.dma("pool", lambda e: e.dma_start(out=wkr[:, :, 0, 64:96], in_=krv[:, :, CKR:CKR + 32]), writes=["wkr"])
            S.dma("pool", lambda e: e.dma_start(out=wkr[:, :, 1, 64:80], in_=krv[:, :, CKR + 16:CKR + 32]),
                  writes=["wkr"])
            S.dma("pool", lambda e: e.dma_start(out=wkr[:, :, 1, 80:96], in_=krv[:, :, CKR:CKR + 16]), writes=["wkr"])
        cqn = S1b[:, 0:2 * L].rearrange("p (c l) -> p c l", c=2)
        ckvn = S2b[:, 0:L]
        krope = S2b[:, L:2 * L]
        wcq = [wload_chunk(l, [(CQ + i * 128, 128)]) for i in range(2)]
        wckv = wload_chunk(l, [(CKV, 128)])
        for (s0, n) in TCH:
            hkeys = [("hT", b) for b in blocks_of(s0, n)]
            for grp, wl, nfeat in (("q", wcq, 256), ("kv", [wckv], 128)):
                raws = []
                for gi_, (wt, wkey) in enumerate(wl):
                    pi = nxt("pj", 2)
                    p = pj[pi]
                    pkey = ("pj", pi)
                    for k in range(8):
                        mm(p[:, 0:n], wt[:, k, :], hT[:, k, s0:s0 + n], k == 0, k == 7, [wkey] + hkeys, [pkey])
                    ri = gi_ if grp == "q" else 2
                    raw = S3[:, ri * 512:ri * 512 + n]
                    rkey = ("S3", "raw%d" % ri)
                    S.op("act", (lambda raw, p, n: lambda e: e.copy(out=raw, in_=p[:, 0:n]))(raw, p, n),
                         reads=[pkey, "S3"], writes=[rkey])
                    raws.append((raw, rkey, gi_))
                si = nxt("st", 2)
                sps = stp[si]
                skey = ("st", si)
                for idx, (raw, rkey, gi_) in enumerate(raws):
                    sq = S3[:, 1536:1536 + n]
                    S.op("act", (lambda sq, raw: lambda e: e.activation(out=sq, in_=raw, func=AF.Square))(sq, raw),
                         reads=[rkey, "S3"], writes=[("S3", "sq")])
                    mm(sps[:, 0:n], onesf[:], sq, idx == 0, idx == len(raws) - 1, [("S3", "sq"), "onesf"], [skey])
                S.op("act", (lambda sps, n, nfeat: lambda e: e.activation(
                    out=rstdb[:, 0:n], in_=sps[:, 0:n], func=AF.Ln, bias=eps_c[:], scale=1.0 / nfeat))(sps, n, nfeat),
                    reads=[skey, "eps_c"], writes=["rstdb"])
                S.op("act", (lambda n: lambda e: e.activation(out=rstdb[:, 0:n], in_=rstdb[:, 0:n], func=AF.Exp,
                                                              scale=-0.5))(n),
                     reads=["rstdb"], writes=["rstdb"])
                for (raw, rkey, gi_) in raws:
                    if grp == "q":
                        dst, dkey, gsc = cqn[:, gi_, s0:s0 + n], ("S1", "cqn%d" % gi_), gq[:, l, gi_:gi_ + 1]
                        kk_ = [dkey, ]
                        rd = ["S1"]
                    else:
                        dst, dkey, gsc = ckvn[:, s0:s0 + n], ("S2", "ckvn"), gkv[:, l:l + 1]
                        rd = ["S2"]
                    S.op("dve", (lambda dst, raw, gsc, n: lambda e: e.scalar_tensor_tensor(
                        out=dst, in0=raw, scalar=gsc, in1=rstdb[:, 0:n], op0=ALU.mult, op1=ALU.mult))(dst, raw, gsc, n),
                        reads=[rkey, "rstdb", "gq", "gkv", "S3"] + rd, writes=[(dkey[0], dkey[1], s0)])
            pa_i = nxt("pj", 2)
            pa = pj[pa_i]
            for k in range(8):
                mm(pa[:, 0:n], wkr[:, k, 0, :], hT[:, k, s0:s0 + n], k == 0, k == 7, ["wkr"] + hkeys, [("pj", pa_i)])
            pb_i = nxt("pj", 2)
            pb = pj[pb_i]
            for k in range(8):
                mm(pb[:, 0:n], wkr[:, k, 1, :], hT[:, k, s0:s0 + n], k == 0, k == 7, ["wkr"] + hkeys, [("pj", pb_i)])
            rope_combine(pa, ("pj", pa_i), pb, ("pj", pb_i), krope, s0, n, [("S2", "krope", s0)], ["S2"])
        for c in range(4):
            for j in range(2):
                h = 2 * c + j
                for (s0, n) in TCH:
                    ckeys = [("S1", "cqn0", s0), ("S1", "cqn1", s0), "S1"]
                    p1i = nxt("pj", 2)
                    p1 = pj[p1i]
                    for k in range(2):
                        mm(p1[:, 0:n], wuq_t[:, k, h * 96:h * 96 + 128], cqn[:, k, s0:s0 + n], k == 0, k == 1,
                           ["wuq"] + ckeys, [("pj", p1i)])
                    p2i = nxt("pj", 2)
                    p2 = pj[p2i]
                    for k in range(2):
                        mm(p2[:, 0:n], wuqrot[:, k, h, :], cqn[:, k, s0:s0 + n], k == 0, k == 1,
                           ["wuqrot"] + ckeys, [("pj", p2i)])
                    copy_op("act", qk[j][0:64, s0:s0 + n], p1[0:64, 0:n], [("pj", p1i)],
                            [("qk", j, b) for b in blocks_of(s0, n)])
                    rope_combine(p1, ("pj", p1i), p2, ("pj", p2i), qk[j], s0, n,
                                 [("qk", j, "aug")], [])
                    p3i = nxt("pj", 2)
                    p3 = pj[p3i]
                    mm(p3[:, 0:n], wukv_t[:, h * 128:h * 128 + 128], ckvn[:, s0:s0 + n], True, True,
                       ["wukv", ("S2", "ckvn", s0), "S2"], [("pj", p3i)])
                    copy_op(evac_eng(), qk[2 + j][0:64, s0:s0 + n], p3[0:64, 0:n], [("pj", p3i)],
                            [("qk", 2 + j, b) for b in blocks_of(s0, n)])
                S.dma("sp", (lambda j: lambda e: e.dma_start(out=qk[2 + j][64:96, :], in_=krope[64:96, :]))(j),
                      reads=[("S2", "krope", s0) for (s0, n) in TCH] + ["S2"], writes=[("qk", 2 + j, "aug")])
            vview = wukv_t[:].rearrange("p (h t d) -> p h t d", h=8, t=2)[:, 2 * c:2 * c + 2, 1, :]
            v_proj_tm(lambda k: vview, "wukv", lambda k, b: ckvn[:, b * 128:(b + 1) * 128],
                      lambda b: [("S2", "ckvn", s0) for (s0, n) in TCH if s0 <= b * 128 < s0 + n] + ["S2"], 1)
            for j in range(2):
                attn_dense(j, 128, 96 ** -0.5, lambda kb: (None, []))
            finish_pair(c)
            if c == 1:
                load_wbr(l, 2)
            if c == 2:
                load_wout_hi(l)
        branch_final(l, 2, first)

    def rope_combine(pa, pakey, pb, pbkey, dst_tile, s0, n, wkeys, rkeys):
        r1 = nxt("tmpf", 2)
        t1 = tmpf[r1]
        S.op("dve", lambda e: e.tensor_tensor(out=t1[64:96, 0:n], in0=pa[64:96, 0:n], in1=ropeT[64:96, 0, s0:s0 + n],
                                              op=ALU.mult), reads=[pakey, "ropeT"], writes=[("tmpf", r1)])
        r2 = nxt("tmpf", 2)
        t2 = tmpf[r2]
        S.op("dve", lambda e: e.tensor_tensor(out=t2[64:96, 0:n], in0=pb[64:96, 0:n], in1=ropeT[64:96, 1, s0:s0 + n],
                                              op=ALU.mult), reads=[pbkey, "ropeT"], writes=[("tmpf", r2)])
        S.op("pool", lambda e: e.tensor_tensor(out=dst_tile[64:96, s0:s0 + n], in0=t1[64:96, 0:n], in1=t2[64:96, 0:n],
                                               op=ALU.add), reads=[("tmpf", r1), ("tmpf", r2)] + rkeys, writes=wkeys)

    for sidx in range(NSEQ):
        first_norm(sidx)
        for l in range(NLAYER):
            mixer_A(l, True)
            mixer_B(l, False)
            mixer_C(l, False)
            wout_phase(l, sidx)

    with nc.allow_non_contiguous_dma(reason='small strided parameter / permuted weight loads'):
        S.emit()
    st.close()
    return nc, S


_CACHE = {}


def kernel(x, meta_tokens, norm_g, w_in, b_f, sinks, q_norm_g, kv_norm_g, w_uq, w_ukv, w_br, w_out, final_norm_g):
    x = np.asarray(x, np.float32)
    B, SEQ, _ = x.shape
    NCORE = 8
    NSEQ = B // NCORE
    NB = SEQ // 128 + 1
    L = NB * 128
    key = (NSEQ, NB)
    if key not in _CACHE:
        _CACHE[key] = build(NSEQ, DEPTH, NB)
    nc, _ = _CACHE[key]
    consts = make_consts(NB)
    xp = np.zeros((B, L, D), np.float32)
    xp[:, 128 - N_META:128, :] = np.asarray(meta_tokens, np.float32)[None]
    xp[:, 128:, :] = x
    shared = {
        "norm_g": np.asarray(norm_g, np.float32), "w_in": np.asarray(w_in, np.float32),
        "b_f": np.asarray(b_f, np.float32), "sinks": np.asarray(sinks, np.float32),
        "q_norm_g": np.asarray(q_norm_g, np.float32), "kv_norm_g": np.asarray(kv_norm_g, np.float32),
        "w_uq": np.asarray(w_uq, np.float32), "w_ukv": np.asarray(w_ukv, np.float32),
        "w_br": np.asarray(w_br, np.float32), "w_out": np.asarray(w_out, np.float32),
        "final_norm_g": np.asarray(final_norm_g, np.float32).reshape(1, D),
    }
    shared.update(consts)
    in_maps = []
    for c in range(NCORE):
        m = dict(shared)
        m["xin"] = np.ascontiguousarray(xp[c * NSEQ:(c + 1) * NSEQ])
        in_maps.append(m)
    res = run_bass_kernel_spmd(nc, in_maps, core_ids=list(range(NCORE)))
    outs = [np.asarray(r["out"], np.float32) for r in res.results]
    return np.concatenate(outs, axis=0)
```

```python
import contextlib
import math
import numpy as np
import concourse.bass as bass
import concourse.mybir as mybir
from concourse.bass_utils import run_bass_kernel_spmd

F32 = mybir.dt.float32
BF16 = mybir.dt.bfloat16
AF = mybir.ActivationFunctionType
ALU = mybir.AluOpType

D = 1024
DEPTH = 4
N_META = 16
HD = 64
D_IN = 7336
AQ, AK, AV, BQ, BK, BV, BFO, CQ, CKV, CKR, ZO, GO = 0, 512, 640, 768, 1280, 1792, 2304, 2312, 2568, 2696, 2728, 4264
EPS = 1e-6
MASKV = -240000.0
ENGINES = ("pe", "act", "dve", "pool", "sp")


class Op:
    __slots__ = ("eng", "fn", "is_dma", "deps", "needs_inc", "tok", "ring")

    def __init__(self, eng, fn, is_dma):
        self.eng = eng
        self.fn = fn
        self.is_dma = is_dma
        self.deps = []
        self.needs_inc = False
        self.tok = None
        self.ring = None


class Sched:
    def __init__(self, nc):
        self.nc = nc
        self.ops = {e: [] for e in ENGINES}
        self.last_w = {}
        self.readers = {}
        self.dma_ring = {"sp": 24, "pool": 12, "act": 4}

    def _add(self, eng, fn, reads, writes, is_dma):
        op = Op(eng, fn, is_dma)
        deps = op.deps
        lw = self.last_w
        rd = self.readers
        for r in reads:
            w = lw.get(r)
            if w is not None:
                deps.append((w, 0))
        for k in writes:
            w = lw.get(k)
            if w is not None:
                deps.append((w, 1))
            lst = rd.get(k)
            if lst:
                for x in lst:
                    deps.append((x, 2))
        for r in reads:
            lst = rd.get(r)
            if lst is None:
                rd[r] = [op]
            else:
                lst.append(op)
        for k in writes:
            lw[k] = op
            rd[k] = []
        self.ops[eng].append(op)
        return op

    def op(self, eng, fn, reads=(), writes=()):
        return self._add(eng, fn, reads, writes, False)

    def dma(self, eng, fn, reads=(), writes=()):
        return self._add(eng, fn, reads, writes, True)

    def emit(self):
        nc = self.nc
        for e in ENGINES:
            for op in self.ops[e]:
                kept = []
                seen = set()
                for (p, kind) in op.deps:
                    if p is op or id(p) in seen:
                        continue
                    if (not p.is_dma) and (not op.is_dma) and p.eng == op.eng:
                        if p.eng == "pe":
                            continue
                    seen.add(id(p))
                    p.needs_inc = True
                    kept.append(p)
                op.deps = kept
        stack = contextlib.ExitStack()
        eng_sem = {e: stack.enter_context(nc.semaphore("c_" + e)) for e in ENGINES}
        ring_sems = {e: [stack.enter_context(nc.semaphore("d_%s%d" % (e, i))) for i in range(n)]
                     for e, n in self.dma_ring.items()}
        final_vals = {}
        for e in ENGINES:
            cnt = 0
            dcnt = 0
            for op in self.ops[e]:
                if op.is_dma:
                    n = self.dma_ring[e]
                    slot, k = dcnt % n, dcnt // n
                    op.ring = (ring_sems[e][slot], 16 * k)
                    op.tok = (ring_sems[e][slot], 16 * (k + 1))
                    final_vals[id(op.tok[0])] = op.tok
                    dcnt += 1
                elif op.needs_inc:
                    cnt += 1
                    op.tok = (eng_sem[e], cnt)
        self.stats = {e: len(self.ops[e]) for e in ENGINES}
        ops = self.ops

        def body(e):
            def run(engh):
                waited = {}
                for op in ops[e]:
                    need = {}
                    if op.is_dma and op.ring[1] > 0:
                        need[id(op.ring[0])] = op.ring
                    for p in op.deps:
                        s, v = p.tok
                        cur = need.get(id(s))
                        if cur is None or cur[1] < v:
                            need[id(s)] = (s, v)
                    for sid, (s, v) in need.items():
                        if waited.get(sid, 0) >= v:
                            continue
                        engh.wait_ge(s, v)
                        waited[sid] = v
                    ins = op.fn(engh)
                    if op.is_dma:
                        ins.then_inc(op.tok[0], 16)
                    elif op.needs_inc:
                        ins.then_inc(op.tok[0], 1)
                if e == "sp":
                    for (s, v) in final_vals.values():
                        if waited.get(id(s), 0) < v:
                            engh.wait_ge(s, v)
                            waited[id(s)] = v
            return run

        with nc.Block() as block:
            block.tensor(body("pe"))
            block.scalar(body("act"))
            block.vector(body("dve"))
            block.gpsimd(body("pool"))
            block.sync(body("sp"))
        stack.close()


def make_consts(NB):
    L = NB * 128
    pad = 128 - N_META
    ident = np.eye(128, dtype=np.float32)
    kk = np.arange(128)[:, None]
    qq = np.arange(128)[None, :]
    cmask = np.where(kk <= qq, 0.0, MASKV).astype(np.float32)
    slopes = 2.0 ** (-8.0 * (np.arange(8) + 1.0) / 8)
    abias = np.zeros((128, 8, 2, 128), np.float32)
    for h in range(8):
        dprev = 128 + qq - kk
        dcur = qq - kk
        abias[:, h, 0, :] = np.where(dprev < 128, -8.0 * slopes[h] * dprev, MASKV)
        abias[:, h, 1, :] = np.where(dcur >= 0, -8.0 * slopes[h] * dcur, MASKV)
    pos = (np.arange(L) - pad).astype(np.float32)
    inv = (10000.0 ** (-np.arange(16, dtype=np.float32) / 16)).astype(np.float32)
    ang = pos[None, :] * inv[:, None]
    cos = np.cos(ang).astype(np.float32)
    sin = np.sin(ang).astype(np.float32)
    rope = np.zeros((128, 2, L), np.float32)
    rope[64:80, 0] = cos
    rope[80:96, 0] = cos
    rope[64:80, 1] = -sin
    rope[80:96, 1] = sin
    return {"c_ident": ident, "c_cmask": cmask, "c_abias": abias, "c_rope": rope}


def build(NSEQ, NLAYER, NB):
    L = NB * 128
    TCH = [(s, min(512, L - s)) for s in range(0, L, 512)]
    nc = bass.Bass("TRN2", target_bir_lowering=False)

    def din(name, shape):
        return nc.dram_tensor(name, list(shape), F32, kind="ExternalInput").ap()

    xin = din("xin", [NSEQ, L, D])
    norm_g = din("norm_g", [DEPTH, D])
    w_in = din("w_in", [DEPTH, D, D_IN])
    b_f = din("b_f", [DEPTH, 8])
    sinks = din("sinks", [DEPTH, 8])
    q_norm_g = din("q_norm_g", [DEPTH, 256])
    kv_norm_g = din("kv_norm_g", [DEPTH, 128])
    w_uq = din("w_uq", [DEPTH, 256, 768])
    w_ukv = din("w_ukv", [DEPTH, 128, 1024])
    w_br = din("w_br", [DEPTH, 3, 512, D])
    w_out = din("w_out", [DEPTH, D, D])
    final_g = din("final_norm_g", [1, D])
    c_ident = din("c_ident", [128, 128])
    c_cmask = din("c_cmask", [128, 128])
    c_abias = din("c_abias", [128, 8, 2, 128])
    c_rope = din("c_rope", [128, 2, L])
    out = nc.dram_tensor("out", [NSEQ, L - 128, D], F32, kind="ExternalOutput").ap()
    xs = nc.dram_tensor("xs_scr", [L, D], F32, kind="Internal").ap()
    ys = nc.dram_tensor("ys_scr", [8, 128, L], F32, kind="Internal").ap()

    st = contextlib.ExitStack()

    def sb(name, shape, dt):
        return st.enter_context(nc.sbuf_tensor(name, list(shape), dt))

    def pst(name, shape, dt):
        return st.enter_context(nc.psum_tensor(name, list(shape), dt))

    hT = sb("hT", [128, 8, L], BF16)
    uT = sb("uT", [128, 4, L], BF16)
    qk = [sb("qk%d" % i, [128, L], BF16) for i in range(4)]
    vp = sb("vp", [128, NB, 2, 65], BF16)
    opair = sb("opair", [128, NB, 128], BF16)
    NPT = 4
    PT = [sb("pt%d" % i, [128, 512], BF16) for i in range(NPT)]
    NWR = 6
    wring = [sb("wr%d" % i, [128, 8, 128], BF16) for i in range(NWR)]
    wbig = sb("wbig", [128, 8, 1024], BF16)
    wuq_t = sb("wuq", [128, 2, 800], BF16)
    wuqrot = sb("wuqrot", [128, 2, 8, 128], BF16)
    wukv_t = sb("wukv", [128, 1024], BF16)
    wkr = sb("wkr", [128, 8, 2, 128], BF16)
    wbf = sb("wbf", [128, 8, 8], BF16)
    S1 = sb("S1", [128, L], F32)
    S2 = sb("S2", [128, L], F32)
    S3 = sb("S3", [128, max(L, 2048)], F32)
    S1b = S1.bitcast(BF16)
    S2b = S2.bitcast(BF16)
    S3b = S3.bitcast(BF16)
    rstdb = sb("rstdb", [128, 512], F32)
    ropeT = sb("ropeT", [128, 2, L], BF16)
    abias = sb("abias", [128, 8, 2, 128], BF16)
    ident = sb("ident", [128, 128], BF16)
    identf = sb("identf", [128, 128], F32)
    onesf = sb("onesf", [128, 128], F32)
    cmask = sb("cmask", [128, 128], BF16)
    gbuf = [sb("gb%d" % i, [128, D], F32) for i in range(2)]
    negbf = sb("negbf", [8, DEPTH], F32)
    esink = sb("esink", [128, DEPTH * 8], F32)
    gq = sb("gq", [128, DEPTH, 2], F32)
    gkv = sb("gkv", [128, DEPTH], F32)
    one_c = sb("one_c", [128, 1], F32)
    xt = [sb("xt%d" % i, [128, D], F32) for i in range(2)]
    hb = [sb("hb%d" % i, [128, D], BF16) for i in range(3)]
    st_ss = [sb("ss%d" % i, [128, 1], F32) for i in range(3)]
    st_ms = [sb("ms%d" % i, [128, 1], F32) for i in range(3)]
    st_rs = [sb("rs%d" % i, [128, 1], F32) for i in range(3)]
    tnh = [sb("tnh%d" % i, [128, 512], BF16) for i in range(2)]
    tmpf = [sb("tmpf%d" % i, [128, 512], F32) for i in range(2)]
    yold = [sb("yold%d" % i, [128, 512], F32) for i in range(2)]
    GT = sb("GT", [128, NB * 8], F32)
    den = [sb("den%d" % i, [128, 1], F32) for i in range(4)]
    den8 = [sb("den8_%d" % i, [128, 8], F32) for i in range(4)]
    eps_c = sb("eps_c", [128, 1], F32)
    osb = [sb("osb%d" % i, [128, 455], F32) for i in range(3)]

    pj = [pst("pj%d" % i, [128, 512], F32) for i in range(2)]
    stp = [pst("st%d" % i, [128, 512], F32) for i in range(2)]
    ob = [pst("ob%d" % i, [128, 512], F32) for i in range(3)]
    trb = pst("trb", [128, 1024], BF16)
    trf = trb.bitcast(F32)

    S = Sched(nc)
    cnt = {"pj": 0, "st": 0, "pt": 0, "wr": 0, "tnh": 0, "tmpf": 0, "yold": 0, "yblk": 0, "xt": 0, "den": 0,
           "rtm": 0, "evac": 0, "trs": 0, "st4": 0}

    def nxt(name, n):
        v = cnt[name] % n
        cnt[name] += 1
        return v

    def evac_eng():
        cnt["evac"] += 1
        return "act" if cnt["evac"] % 2 == 0 else "dve"

    def copy_op(eng, out_ap, in_ap, reads, writes):
        if eng == "act":
            S.op("act", lambda e: e.copy(out=out_ap, in_=in_ap), reads, writes)
        else:
            S.op("dve", lambda e: e.tensor_copy(out=out_ap, in_=in_ap), reads, writes)

    def mm(out_ap, lhsT, rhs, start, stop, reads, writes):
        S.op("pe", lambda e: e.matmul(out_ap, lhsT=lhsT, rhs=rhs, start=start, stop=stop, skip_group_check=True),
             reads, writes)

    TRBK = ["trb"]

    def blocks_of(s0, n):
        return range(s0 // 128, (s0 + n) // 128)

    S.dma("pool", lambda e: e.dma_start(out=ident[:], in_=c_ident), writes=["ident"])
    S.dma("pool", lambda e: e.dma_start(out=cmask[:], in_=c_cmask), writes=["cmask"])
    S.dma("pool", lambda e: e.dma_start(out=abias[:], in_=c_abias), writes=["abias"])
    S.dma("pool", lambda e: e.dma_start(out=ropeT[:], in_=c_rope), writes=["ropeT"])
    S.dma("sp", lambda e: e.dma_start(out=identf[:], in_=c_ident), writes=["identf"])
    S.op("pool", lambda e: e.memset(onesf[:], 1.0), writes=["onesf"])
    S.op("pool", lambda e: e.memset(one_c[:], 1.0), writes=["one_c"])
    S.op("pool", lambda e: e.memset(eps_c[:], EPS), writes=["eps_c"])
    S.op("pool", lambda e: e.memset(vp[:, :, :, 64:65], 1.0), writes=["vp_ones"])
    S.op("pool", lambda e: e.memset(vp[0:128 - N_META, 0:1, :, 64:65], 0.0), reads=[], writes=["vp_ones"])
    for _i in range(4):
        S.op("pool", (lambda _i: lambda e: e.memset(qk[_i][:], 0.0))(_i),
             writes=[("qk", _i, b) for b in range(NB)] + [("qk", _i, "aug")])
    S.op("pool", lambda e: e.memset(wuqrot[:], 0.0), writes=["wuqrot"])
    S.op("pool", lambda e: e.memset(wuq_t[:], 0.0), writes=["wuq"])
    S.op("pool", lambda e: e.memset(wkr[:], 0.0), writes=["wkr"])
    with nc.allow_non_contiguous_dma(reason="tiny param loads"):
        S.dma("sp", lambda e: e.dma_start(out=negbf[:], in_=b_f.rearrange("l h -> h l")), writes=["negbf"])
        S.dma("sp", lambda e: e.dma_start(out=gq[:], in_=q_norm_g.rearrange("l (c p) -> p l c", p=128)),
              writes=["gq"])
        S.dma("sp", lambda e: e.dma_start(out=gkv[:], in_=kv_norm_g.rearrange("l p -> p l")), writes=["gkv"])
    S.dma("sp", lambda e: e.dma_start(out=esink[:], in_=sinks.rearrange("l h -> (l h)").partition_broadcast(128)),
          writes=["esink"])
    S.op("dve", lambda e: e.tensor_scalar(out=negbf[:], in0=negbf[:], scalar1=-1.0, scalar2=None, op0=ALU.mult),
         reads=["negbf"], writes=["negbf"])
    S.op("act", lambda e: e.activation(out=esink[:], in_=esink[:], func=AF.Exp), reads=["esink"], writes=["esink"])

    def layer_specs(l):
        sp = []
        for c in range(4):
            sp.append((l, [(ZO + 0 * 512 + c * 128, 128)]))
        for c in range(4):
            g = c // 2
            sp.append((l, [(AQ + c * 128, 128)]))
            sp.append((l, [(AK + g * 64, 64), (AK + g * 64, 64)]))
            sp.append((l, [(AV + g * 64, 64), (AV + g * 64, 64)]))
        for m in range(8):
            sp.append((l, [(GO + 0 * 1024 + m * 128, 128)]))
        for c in range(4):
            sp.append((l, [(ZO + 1 * 512 + c * 128, 128)]))
        for c in range(4):
            sp.append((l, [(BQ + c * 128, 128)]))
            sp.append((l, [(BK + c * 128, 128)]))
            sp.append((l, [(BV + c * 128, 128)]))
        for m in range(8):
            sp.append((l, [(GO + 1 * 1024 + m * 128, 128)]))
        for c in range(4):
            sp.append((l, [(ZO + 2 * 512 + c * 128, 128)]))
        for i in range(2):
            sp.append((l, [(CQ + i * 128, 128)]))
        sp.append((l, [(CKV, 128)]))
        for m in range(8):
            sp.append((l, [(GO + 2 * 1024 + m * 128, 128)]))
        return sp

    WSPECS = []
    for _s in range(NSEQ):
        for _l in range(NLAYER):
            WSPECS.extend(layer_specs(_l))
    wstate = {"issued": 0, "used": 0}
    WPF = NWR - 3

    def _issue_chunk(idx):
        l, col_runs = WSPECS[idx]
        slot = idx % NWR
        t = wring[slot]
        key = ("wr", slot)
        o = 0
        for (c0, n) in col_runs:
            src = w_in[l, :, c0:c0 + n].rearrange("(kc p) n -> p kc n", p=128)
            dst = t[:, :, o:o + n]
            S.dma("pool", (lambda dst, src: lambda e: e.dma_start(out=dst, in_=src))(dst, src), writes=[key])
            o += n

    def wload_chunk(l, col_runs):
        idx = wstate["used"]
        assert WSPECS[idx] == (l, col_runs), (idx, WSPECS[idx], l, col_runs)
        while wstate["issued"] < min(len(WSPECS), idx + 1 + WPF):
            _issue_chunk(wstate["issued"])
            wstate["issued"] += 1
        wstate["used"] += 1
        slot = idx % NWR
        return wring[slot], ("wr", slot)

    def proj_fm(wt, wkey, M, evac_fn, extra_reads=()):
        for (s0, n) in TCH:
            pi = nxt("pj", 2)
            p = pj[pi]
            pkey = ("pj", pi)
            hkeys = [("hT", b) for b in blocks_of(s0, n)]
            for k in range(8):
                mm(p[0:M, 0:n], wt[:, k, 0:M], hT[:, k, s0:s0 + n], k == 0, k == 7,
                   [wkey] + hkeys + list(extra_reads), [pkey])
            evac_fn(p, pkey, s0, n)

    def norm_stats(xtile, xkey, b, gtile, gkey, last, sidx):
        i = b % 3
        hbt = hb[i]
        S.op("act", lambda e: e.activation(out=hbt[:], in_=xtile[:], func=AF.Square, accum_out=st_ss[i][:]),
             reads=[xkey], writes=[("hb", i), ("ss", i)])
        S.op("act", lambda e: e.activation(out=st_ms[i][:], in_=st_ss[i][:], func=AF.Ln, bias=eps_c[:], scale=1.0 / D),
             reads=[("ss", i), "eps_c"], writes=[("ms", i)])
        S.op("act", lambda e: e.activation(out=st_rs[i][:], in_=st_ms[i][:], func=AF.Exp, scale=-0.5),
             reads=[("ms", i)], writes=[("rs", i)])
        if last:
            if b == 0:
                return
            S.op("dve", lambda e: e.scalar_tensor_tensor(out=xtile[:], in0=xtile[:], scalar=st_rs[i][:], in1=gtile[:],
                                                         op0=ALU.mult, op1=ALU.mult),
                 reads=[xkey, ("rs", i), gkey], writes=[xkey])
            S.dma("sp", lambda e: e.dma_start(out=out[sidx, (b - 1) * 128:b * 128, :], in_=xtile[:]), reads=[xkey])
            return
        S.op("dve", lambda e: e.scalar_tensor_tensor(out=hbt[:], in0=xtile[:], scalar=st_rs[i][:], in1=gtile[:],
                                                     op0=ALU.mult, op1=ALU.mult),
             reads=[xkey, ("rs", i), gkey], writes=[("hb", i)])

    def norm_transposes(b):
        i = b % 3
        hbt = hb[i]
        for c in range(8):
            S.op("pe", (lambda c: lambda e: e.transpose(out=trb[:, c * 128:(c + 1) * 128],
                                                        in_=hbt[:, c * 128:(c + 1) * 128], identity=ident[:]))(c),
                 reads=[("hb", i), "ident"], writes=TRBK)
        copy_op(evac_eng(), hT[:, :, b * 128:(b + 1) * 128], trb[:].rearrange("p (c t) -> p c t", c=8),
                TRBK, [("hT", b)])

    def norm_block(xtile, xkey, b, gtile, gkey, last, sidx):
        norm_stats(xtile, xkey, b, gtile, gkey, last, sidx)
        if not last:
            norm_transposes(b)

    def load_gb(which, src_row):
        S.dma("sp", lambda e: e.dma_start(out=gbuf[which][:], in_=src_row.partition_broadcast(128)),
              writes=[("gb", which)])

    SKEW = 2

    def st_next():
        i = nxt("st4", 4)
        return [(stp[0], ("st", 0)), (stp[1], ("st", 1)), (pj[0], ("pj", 0)), (pj[1], ("pj", 1))][i]

    def run_pipeline(stages, qk_stage, pv_stage):
        pend = []
        for stg in stages:
            pend.append(qk_stage(stg))
            if len(pend) > SKEW:
                pv_stage(*pend.pop(0))
        while pend:
            pv_stage(*pend.pop(0))

    def o_ap(qb):
        return ob[qb // 7][:, (qb % 7) * 65:(qb % 7) * 65 + 65], ("ob", qb // 7)

    def normalize(oap, okey, qb, j, sink_ap=None):
        di = nxt("den", 4)
        d = den[di]
        if sink_ap is None:
            S.op("dve", lambda e: e.tensor_scalar(out=d[:], in0=oap[:, 64:65], scalar1=1e-30, scalar2=None,
                                                  op0=ALU.max), reads=[okey], writes=[("den", di)])
        else:
            S.op("dve", lambda e: e.tensor_scalar(out=d[:], in0=oap[:, 64:65], scalar1=sink_ap, scalar2=None,
                                                  op0=ALU.add), reads=[okey, "esink"], writes=[("den", di)])
        S.op("dve", lambda e: e.reciprocal(out=d[:], in_=d[:]), reads=[("den", di)], writes=[("den", di)])
        S.op("dve", lambda e: e.tensor_scalar(out=opair[:, qb, j * 64:(j + 1) * 64], in0=oap[:, 0:64],
                                              scalar1=d[:], scalar2=None, op0=ALU.mult),
             reads=[okey, ("den", di)], writes=[("opair", qb, j)])

    def attn_dense(j, Kd, scale, bias_fn):
        qt, kt = qk[j], qk[2 + j]
        qkeys = lambda s0, n: [("qk", j, b) for b in blocks_of(s0, n)] + [("qk", j, "aug")]
        stages = []
        for kb in range(NB):
            q0 = kb * 128
            for ci, s0 in enumerate(range(q0, L, 512)):
                stages.append((kb, ci, s0, min(512, L - s0)))

        def qk_stage(stg):
            kb, ci, s0, n = stg
            q0 = kb * 128
            sp_, skey = st_next()
            mm(sp_[:, 0:n], kt[0:Kd, q0:q0 + 128], qt[0:Kd, s0:s0 + n], True, ci != 0,
               [("qk", 2 + j, kb), ("qk", 2 + j, "aug")] + qkeys(s0, n), [skey])
            if ci == 0:
                mm(sp_[:, 0:128], ident[:], cmask[:], False, True, ["ident", "cmask"], [skey])
            return stg, sp_, skey

        def pv_stage(stg, sp_, skey):
            kb, ci, s0, n = stg
            pi = nxt("pt", NPT)
            pt = PT[pi]
            pkey = ("pt", pi)
            bias_ap, bias_keys = bias_fn(kb)
            if bias_ap is None:
                S.op("act", (lambda pt, sp_, n: lambda e: e.activation(out=pt[:, 0:n], in_=sp_[:, 0:n],
                                                                       func=AF.Exp, scale=scale))(pt, sp_, n),
                     reads=[skey], writes=[pkey])
            else:
                S.op("act", (lambda pt, sp_, n, bias_ap: lambda e: e.activation(
                    out=pt[:, 0:n], in_=sp_[:, 0:n], func=AF.Exp, bias=bias_ap, scale=scale))(pt, sp_, n, bias_ap),
                    reads=[skey] + bias_keys, writes=[pkey])
            for qi, qb in enumerate(blocks_of(s0, n)):
                oap, okey = o_ap(qb)
                mm(oap, pt[:, qi * 128:(qi + 1) * 128], vp[:, kb, j, :], kb == 0 and qb % 7 == 0, kb == qb,
                   [pkey, ("vp", kb), "vp_ones"], [okey])

        run_pipeline(stages, qk_stage, pv_stage)
        dense_finish(j)

    def dense_finish(j):
        for bank in range((NB + 6) // 7):
            nq = min(7, NB - bank * 7)
            ncol = nq * 65
            copy_op(evac_eng(), osb[bank][:, 0:ncol], ob[bank][:, 0:ncol], [("ob", bank)], [("osb", bank)])
            ov = osb[bank][:, 0:ncol].rearrange("p (q d) -> p q d", d=65)
            di = nxt("den", 4)
            d = den8[di]
            S.op("dve", (lambda d, ov, nq: lambda e: e.tensor_scalar(out=d[:, 0:nq], in0=ov[:, :, 64], scalar1=1e-30,
                                                                    scalar2=None, op0=ALU.max))(d, ov, nq),
                 reads=[("osb", bank)], writes=[("den8", di)])
            S.op("dve", (lambda d, nq: lambda e: e.reciprocal(out=d[:, 0:nq], in_=d[:, 0:nq]))(d, nq),
                 reads=[("den8", di)], writes=[("den8", di)])
            q0 = bank * 7
            S.op("dve", (lambda d, ov, nq, q0: lambda e: e.tensor_tensor(
                out=opair[:, q0:q0 + nq, j * 64:(j + 1) * 64], in0=ov[:, :, 0:64],
                in1=d[:, 0:nq].unsqueeze(2).broadcast_to([128, nq, 64]), op=ALU.mult))(d, ov, nq, q0),
                reads=[("osb", bank), ("den8", di)], writes=[("opair", q0 + q, j) for q in range(nq)])

    def attn_swa(j, h, l):
        qt, kt = qk[j], qk[2 + j]
        sink_ap = esink[:, l * 8 + h:l * 8 + h + 1]

        def qk_stage(qb):
            sp_, skey = st_next()
            parts = []
            if qb >= 1:
                parts.append((qb - 1, 0))
            parts.append((qb, 1))
            for pi_, (kb, which) in enumerate(parts):
                col = pi_ * 128
                mm(sp_[:, col:col + 128], kt[:, kb * 128:(kb + 1) * 128], qt[:, qb * 128:(qb + 1) * 128],
                   True, False, [("qk", 2 + j, kb), ("qk", 2 + j, "aug"), ("qk", j, qb), ("qk", j, "aug")], [skey])
                mm(sp_[:, col:col + 128], ident[:], abias[:, h, which, :], False, True, ["ident", "abias"], [skey])
            return qb, parts, sp_, skey

        def pv_stage(qb, parts, sp_, skey):
            n = 128 * len(parts)
            pi = nxt("pt", NPT)
            pt = PT[pi]
            pkey = ("pt", pi)
            S.op("act", (lambda pt, sp_, n: lambda e: e.activation(out=pt[:, 0:n], in_=sp_[:, 0:n], func=AF.Exp,
                                                                   scale=0.125))(pt, sp_, n),
                 reads=[skey], writes=[pkey])
            oap, okey = o_ap(qb)
            for pi_, (kb, which) in enumerate(parts):
                mm(oap, pt[:, pi_ * 128:(pi_ + 1) * 128], vp[:, kb, j, :], pi_ == 0 and qb % 7 == 0,
                   pi_ == len(parts) - 1, [pkey, ("vp", kb), "vp_ones"], [okey])
            if qb % 7 == 6 or qb == NB - 1:
                bank = qb // 7
                q0 = bank * 7
                nq = qb - q0 + 1
                ov = ob[bank][:, 0:nq * 65].rearrange("p (q d) -> p q d", d=65)
                di = nxt("den", 4)
                d = den8[di]
                S.op("dve", (lambda d, ov, nq: lambda e: e.tensor_scalar(out=d[:, 0:nq], in0=ov[:, :, 64], scalar1=sink_ap,
                                                                        scalar2=None, op0=ALU.add))(d, ov, nq),
                     reads=[("ob", bank), "esink"], writes=[("den8", di)])
                S.op("dve", (lambda d, nq: lambda e: e.reciprocal(out=d[:, 0:nq], in_=d[:, 0:nq]))(d, nq),
                     reads=[("den8", di)], writes=[("den8", di)])
                S.op("dve", (lambda d, ov, nq, q0: lambda e: e.tensor_tensor(
                    out=opair[:, q0:q0 + nq, j * 64:(j + 1) * 64], in0=ov[:, :, 0:64],
                    in1=d[:, 0:nq].unsqueeze(2).broadcast_to([128, nq, 64]), op=ALU.mult))(d, ov, nq, q0),
                    reads=[("ob", bank), ("den8", di)], writes=[("opair", q0 + q, j) for q in range(nq)])

        run_pipeline(list(range(NB)), qk_stage, pv_stage)

    def finish_pair(c):
        banks = [(trb, "trb"), (pj[0].bitcast(BF16), ("pj", 0)), (pj[1].bitcast(BF16), ("pj", 1))]
        for g0 in range(0, NB, 8):
            gn = min(8, NB - g0)
            bi = nxt("trs", 3)
            bt, bkey = banks[bi]
            for qi in range(gn):
                qb = g0 + qi
                S.op("pe", (lambda qb, qi, bt: lambda e: e.transpose(out=bt[:, qi * 128:(qi + 1) * 128],
                                                                   in_=opair[:, qb, :], identity=ident[:]))(qb, qi, bt),
                     reads=[("opair", qb, 0), ("opair", qb, 1), "ident"], writes=[bkey])
            S.op("dve", (lambda g0, gn, bt: lambda e: e.tensor_tensor(
                out=uT[:, c, g0 * 128:(g0 + gn) * 128], in0=bt[:, 0:gn * 128],
                in1=uT[:, c, g0 * 128:(g0 + gn) * 128], op=ALU.mult))(g0, gn, bt),
                reads=[bkey] + [("uT", c, g0 + q) for q in range(gn)], writes=[("uT", c, g0 + q) for q in range(gn)])

    def z_proj(l, i):
        for c in range(4):
            wt, wkey = wload_chunk(l, [(ZO + i * 512 + c * 128, 128)])

            def ev(p, pkey, s0, n, c=c):
                ti = nxt("tnh", 2)
                t = tnh[ti]
                S.op("act", lambda e: e.activation(out=t[:, 0:n], in_=p[:, 0:n], func=AF.Tanh, scale=0.5),
                     reads=[pkey], writes=[("tnh", ti)])
                S.op("dve", lambda e: e.scalar_tensor_tensor(out=uT[:, c, s0:s0 + n], in0=t[:, 0:n], scalar=1.0,
                                                             in1=p[:, 0:n], op0=ALU.add, op1=ALU.mult),
                     reads=[("tnh", ti), pkey], writes=[("uT", c, b) for b in blocks_of(s0, n)])
            proj_fm(wt, wkey, 128, ev)

    def yT_ap(m, s0, n):
        if m < 4:
            return qk[m][:, s0:s0 + n]
        if m == 4:
            return S1b[:, s0:s0 + n]
        if m == 5:
            return S1b[:, L + s0:L + s0 + n]
        if m == 6:
            return S2b[:, s0:s0 + n]
        return S2b[:, L + s0:L + s0 + n]

    def yT_keys(m, s0, n):
        c0 = (s0 // 512) * 512
        if m < 4:
            return [("qk", m, b) for b in blocks_of(s0, n)] + [("qk", m, "aug")], []
        fine = {4: ("S1", "cqn0", c0), 5: ("S1", "cqn1", c0), 6: ("S2", "ckvn", c0), 7: ("S2", "krope", c0)}[m]
        return [fine], ["S1" if m < 6 else "S2"]

    def load_wbr(l, i):
        src = w_br[l, i].rearrange("(c p) n -> p c n", p=128)
        S.dma("pool", lambda e: e.dma_start(out=wbig[:, 0:4, :], in_=src), writes=[("wbig", 0)])

    def branch_final(l, i, first):
        its = [(m, s0, n) for m in range(8) for (s0, n) in TCH]
        yolds = {}

        def issue_yold(k):
            if first or k >= len(its):
                return
            m, s0, n = its[k]
            yi = nxt("yold", 2)
            yo = yold[yi]
            S.dma("sp", (lambda yo, m, s0, n: lambda e: e.dma_start(out=yo[:, 0:n], in_=ys[m, :, s0:s0 + n]))(yo, m, s0, n),
                  reads=[("ys", m, s0)], writes=[("yold", yi)])
            yolds[k] = (yo, yi)

        issue_yold(0)
        wt = wkey = None
        for k, (m, s0, n) in enumerate(its):
            if s0 == 0:
                wt, wkey = wload_chunk(l, [(GO + i * 1024 + m * 128, 128)])
            issue_yold(k + 1)
            pi = nxt("pj", 2)
            p = pj[pi]
            pkey = ("pj", pi)
            hkeys = [("hT", b) for b in blocks_of(s0, n)]
            for kk in range(8):
                mm(p[:, 0:n], wt[:, kk, :], hT[:, kk, s0:s0 + n], kk == 0, kk == 7, [wkey] + hkeys, [pkey])
            ti = nxt("tnh", 2)
            t = tnh[ti]
            S.op("act", (lambda t, p, n: lambda e: e.activation(out=t[:, 0:n], in_=p[:, 0:n], func=AF.Tanh,
                                                                scale=0.5))(t, p, n),
                 reads=[pkey], writes=[("tnh", ti)])
            pi2 = nxt("pj", 2)
            p2 = pj[pi2]
            pkey2 = ("pj", pi2)
            for c in range(4):
                mm(p2[:, 0:n], wbig[:, c, m * 128:(m + 1) * 128], uT[:, c, s0:s0 + n], c == 0, c == 3,
                   [("wbig", 0)] + [("uT", c, b) for b in blocks_of(s0, n)], [pkey2])
            fi = nxt("tmpf", 2)
            tf = tmpf[fi]
            S.op("dve", (lambda tf, t, p2, n: lambda e: e.scalar_tensor_tensor(
                out=tf[:, 0:n], in0=t[:, 0:n], scalar=1.0, in1=p2[:, 0:n], op0=ALU.add, op1=ALU.mult))(tf, t, p2, n),
                reads=[("tnh", ti), pkey2], writes=[("tmpf", fi)])
            ykey = ("ys", m, s0)
            ydst = ys[m, :, s0:s0 + n]
            if i == 2 and not first:
                yo, yi = yolds.pop(k)
                wk_, rk_ = yT_keys(m, s0, n)
                yap = yT_ap(m, s0, n)
                S.op("dve", (lambda tf, yo, n, yap: lambda e: e.tensor_tensor(out=yap, in0=tf[:, 0:n],
                                                                              in1=yo[:, 0:n], op=ALU.add))(tf, yo, n, yap),
                     reads=[("tmpf", fi), ("yold", yi)] + rk_, writes=wk_)
                continue
            if not first:
                yo, yi = yolds.pop(k)
                S.op("dve", (lambda tf, yo, n: lambda e: e.tensor_tensor(out=tf[:, 0:n], in0=tf[:, 0:n],
                                                                         in1=yo[:, 0:n], op=ALU.add))(tf, yo, n),
                     reads=[("tmpf", fi), ("yold", yi)], writes=[("tmpf", fi)])
            S.dma("sp", (lambda tf, ydst, n: lambda e: e.dma_start(out=ydst, in_=tf[:, 0:n]))(tf, ydst, n),
                  reads=[("tmpf", fi)], writes=[ykey])

    def load_wout_hi(l):
        src = w_out[l, 512:1024, :].rearrange("(c p) n -> p c n", p=128)
        S.dma("pool", lambda e: e.dma_start(out=wbig[:, 4:8, :], in_=src), writes=[("wbig", 1)])

    def wout_phase(l, sidx):
        last = (l == NLAYER - 1)
        src = w_out[l, 0:512, :].rearrange("(c p) n -> p c n", p=128)
        S.dma("pool", lambda e: e.dma_start(out=wbig[:, 0:4, :], in_=src), writes=[("wbig", 0)])
        gi = (l + 1) % 2
        load_gb(gi, final_g[0] if last else norm_g[l + 1])
        xsrc = xin[sidx] if l == 0 else xs
        pending = []
        for b in range(NB):
            xi = nxt("xt", 2)
            x = xt[xi]
            xkey = ("xt", xi)
            S.dma("sp", (lambda x, b: lambda e: e.dma_start(out=x[:], in_=xsrc[b * 128:(b + 1) * 128, :]))(x, b),
                  reads=[("xs", b)] if l > 0 else [], writes=[xkey])
            for half in range(2):
                pi = nxt("pj", 2)
                p = pj[pi]
                pkey = ("pj", pi)
                order = [4, 5, 6, 7, 0, 1, 2, 3]
                for oi, m in enumerate(order):
                    wk_, rk_ = yT_keys(m, b * 128, 128)
                    mm(p[:, :], yT_ap(m, b * 128, 128), wbig[:, m, half * 512:(half + 1) * 512], oi == 0, oi == 7,
                       wk_ + rk_ + [("wbig", m // 4)], [pkey])
                S.op("dve", (lambda x, p, half: lambda e: e.scalar_tensor_tensor(
                    out=x[:, half * 512:(half + 1) * 512], in0=p[:, :], scalar=0.25,
                    in1=x[:, half * 512:(half + 1) * 512], op0=ALU.mult, op1=ALU.add))(x, p, half),
                    reads=[pkey, xkey], writes=[xkey])
            if not last:
                S.dma("sp", (lambda x, b: lambda e: e.dma_start(out=xs[b * 128:(b + 1) * 128, :], in_=x[:]))(x, b),
                      reads=[xkey], writes=[("xs", b)])
            norm_stats(x, xkey, b, gbuf[gi], ("gb", gi), last, sidx)
            pending.append(b)
            if len(pending) > 2 and not last:
                norm_transposes(pending.pop(0))
        while pending and not last:
            norm_transposes(pending.pop(0))

    def first_norm(sidx):
        load_gb(0, norm_g[0])
        for b in range(NB):
            xi = nxt("xt", 2)
            x = xt[xi]
            xkey = ("xt", xi)
            S.dma("sp", (lambda x, b: lambda e: e.dma_start(out=x[:], in_=xin[sidx, b * 128:(b + 1) * 128, :]))(x, b),
                  writes=[xkey])
            norm_block(x, xkey, b, gbuf[0], ("gb", 0), False, sidx)

    S1K = ["S1"]
    S2K = ["S2"]
    S3K = ["S3"]

    def evac_pair(p, pkey, s0, n, t0, k0, t1, k1, rows=64):
        copy_op("act", t0[0:rows, s0:s0 + n], p[0:rows, 0:n], [pkey], [(k0[0], k0[1], b) for b in blocks_of(s0, n)])
        copy_op("dve", t1[0:rows, s0:s0 + n], p[64:64 + rows, 0:n], [pkey],
                [(k1[0], k1[1], b) for b in blocks_of(s0, n)])

    def v_proj_tm(wt, wkey, lhs_fn, lhs_keys_fn, nk):
        for b in range(NB):
            pi = nxt("pj", 2)
            p = pj[pi]
            pkey = ("pj", pi)
            for k in range(nk):
                mm(p[:, 0:128], lhs_fn(k, b), wt(k), k == 0, k == nk - 1, [wkey] + lhs_keys_fn(b), [pkey])
            copy_op(evac_eng(), vp[:, b, :, 0:64], p[:, 0:128].rearrange("p (j d) -> p j d", j=2), [pkey],
                    [("vp", b)])

    def mixer_A(l, first=True):
        for _i in range(2):
            S.op("pool", (lambda _i: lambda e: e.memset(qk[_i][64:128, :], 0.0))(_i), writes=[("qk", _i, "aug")])
        z_proj(l, 0)
        for c in range(4):
            g = c // 2
            wq_, wqk = wload_chunk(l, [(AQ + c * 128, 128)])
            proj_fm(wq_, wqk, 128, lambda p, pkey, s0, n: evac_pair(p, pkey, s0, n, qk[0], ("qk", 0), qk[1], ("qk", 1)))
            wk_, wkk = wload_chunk(l, [(AK + g * 64, 64), (AK + g * 64, 64)])
            proj_fm(wk_, wkk, 128, lambda p, pkey, s0, n: evac_pair(p, pkey, s0, n, qk[2], ("qk", 2), qk[3], ("qk", 3)))
            wv_, wvk = wload_chunk(l, [(AV + g * 64, 64), (AV + g * 64, 64)])
            v_proj_tm(lambda k: wv_[:, k, :], wvk, lambda k, b: hT[:, k, b * 128:(b + 1) * 128],
                      lambda b: [("hT", b)], 8)
            for j in range(2):
                attn_swa(j, 2 * c + j, l)
            finish_pair(c)
            if c == 1:
                load_wbr(l, 0)
        branch_final(l, 0, first)

    def fox_gates(l):
        src = w_in[l, :, BFO:BFO + 8].rearrange("(kc p) n -> p kc n", p=128)
        with nc.allow_non_contiguous_dma(reason="8-col gate weights"):
            S.dma("pool", lambda e: e.dma_start(out=wbf[:], in_=src), writes=["wbf"])
        for (s0, n) in TCH:
            pi = nxt("pj", 2)
            p = pj[pi]
            pkey = ("pj", pi)
            for k in range(8):
                mm(p[0:8, 0:n], wbf[:, k, :], hT[:, k, s0:s0 + n], k == 0, k == 7,
                   ["wbf"] + [("hT", b) for b in blocks_of(s0, n)], [pkey])
            S.op("act", (lambda p, s0, n: lambda e: e.activation(out=S1[0:8, s0:s0 + n], in_=p[0:8, 0:n], func=AF.Exp,
                                                                 bias=negbf[:, l:l + 1], scale=-1.0))(p, s0, n),
                 reads=[pkey, "negbf"], writes=S1K)
        S.op("act", lambda e: e.activation(out=S1[0:8, :], in_=S1[0:8, :], func=AF.Ln, bias=one_c[0:8, :], scale=1.0),
             reads=S1K + ["one_c"], writes=S1K)
        S.op("dve", lambda e: e.tensor_tensor_scan(out=S2[0:8, :], data0=S1[0:8, :], data1=S1[0:8, :], initial=0.0,
                                                   op0=ALU.add, op1=ALU.max), reads=S1K, writes=S2K)
        fq = S3b[0:8, 0:2 * L].rearrange("p (t l) -> p t l", t=2)
        S.op("dve", lambda e: e.tensor_scalar(out=fq[:, 0, :], in0=S2[0:8, :], scalar1=-8.0, scalar2=None,
                                              op0=ALU.mult), reads=S2K, writes=S3K)
        S.op("dve", lambda e: e.scalar_tensor_tensor(out=fq[:, 1, :], in0=S2[0:8, :], scalar=-8.0, in1=fq[:, 0, :],
                                                     op0=ALU.mult, op1=ALU.subtract), reads=S2K + S3K, writes=S3K)
        for b in range(NB):
            S.op("pe", (lambda b: lambda e: e.transpose(out=trf[:, b * 8:(b + 1) * 8], in_=S2[0:8, b * 128:(b + 1) * 128],
                                                        identity=identf[0:8, 0:8]))(b),
                 reads=S2K + ["identf"], writes=TRBK)
        S.op("dve", lambda e: e.tensor_copy(out=GT[:], in_=trf[:, 0:NB * 8]), reads=TRBK, writes=["GT"])
        return fq

    def mixer_B(l, first=False):
        fq = fox_gates(l)
        for _i in range(4):
            S.op("pool", (lambda _i: lambda e: e.memset(qk[_i][64:128, :], 0.0))(_i), writes=[("qk", _i, "aug")])
        z_proj(l, 1)
        for c in range(4):
            wq_, wqk = wload_chunk(l, [(BQ + c * 128, 128)])
            proj_fm(wq_, wqk, 128, lambda p, pkey, s0, n: evac_pair(p, pkey, s0, n, qk[0], ("qk", 0), qk[1], ("qk", 1)))
            wk_, wkk = wload_chunk(l, [(BK + c * 128, 128)])
            proj_fm(wk_, wkk, 128, lambda p, pkey, s0, n: evac_pair(p, pkey, s0, n, qk[2], ("qk", 2), qk[3], ("qk", 3)))
            wv_, wvk = wload_chunk(l, [(BV + c * 128, 128)])
            v_proj_tm(lambda k: wv_[:, k, :], wvk, lambda k, b: hT[:, k, b * 128:(b + 1) * 128],
                      lambda b: [("hT", b)], 8)
            for j in range(2):
                h = 2 * c + j
                for t_ in range(2):
                    S.dma("sp", (lambda j, h, t_: lambda e: e.dma_start(out=qk[j][64 + t_:65 + t_, :],
                                                                        in_=fq[h:h + 1, t_, :]))(j, h, t_),
                          reads=S3K, writes=[("qk", j, "aug")])
                S.op("pool", (lambda j: lambda e: e.memset(qk[2 + j][64:66, :], 1.0))(j), writes=[("qk", 2 + j, "aug")])
            for j in range(2):
                h = 2 * c + j
                attn_dense(j, 128, 0.125,
                           lambda kb, h=h: (GT[:, kb * 8 + h:kb * 8 + h + 1], ["GT"]))
            finish_pair(c)
            if c == 1:
                load_wbr(l, 1)
        branch_final(l, 1, first)

    def mixer_C(l, first=False):
        z_proj(l, 2)
        S.dma("pool", lambda e: e.dma_start(out=wuq_t[:, :, 0:768], in_=w_uq[l].rearrange("(c p) n -> p c n", p=128)),
              writes=["wuq"])
        S.dma("pool", lambda e: e.dma_start(out=wukv_t[:], in_=w_ukv[l]), writes=["wukv"])
        uqv = w_uq[l].rearrange("(c p) (h r) -> p c h r", p=128, r=96)
        with nc.allow_non_contiguous_dma(reason="rope column permutation"):
            for c2 in range(2):
                S.dma("pool", (lambda c2: lambda e: e.dma_start(out=wuqrot[:, c2, :, 64:80], in_=uqv[:, c2, :, 80:96]))(c2),
                      writes=["wuqrot"])
                S.dma("pool", (lambda c2: lambda e: e.dma_start(out=wuqrot[:, c2, :, 80:96], in_=uqv[:, c2, :, 64:80]))(c2),
                      writes=["wuqrot"])
            krv = w_in[l].rearrange("(kc p) n -> p kc n", p=128)
            S.dma("pool", lambda e: e.dma_start(out=wkr[:, :, 0, 64:96], in_=krv[:, :, CKR:CKR + 32]), writes=["wkr"])
            S.dma("pool", lambda e: e.dma_start(out=wkr[:, :, 1, 64:80], in_=krv[:, :, CKR + 16:CKR + 32]),
                  writes=["wkr"])
            S.dma("pool", lambda e: e.dma_start(out=wkr[:, :, 1, 80:96], in_=krv[:, :, CKR:CKR + 16]), writes=["wkr"])
        cqn = S1b[:, 0:2 * L].rearrange("p (c l) -> p c l", c=2)
        ckvn = S2b[:, 0:L]
        krope = S2b[:, L:2 * L]
        wcq = [wload_chunk(l, [(CQ + i * 128, 128)]) for i in range(2)]
        wckv = wload_chunk(l, [(CKV, 128)])
        for (s0, n) in TCH:
            hkeys = [("hT", b) for b in blocks_of(s0, n)]
            for grp, wl, nfeat in (("q", wcq, 256), ("kv", [wckv], 128)):
                raws = []
                for gi_, (wt, wkey) in enumerate(wl):
                    pi = nxt("pj", 2)
                    p = pj[pi]
                    pkey = ("pj", pi)
                    for k in range(8):
                        mm(p[:, 0:n], wt[:, k, :], hT[:, k, s0:s0 + n], k == 0, k == 7, [wkey] + hkeys, [pkey])
                    ri = gi_ if grp == "q" else 2
                    raw = S3[:, ri * 512:ri * 512 + n]
                    rkey = ("S3", "raw%d" % ri)
                    S.op("act", (lambda raw, p, n: lambda e: e.copy(out=raw, in_=p[:, 0:n]))(raw, p, n),
                         reads=[pkey, "S3"], writes=[rkey])
                    raws.append((raw, rkey, gi_))
                si = nxt("st", 2)
                sps = stp[si]
                skey = ("st", si)
                for idx, (raw, rkey, gi_) in enumerate(raws):
                    sq = S3[:, 1536:1536 + n]
                    S.op("act", (lambda sq, raw: lambda e: e.activation(out=sq, in_=raw, func=AF.Square))(sq, raw),
                         reads=[rkey, "S3"], writes=[("S3", "sq")])
                    mm(sps[:, 0:n], onesf[:], sq, idx == 0, idx == len(raws) - 1, [("S3", "sq"), "onesf"], [skey])
                S.op("act", (lambda sps, n, nfeat: lambda e: e.activation(
                    out=rstdb[:, 0:n], in_=sps[:, 0:n], func=AF.Ln, bias=eps_c[:], scale=1.0 / nfeat))(sps, n, nfeat),
                    reads=[skey, "eps_c"], writes=["rstdb"])
                S.op("act", (lambda n: lambda e: e.activation(out=rstdb[:, 0:n], in_=rstdb[:, 0:n], func=AF.Exp,
                                                              scale=-0.5))(n),
                     reads=["rstdb"], writes=["rstdb"])
                for (raw, rkey, gi_) in raws:
                    if grp == "q":
                        dst, dkey, gsc = cqn[:, gi_, s0:s0 + n], ("S1", "cqn%d" % gi_), gq[:, l, gi_:gi_ + 1]
                        kk_ = [dkey, ]
                        rd = ["S1"]
                    else:
                        dst, dkey, gsc = ckvn[:, s0:s0 + n], ("S2", "ckvn"), gkv[:, l:l + 1]
                        rd = ["S2"]
                    S.op("dve", (lambda dst, raw, gsc, n: lambda e: e.scalar_tensor_tensor(
                        out=dst, in0=raw, scalar=gsc, in1=rstdb[:, 0:n], op0=ALU.mult, op1=ALU.mult))(dst, raw, gsc, n),
                        reads=[rkey, "rstdb", "gq", "gkv", "S3"] + rd, writes=[(dkey[0], dkey[1], s0)])
            pa_i = nxt("pj", 2)
            pa = pj[pa_i]
            for k in range(8):
                mm(pa[:, 0:n], wkr[:, k, 0, :], hT[:, k, s0:s0 + n], k == 0, k == 7, ["wkr"] + hkeys, [("pj", pa_i)])
            pb_i = nxt("pj", 2)
            pb = pj[pb_i]
            for k in range(8):
                mm(pb[:, 0:n], wkr[:, k, 1, :], hT[:, k, s0:s0 + n], k == 0, k == 7, ["wkr"] + hkeys, [("pj", pb_i)])
            rope_combine(pa, ("pj", pa_i), pb, ("pj", pb_i), krope, s0, n, [("S2", "krope", s0)], ["S2"])
        for c in range(4):
            for j in range(2):
                h = 2 * c + j
                for (s0, n) in TCH:
                    ckeys = [("S1", "cqn0", s0), ("S1", "cqn1", s0), "S1"]
                    p1i = nxt("pj", 2)
                    p1 = pj[p1i]
                    for k in range(2):
                        mm(p1[:, 0:n], wuq_t[:, k, h * 96:h * 96 + 128], cqn[:, k, s0:s0 + n], k == 0, k == 1,
                           ["wuq"] + ckeys, [("pj", p1i)])
                    p2i = nxt("pj", 2)
                    p2 = pj[p2i]
                    for k in range(2):
                        mm(p2[:, 0:n], wuqrot[:, k, h, :], cqn[:, k, s0:s0 + n], k == 0, k == 1,
                           ["wuqrot"] + ckeys, [("pj", p2i)])
                    copy_op("act", qk[j][0:64, s0:s0 + n], p1[0:64, 0:n], [("pj", p1i)],
                            [("qk", j, b) for b in blocks_of(s0, n)])
                    rope_combine(p1, ("pj", p1i), p2, ("pj", p2i), qk[j], s0, n,
                                 [("qk", j, "aug")], [])
                    p3i = nxt("pj", 2)
                    p3 = pj[p3i]
                    mm(p3[:, 0:n], wukv_t[:, h * 128:h * 128 + 128], ckvn[:, s0:s0 + n], True, True,
                       ["wukv", ("S2", "ckvn", s0), "S2"], [("pj", p3i)])
                    copy_op(evac_eng(), qk[2 + j][0:64, s0:s0 + n], p3[0:64, 0:n], [("pj", p3i)],
                            [("qk", 2 + j, b) for b in blocks_of(s0, n)])
                S.dma("sp", (lambda j: lambda e: e.dma_start(out=qk[2 + j][64:96, :], in_=krope[64:96, :]))(j),
                      reads=[("S2", "krope", s0) for (s0, n) in TCH] + ["S2"], writes=[("qk", 2 + j, "aug")])
            vview = wukv_t[:].rearrange("p (h t d) -> p h t d", h=8, t=2)[:, 2 * c:2 * c + 2, 1, :]
            v_proj_tm(lambda k: vview, "wukv", lambda k, b: ckvn[:, b * 128:(b + 1) * 128],
                      lambda b: [("S2", "ckvn", s0) for (s0, n) in TCH if s0 <= b * 128 < s0 + n] + ["S2"], 1)
            for j in range(2):
                attn_dense(j, 128, 96 ** -0.5, lambda kb: (None, []))
            finish_pair(c)
            if c == 1:
                load_wbr(l, 2)
            if c == 2:
                load_wout_hi(l)
        branch_final(l, 2, first)

    def rope_combine(pa, pakey, pb, pbkey, dst_tile, s0, n, wkeys, rkeys):
        r1 = nxt("tmpf", 2)
        t1 = tmpf[r1]
        S.op("dve", lambda e: e.tensor_tensor(out=t1[64:96, 0:n], in0=pa[64:96, 0:n], in1=ropeT[64:96, 0, s0:s0 + n],
                                              op=ALU.mult), reads=[pakey, "ropeT"], writes=[("tmpf", r1)])
        r2 = nxt("tmpf", 2)
        t2 = tmpf[r2]
        S.op("dve", lambda e: e.tensor_tensor(out=t2[64:96, 0:n], in0=pb[64:96, 0:n], in1=ropeT[64:96, 1, s0:s0 + n],
                                              op=ALU.mult), reads=[pbkey, "ropeT"], writes=[("tmpf", r2)])
        S.op("pool", lambda e: e.tensor_tensor(out=dst_tile[64:96, s0:s0 + n], in0=t1[64:96, 0:n], in1=t2[64:96, 0:n],
                                               op=ALU.add), reads=[("tmpf", r1), ("tmpf", r2)] + rkeys, writes=wkeys)

    for sidx in range(NSEQ):
        first_norm(sidx)
        for l in range(NLAYER):
            mixer_A(l, True)
            mixer_B(l, False)
            mixer_C(l, False)
            wout_phase(l, sidx)

    with nc.allow_non_contiguous_dma(reason='small strided parameter / permuted weight loads'):
        S.emit()
    st.close()
    return nc, S


_CACHE = {}


def kernel(x, meta_tokens, norm_g, w_in, b_f, sinks, q_norm_g, kv_norm_g, w_uq, w_ukv, w_br, w_out, final_norm_g):
    x = np.asarray(x, np.float32)
    B, SEQ, _ = x.shape
    NCORE = 8
    NSEQ = B // NCORE
    NB = SEQ // 128 + 1
    L = NB * 128
    key = (NSEQ, NB)
    if key not in _CACHE:
        _CACHE[key] = build(NSEQ, DEPTH, NB)
    nc, _ = _CACHE[key]
    consts = make_consts(NB)
    xp = np.zeros((B, L, D), np.float32)
    xp[:, 128 - N_META:128, :] = np.asarray(meta_tokens, np.float32)[None]
    xp[:, 128:, :] = x
    shared = {
        "norm_g": np.asarray(norm_g, np.float32), "w_in": np.asarray(w_in, np.float32),
        "b_f": np.asarray(b_f, np.float32), "sinks": np.asarray(sinks, np.float32),
        "q_norm_g": np.asarray(q_norm_g, np.float32), "kv_norm_g": np.asarray(kv_norm_g, np.float32),
        "w_uq": np.asarray(w_uq, np.float32), "w_ukv": np.asarray(w_ukv, np.float32),
        "w_br": np.asarray(w_br, np.float32), "w_out": np.asarray(w_out, np.float32),
        "final_norm_g": np.asarray(final_norm_g, np.float32).reshape(1, D),
    }
    shared.update(consts)
    in_maps = []
    for c in range(NCORE):
        m = dict(shared)
        m["xin"] = np.ascontiguousarray(xp[c * NSEQ:(c + 1) * NSEQ])
        in_maps.append(m)
    res = run_bass_kernel_spmd(nc, in_maps, core_ids=list(range(NCORE)))
    outs = [np.asarray(r["out"], np.float32) for r in res.results]
    return np.concatenate(outs, axis=0)
```

```python
import contextlib
import math
import numpy as np
import concourse.bass as bass
import concourse.mybir as mybir
from concourse.bass_utils import run_bass_kernel_spmd

F32 = mybir.dt.float32
BF16 = mybir.dt.bfloat16
AF = mybir.ActivationFunctionType
ALU = mybir.AluOpType

D = 1024
DEPTH = 4
N_META = 16
HD = 64
D_IN = 7336
AQ, AK, AV, BQ, BK, BV, BFO, CQ, CKV, CKR, ZO, GO = 0, 512, 640, 768, 1280, 1792, 2304, 2312, 2568, 2696, 2728, 4264
EPS = 1e-6
MASKV = -240000.0
ENGINES = ("pe", "act", "dve", "pool", "sp")


class Op:
    __slots__ = ("eng", "fn", "is_dma", "deps", "needs_inc", "tok", "ring")

    def __init__(self, eng, fn, is_dma):
        self.eng = eng
        self.fn = fn
        self.is_dma = is_dma
        self.deps = []
        self.needs_inc = False
        self.tok = None
        self.ring = None


class Sched:
    def __init__(self, nc):
        self.nc = nc
        self.ops = {e: [] for e in ENGINES}
        self.last_w = {}
        self.readers = {}
        self.dma_ring = {"sp": 24, "pool": 12, "act": 4}

    def _add(self, eng, fn, reads, writes, is_dma):
        op = Op(eng, fn, is_dma)
        deps = op.deps
        lw = self.last_w
        rd = self.readers
        for r in reads:
            w = lw.get(r)
            if w is not None:
                deps.append((w, 0))
        for k in writes:
            w = lw.get(k)
            if w is not None:
                deps.append((w, 1))
            lst = rd.get(k)
            if lst:
                for x in lst:
                    deps.append((x, 2))
        for r in reads:
            lst = rd.get(r)
            if lst is None:
                rd[r] = [op]
            else:
                lst.append(op)
        for k in writes:
            lw[k] = op
            rd[k] = []
        self.ops[eng].append(op)
        return op

    def op(self, eng, fn, reads=(), writes=()):
        return self._add(eng, fn, reads, writes, False)

    def dma(self, eng, fn, reads=(), writes=()):
        return self._add(eng, fn, reads, writes, True)

    def emit(self):
        nc = self.nc
        for e in ENGINES:
            for op in self.ops[e]:
                kept = []
                seen = set()
                for (p, kind) in op.deps:
                    if p is op or id(p) in seen:
                        continue
                    if (not p.is_dma) and (not op.is_dma) and p.eng == op.eng:
                        if p.eng == "pe":
                            continue
                    seen.add(id(p))
                    p.needs_inc = True
                    kept.append(p)
                op.deps = kept
        stack = contextlib.ExitStack()
        eng_sem = {e: stack.enter_context(nc.semaphore("c_" + e)) for e in ENGINES}
        ring_sems = {e: [stack.enter_context(nc.semaphore("d_%s%d" % (e, i))) for i in range(n)]
                     for e, n in self.dma_ring.items()}
        final_vals = {}
        for e in ENGINES:
            cnt = 0
            dcnt = 0
            for op in self.ops[e]:
                if op.is_dma:
                    n = self.dma_ring[e]
                    slot, k = dcnt % n, dcnt // n
                    op.ring = (ring_sems[e][slot], 16 * k)
                    op.tok = (ring_sems[e][slot], 16 * (k + 1))
                    final_vals[id(op.tok[0])] = op.tok
                    dcnt += 1
                elif op.needs_inc:
                    cnt += 1
                    op.tok = (eng_sem[e], cnt)
        self.stats = {e: len(self.ops[e]) for e in ENGINES}
        ops = self.ops

        def body(e):
            def run(engh):
                waited = {}
                for op in ops[e]:
                    need = {}
                    if op.is_dma and op.ring[1] > 0:
                        need[id(op.ring[0])] = op.ring
                    for p in op.deps:
                        s, v = p.tok
                        cur = need.get(id(s))
                        if cur is None or cur[1] < v:
                            need[id(s)] = (s, v)
                    for sid, (s, v) in need.items():
                        if waited.get(sid, 0) >= v:
                            continue
                        engh.wait_ge(s, v)
                        waited[sid] = v
                    ins = op.fn(engh)
                    if op.is_dma:
                        ins.then_inc(op.tok[0], 16)
                    elif op.needs_inc:
                        ins.then_inc(op.tok[0], 1)
                if e == "sp":
                    for (s, v) in final_vals.values():
                        if waited.get(id(s), 0) < v:
                            engh.wait_ge(s, v)
                            waited[id(s)] = v
            return run

        with nc.Block() as block:
            block.tensor(body("pe"))
            block.scalar(body("act"))
            block.vector(body("dve"))
            block.gpsimd(body("pool"))
            block.sync(body("sp"))
        stack.close()


def make_consts(NB):
    L = NB * 128
    pad = 128 - N_META
    ident = np.eye(128, dtype=np.float32)
    kk = np.arange(128)[:, None]
    qq = np.arange(128)[None, :]
    cmask = np.where(kk <= qq, 0.0, MASKV).astype(np.float32)
    slopes = 2.0 ** (-8.0 * (np.arange(8) + 1.0) / 8)
    abias = np.zeros((128, 8, 2, 128), np.float32)
    for h in range(8):
        dprev = 128 + qq - kk
        dcur = qq - kk
        abias[:, h, 0, :] = np.where(dprev < 128, -8.0 * slopes[h] * dprev, MASKV)
        abias[:, h, 1, :] = np.where(dcur >= 0, -8.0 * slopes[h] * dcur, MASKV)
    pos = (np.arange(L) - pad).astype(np.float32)
    inv = (10000.0 ** (-np.arange(16, dtype=np.float32) / 16)).astype(np.float32)
    ang = pos[None, :] * inv[:, None]
    cos = np.cos(ang).astype(np.float32)
    sin = np.sin(ang).astype(np.float32)
    rope = np.zeros((128, 2, L), np.float32)
    rope[64:80, 0] = cos
    rope[80:96, 0] = cos
    rope[64:80, 1] = -sin
    rope[80:96, 1] = sin
    return {"c_ident": ident, "c_cmask": cmask, "c_abias": abias, "c_rope": rope}


def build(NSEQ, NLAYER, NB):
    L = NB * 128
    TCH = [(s, min(512, L - s)) for s in range(0, L, 512)]
    nc = bass.Bass("TRN2", target_bir_lowering=False)

    def din(name, shape):
        return nc.dram_tensor(name, list(shape), F32, kind="ExternalInput").ap()

    xin = din("xin", [NSEQ, L, D])
    norm_g = din("norm_g", [DEPTH, D])
    w_in = din("w_in", [DEPTH, D, D_IN])
    b_f = din("b_f", [DEPTH, 8])
    sinks = din("sinks", [DEPTH, 8])
    q_norm_g = din("q_norm_g", [DEPTH, 256])
    kv_norm_g = din("kv_norm_g", [DEPTH, 128])
    w_uq = din("w_uq", [DEPTH, 256, 768])
    w_ukv = din("w_ukv", [DEPTH, 128, 1024])
    w_br = din("w_br", [DEPTH, 3, 512, D])
    w_out = din("w_out", [DEPTH, D, D])
    final_g = din("final_norm_g", [1, D])
    c_ident = din("c_ident", [128, 128])
    c_cmask = din("c_cmask", [128, 128])
    c_abias = din("c_abias", [128, 8, 2, 128])
    c_rope = din("c_rope", [128, 2, L])
    out = nc.dram_tensor("out", [NSEQ, L - 128, D], F32, kind="ExternalOutput").ap()
    xs = nc.dram_tensor("xs_scr", [L, D], F32, kind="Internal").ap()
    ys = nc.dram_tensor("ys_scr", [8, 128, L], F32, kind="Internal").ap()

    st = contextlib.ExitStack()

    def sb(name, shape, dt):
        return st.enter_context(nc.sbuf_tensor(name, list(shape), dt))

    def pst(name, shape, dt):
        return st.enter_context(nc.psum_tensor(name, list(shape), dt))

    hT = sb("hT", [128, 8, L], BF16)
    uT = sb("uT", [128, 4, L], BF16)
    qk = [sb("qk%d" % i, [128, L], BF16) for i in range(4)]
    vp = sb("vp", [128, NB, 2, 65], BF16)
    opair = sb("opair", [128, NB, 128], BF16)
    NPT = 4
    PT = [sb("pt%d" % i, [128, 512], BF16) for i in range(NPT)]
    NWR = 6
    wring = [sb("wr%d" % i, [128, 8, 128], BF16) for i in range(NWR)]
    wbig = sb("wbig", [128, 8, 1024], BF16)
    wuq_t = sb("wuq", [128, 2, 800], BF16)
    wuqrot = sb("wuqrot", [128, 2, 8, 128], BF16)
    wukv_t = sb("wukv", [128, 1024], BF16)
    wkr = sb("wkr", [128, 8, 2, 128], BF16)
    wbf = sb("wbf", [128, 8, 8], BF16)
    S1 = sb("S1", [128, L], F32)
    S2 = sb("S2", [128, L], F32)
    S3 = sb("S3", [128, max(L, 2048)], F32)
    S1b = S1.bitcast(BF16)
    S2b = S2.bitcast(BF16)
    S3b = S3.bitcast(BF16)
    rstdb = sb("rstdb", [128, 512], F32)
    ropeT = sb("ropeT", [128, 2, L], BF16)
    abias = sb("abias", [128, 8, 2, 128], BF16)
    ident = sb("ident", [128, 128], BF16)
    identf = sb("identf", [128, 128], F32)
    onesf = sb("onesf", [128, 128], F32)
    cmask = sb("cmask", [128, 128], BF16)
    gbuf = [sb("gb%d" % i, [128, D], F32) for i in range(2)]
    negbf = sb("negbf", [8, DEPTH], F32)
    esink = sb("esink", [128, DEPTH * 8], F32)
    gq = sb("gq", [128, DEPTH, 2], F32)
    gkv = sb("gkv", [128, DEPTH], F32)
    one_c = sb("one_c", [128, 1], F32)
    NXT = 3
    xt = [sb("xt%d" % i, [128, D], F32) for i in range(NXT)]
    hb = [sb("hb%d" % i, [128, D], BF16) for i in range(3)]
    st_ss = [sb("ss%d" % i, [128, 1], F32) for i in range(3)]
    st_ms = [sb("ms%d" % i, [128, 1], F32) for i in range(3)]
    st_rs = [sb("rs%d" % i, [128, 1], F32) for i in range(3)]
    tnh = [sb("tnh%d" % i, [128, 512], BF16) for i in range(2)]
    tmpf = [sb("tmpf%d" % i, [128, 512], F32) for i in range(2)]
    yold = [sb("yold%d" % i, [128, 512], F32) for i in range(2)]
    GT = sb("GT", [128, NB * 8], F32)
    den = [sb("den%d" % i, [128, 1], F32) for i in range(4)]
    den8 = [sb("den8_%d" % i, [128, 8], F32) for i in range(4)]
    eps_c = sb("eps_c", [128, 1], F32)
    osb = [sb("osb%d" % i, [128, 455], F32) for i in range(3)]

    pj = [pst("pj%d" % i, [128, 512], F32) for i in range(2)]
    stp = [pst("st%d" % i, [128, 512], F32) for i in range(2)]
    ob = [pst("ob%d" % i, [128, 512], F32) for i in range(3)]
    trb = pst("trb", [128, 1024], BF16)
    trf = trb.bitcast(F32)

    S = Sched(nc)
    cnt = {"pj": 0, "st": 0, "pt": 0, "wr": 0, "tnh": 0, "tmpf": 0, "yold": 0, "yblk": 0, "xt": 0, "den": 0,
           "rtm": 0, "evac": 0, "trs": 0, "st4": 0}

    def nxt(name, n):
        v = cnt[name] % n
        cnt[name] += 1
        return v

    def evac_eng():
        cnt["evac"] += 1
        return "act" if cnt["evac"] % 2 == 0 else "dve"

    def copy_op(eng, out_ap, in_ap, reads, writes):
        if eng == "act":
            S.op("act", lambda e: e.copy(out=out_ap, in_=in_ap), reads, writes)
        else:
            S.op("dve", lambda e: e.tensor_copy(out=out_ap, in_=in_ap), reads, writes)

    def mm(out_ap, lhsT, rhs, start, stop, reads, writes):
        S.op("pe", lambda e: e.matmul(out_ap, lhsT=lhsT, rhs=rhs, start=start, stop=stop, skip_group_check=True),
             reads, writes)

    TRBK = ["trb"]

    def blocks_of(s0, n):
        return range(s0 // 128, (s0 + n) // 128)

    S.dma("pool", lambda e: e.dma_start(out=ident[:], in_=c_ident), writes=["ident"])
    S.dma("pool", lambda e: e.dma_start(out=cmask[:], in_=c_cmask), writes=["cmask"])
    S.dma("pool", lambda e: e.dma_start(out=abias[:], in_=c_abias), writes=["abias"])
    S.dma("pool", lambda e: e.dma_start(out=ropeT[:], in_=c_rope), writes=["ropeT"])
    S.dma("sp", lambda e: e.dma_start(out=identf[:], in_=c_ident), writes=["identf"])
    S.op("pool", lambda e: e.memset(onesf[:], 1.0), writes=["onesf"])
    S.op("pool", lambda e: e.memset(one_c[:], 1.0), writes=["one_c"])
    S.op("pool", lambda e: e.memset(eps_c[:], EPS), writes=["eps_c"])
    S.op("pool", lambda e: e.memset(vp[:, :, :, 64:65], 1.0), writes=["vp_ones"])
    S.op("pool", lambda e: e.memset(vp[0:128 - N_META, 0:1, :, 64:65], 0.0), reads=[], writes=["vp_ones"])
    for _i in range(4):
        S.op("pool", (lambda _i: lambda e: e.memset(qk[_i][:], 0.0))(_i),
             writes=[("qk", _i, b) for b in range(NB)] + [("qk", _i, "aug")])
    S.op("pool", lambda e: e.memset(wuqrot[:], 0.0), writes=["wuqrot"])
    S.op("pool", lambda e: e.memset(wuq_t[:], 0.0), writes=["wuq"])
    S.op("pool", lambda e: e.memset(wkr[:], 0.0), writes=["wkr"])
    with nc.allow_non_contiguous_dma(reason="tiny param loads"):
        S.dma("sp", lambda e: e.dma_start(out=negbf[:], in_=b_f.rearrange("l h -> h l")), writes=["negbf"])
        S.dma("sp", lambda e: e.dma_start(out=gq[:], in_=q_norm_g.rearrange("l (c p) -> p l c", p=128)),
              writes=["gq"])
        S.dma("sp", lambda e: e.dma_start(out=gkv[:], in_=kv_norm_g.rearrange("l p -> p l")), writes=["gkv"])
    S.dma("sp", lambda e: e.dma_start(out=esink[:], in_=sinks.rearrange("l h -> (l h)").partition_broadcast(128)),
          writes=["esink"])
    S.op("dve", lambda e: e.tensor_scalar(out=negbf[:], in0=negbf[:], scalar1=-1.0, scalar2=None, op0=ALU.mult),
         reads=["negbf"], writes=["negbf"])
    S.op("act", lambda e: e.activation(out=esink[:], in_=esink[:], func=AF.Exp), reads=["esink"], writes=["esink"])

    def layer_specs(l):
        sp = []
        for c in range(4):
            sp.append((l, [(ZO + 0 * 512 + c * 128, 128)]))
        for c in range(4):
            g = c // 2
            sp.append((l, [(AQ + c * 128, 128)]))
            sp.append((l, [(AK + g * 64, 64), (AK + g * 64, 64)]))
            sp.append((l, [(AV + g * 64, 64), (AV + g * 64, 64)]))
        for m in range(8):
            sp.append((l, [(GO + 0 * 1024 + m * 128, 128)]))
        for c in range(4):
            sp.append((l, [(ZO + 1 * 512 + c * 128, 128)]))
        for c in range(4):
            sp.append((l, [(BQ + c * 128, 128)]))
            sp.append((l, [(BK + c * 128, 128)]))
            sp.append((l, [(BV + c * 128, 128)]))
        for m in range(8):
            sp.append((l, [(GO + 1 * 1024 + m * 128, 128)]))
        for c in range(4):
            sp.append((l, [(ZO + 2 * 512 + c * 128, 128)]))
        for i in range(2):
            sp.append((l, [(CQ + i * 128, 128)]))
        sp.append((l, [(CKV, 128)]))
        for m in range(8):
            sp.append((l, [(GO + 2 * 1024 + m * 128, 128)]))
        return sp

    WSPECS = []
    for _s in range(NSEQ):
        for _l in range(NLAYER):
            WSPECS.extend(layer_specs(_l))
    wstate = {"issued": 0, "used": 0}
    WPF = NWR - 3

    def _issue_chunk(idx):
        l, col_runs = WSPECS[idx]
        slot = idx % NWR
        t = wring[slot]
        key = ("wr", slot)
        o = 0
        for (c0, n) in col_runs:
            src = w_in[l, :, c0:c0 + n].rearrange("(kc p) n -> p kc n", p=128)
            dst = t[:, :, o:o + n]
            S.dma("pool", (lambda dst, src: lambda e: e.dma_start(out=dst, in_=src))(dst, src), writes=[key])
            o += n

    def wload_chunk(l, col_runs):
        idx = wstate["used"]
        assert WSPECS[idx] == (l, col_runs), (idx, WSPECS[idx], l, col_runs)
        while wstate["issued"] < min(len(WSPECS), idx + 1 + WPF):
            _issue_chunk(wstate["issued"])
            wstate["issued"] += 1
        wstate["used"] += 1
        slot = idx % NWR
        return wring[slot], ("wr", slot)

    def proj_fm(wt, wkey, M, evac_fn, extra_reads=()):
        for (s0, n) in TCH:
            p, pkey = st_next()
            hkeys = [("hT", b) for b in blocks_of(s0, n)]
            for k in range(8):
                mm(p[0:M, 0:n], wt[:, k, 0:M], hT[:, k, s0:s0 + n], k == 0, k == 7,
                   [wkey] + hkeys + list(extra_reads), [pkey])
            evac_fn(p, pkey, s0, n)

    def norm_stats(xtile, xkey, b, gtile, gkey, last, sidx):
        i = b % 3
        hbt = hb[i]
        S.op("act", lambda e: e.activation(out=hbt[:], in_=xtile[:], func=AF.Square, accum_out=st_ss[i][:]),
             reads=[xkey], writes=[("hb", i), ("ss", i)])
        S.op("act", lambda e: e.activation(out=st_ms[i][:], in_=st_ss[i][:], func=AF.Ln, bias=eps_c[:], scale=1.0 / D),
             reads=[("ss", i), "eps_c"], writes=[("ms", i)])
        S.op("act", lambda e: e.activation(out=st_rs[i][:], in_=st_ms[i][:], func=AF.Exp, scale=-0.5),
             reads=[("ms", i)], writes=[("rs", i)])
        if last:
            if b == 0:
                return
            S.op("dve", lambda e: e.scalar_tensor_tensor(out=xtile[:], in0=xtile[:], scalar=st_rs[i][:], in1=gtile[:],
                                                         op0=ALU.mult, op1=ALU.mult),
                 reads=[xkey, ("rs", i), gkey], writes=[xkey])
            S.dma("sp", lambda e: e.dma_start(out=out[sidx, (b - 1) * 128:b * 128, :], in_=xtile[:]), reads=[xkey])
            return
        S.op("dve", lambda e: e.scalar_tensor_tensor(out=hbt[:], in0=xtile[:], scalar=st_rs[i][:], in1=gtile[:],
                                                     op0=ALU.mult, op1=ALU.mult),
             reads=[xkey, ("rs", i), gkey], writes=[("hb", i)])

    def norm_transposes(b):
        i = b % 3
        hbt = hb[i]
        for c in range(8):
            S.op("pe", (lambda c: lambda e: e.transpose(out=trb[:, c * 128:(c + 1) * 128],
                                                        in_=hbt[:, c * 128:(c + 1) * 128], identity=ident[:]))(c),
                 reads=[("hb", i), "ident"], writes=TRBK)
        copy_op(evac_eng(), hT[:, :, b * 128:(b + 1) * 128], trb[:].rearrange("p (c t) -> p c t", c=8),
                TRBK, [("hT", b)])

    def norm_block(xtile, xkey, b, gtile, gkey, last, sidx):
        norm_stats(xtile, xkey, b, gtile, gkey, last, sidx)
        if not last:
            norm_transposes(b)

    def load_gb(which, src_row):
        S.dma("sp", lambda e: e.dma_start(out=gbuf[which][:], in_=src_row.partition_broadcast(128)),
              writes=[("gb", which)])

    SKEW = 3

    def st_next():
        i = nxt("st4", 4)
        return [(stp[0], ("st", 0)), (stp[1], ("st", 1)), (pj[0], ("pj", 0)), (pj[1], ("pj", 1))][i]

    def run_pipeline(stages, qk_stage, pv_stage):
        pend = []
        for stg in stages:
            pend.append(qk_stage(stg))
            if len(pend) > SKEW:
                pv_stage(*pend.pop(0))
        while pend:
            pv_stage(*pend.pop(0))

    def o_ap(qb):
        return ob[qb // 7][:, (qb % 7) * 65:(qb % 7) * 65 + 65], ("ob", qb // 7)

    def normalize(oap, okey, qb, j, sink_ap=None):
        di = nxt("den", 4)
        d = den[di]
        if sink_ap is None:
            S.op("dve", lambda e: e.tensor_scalar(out=d[:], in0=oap[:, 64:65], scalar1=1e-30, scalar2=None,
                                                  op0=ALU.max), reads=[okey], writes=[("den", di)])
        else:
            S.op("dve", lambda e: e.tensor_scalar(out=d[:], in0=oap[:, 64:65], scalar1=sink_ap, scalar2=None,
                                                  op0=ALU.add), reads=[okey, "esink"], writes=[("den", di)])
        S.op("dve", lambda e: e.reciprocal(out=d[:], in_=d[:]), reads=[("den", di)], writes=[("den", di)])
        S.op("dve", lambda e: e.tensor_scalar(out=opair[:, qb, j * 64:(j + 1) * 64], in0=oap[:, 0:64],
                                              scalar1=d[:], scalar2=None, op0=ALU.mult),
             reads=[okey, ("den", di)], writes=[("opair", qb, j)])

    def attn_dense(j, Kd, scale, bias_fn):
        qt, kt = qk[j], qk[2 + j]
        qkeys = lambda s0, n: [("qk", j, b) for b in blocks_of(s0, n)] + [("qk", j, "aug")]
        stages = []
        for kb in range(NB):
            q0 = kb * 128
            for ci, s0 in enumerate(range(q0, L, 512)):
                stages.append((kb, ci, s0, min(512, L - s0)))

        def qk_stage(stg):
            kb, ci, s0, n = stg
            q0 = kb * 128
            sp_, skey = st_next()
            mm(sp_[:, 0:n], kt[0:Kd, q0:q0 + 128], qt[0:Kd, s0:s0 + n], True, ci != 0,
               [("qk", 2 + j, kb), ("qk", 2 + j, "aug")] + qkeys(s0, n), [skey])
            if ci == 0:
                mm(sp_[:, 0:128], ident[:], cmask[:], False, True, ["ident", "cmask"], [skey])
            return stg, sp_, skey

        def pv_stage(stg, sp_, skey):
            kb, ci, s0, n = stg
            pi = nxt("pt", NPT)
            pt = PT[pi]
            pkey = ("pt", pi)
            bias_ap, bias_keys = bias_fn(kb)
            if bias_ap is None:
                S.op("act", (lambda pt, sp_, n: lambda e: e.activation(out=pt[:, 0:n], in_=sp_[:, 0:n],
                                                                       func=AF.Exp, scale=scale))(pt, sp_, n),
                     reads=[skey], writes=[pkey])
            else:
                S.op("act", (lambda pt, sp_, n, bias_ap: lambda e: e.activation(
                    out=pt[:, 0:n], in_=sp_[:, 0:n], func=AF.Exp, bias=bias_ap, scale=scale))(pt, sp_, n, bias_ap),
                    reads=[skey] + bias_keys, writes=[pkey])
            for qi, qb in enumerate(blocks_of(s0, n)):
                oap, okey = o_ap(qb)
                mm(oap, pt[:, qi * 128:(qi + 1) * 128], vp[:, kb, j, :], kb == 0 and qb % 7 == 0, kb == qb,
                   [pkey, ("vp", kb), "vp_ones"], [okey])

        run_pipeline(stages, qk_stage, pv_stage)
        dense_finish(j)

    def dense_finish(j):
        for bank in range((NB + 6) // 7):
            nq = min(7, NB - bank * 7)
            ncol = nq * 65
            copy_op(evac_eng(), osb[bank][:, 0:ncol], ob[bank][:, 0:ncol], [("ob", bank)], [("osb", bank)])
            ov = osb[bank][:, 0:ncol].rearrange("p (q d) -> p q d", d=65)
            di = nxt("den", 4)
            d = den8[di]
            S.op("dve", (lambda d, ov, nq: lambda e: e.tensor_scalar(out=d[:, 0:nq], in0=ov[:, :, 64], scalar1=1e-30,
                                                                    scalar2=None, op0=ALU.max))(d, ov, nq),
                 reads=[("osb", bank)], writes=[("den8", di)])
            S.op("dve", (lambda d, nq: lambda e: e.reciprocal(out=d[:, 0:nq], in_=d[:, 0:nq]))(d, nq),
                 reads=[("den8", di)], writes=[("den8", di)])
            q0 = bank * 7
            S.op("dve", (lambda d, ov, nq, q0: lambda e: e.tensor_tensor(
                out=opair[:, q0:q0 + nq, j * 64:(j + 1) * 64], in0=ov[:, :, 0:64],
                in1=d[:, 0:nq].unsqueeze(2).broadcast_to([128, nq, 64]), op=ALU.mult))(d, ov, nq, q0),
                reads=[("osb", bank), ("den8", di)], writes=[("opair", q0 + q, j) for q in range(nq)])

    def attn_swa(j, h, l):
        qt, kt = qk[j], qk[2 + j]
        sink_ap = esink[:, l * 8 + h:l * 8 + h + 1]

        def qk_stage(qb):
            sp_, skey = st_next()
            parts = []
            if qb >= 1:
                parts.append((qb - 1, 0))
            parts.append((qb, 1))
            for pi_, (kb, which) in enumerate(parts):
                col = pi_ * 128
                mm(sp_[:, col:col + 128], kt[:, kb * 128:(kb + 1) * 128], qt[:, qb * 128:(qb + 1) * 128],
                   True, False, [("qk", 2 + j, kb), ("qk", 2 + j, "aug"), ("qk", j, qb), ("qk", j, "aug")], [skey])
                mm(sp_[:, col:col + 128], ident[:], abias[:, h, which, :], False, True, ["ident", "abias"], [skey])
            return qb, parts, sp_, skey

        def pv_stage(qb, parts, sp_, skey):
            n = 128 * len(parts)
            pi = nxt("pt", NPT)
            pt = PT[pi]
            pkey = ("pt", pi)
            S.op("act", (lambda pt, sp_, n: lambda e: e.activation(out=pt[:, 0:n], in_=sp_[:, 0:n], func=AF.Exp,
                                                                   scale=0.125))(pt, sp_, n),
                 reads=[skey], writes=[pkey])
            oap, okey = o_ap(qb)
            for pi_, (kb, which) in enumerate(parts):
                mm(oap, pt[:, pi_ * 128:(pi_ + 1) * 128], vp[:, kb, j, :], pi_ == 0 and qb % 7 == 0,
                   pi_ == len(parts) - 1, [pkey, ("vp", kb), "vp_ones"], [okey])
            if qb % 7 == 6 or qb == NB - 1:
                bank = qb // 7
                q0 = bank * 7
                nq = qb - q0 + 1
                ov = ob[bank][:, 0:nq * 65].rearrange("p (q d) -> p q d", d=65)
                di = nxt("den", 4)
                d = den8[di]
                S.op("dve", (lambda d, ov, nq: lambda e: e.tensor_scalar(out=d[:, 0:nq], in0=ov[:, :, 64], scalar1=sink_ap,
                                                                        scalar2=None, op0=ALU.add))(d, ov, nq),
                     reads=[("ob", bank), "esink"], writes=[("den8", di)])
                S.op("dve", (lambda d, nq: lambda e: e.reciprocal(out=d[:, 0:nq], in_=d[:, 0:nq]))(d, nq),
                     reads=[("den8", di)], writes=[("den8", di)])
                S.op("dve", (lambda d, ov, nq, q0: lambda e: e.tensor_tensor(
                    out=opair[:, q0:q0 + nq, j * 64:(j + 1) * 64], in0=ov[:, :, 0:64],
                    in1=d[:, 0:nq].unsqueeze(2).broadcast_to([128, nq, 64]), op=ALU.mult))(d, ov, nq, q0),
                    reads=[("ob", bank), ("den8", di)], writes=[("opair", q0 + q, j) for q in range(nq)])

        run_pipeline(list(range(NB)), qk_stage, pv_stage)

    def finish_pair(c):
        banks = [(trb, "trb"), (pj[0].bitcast(BF16), ("pj", 0)), (pj[1].bitcast(BF16), ("pj", 1))]
        for g0 in range(0, NB, 8):
            gn = min(8, NB - g0)
            bi = nxt("trs", 3)
            bt, bkey = banks[bi]
            for qi in range(gn):
                qb = g0 + qi
                S.op("pe", (lambda qb, qi, bt: lambda e: e.transpose(out=bt[:, qi * 128:(qi + 1) * 128],
                                                                   in_=opair[:, qb, :], identity=ident[:]))(qb, qi, bt),
                     reads=[("opair", qb, 0), ("opair", qb, 1), "ident"], writes=[bkey])
            S.op("dve", (lambda g0, gn, bt: lambda e: e.tensor_tensor(
                out=uT[:, c, g0 * 128:(g0 + gn) * 128], in0=bt[:, 0:gn * 128],
                in1=uT[:, c, g0 * 128:(g0 + gn) * 128], op=ALU.mult))(g0, gn, bt),
                reads=[bkey] + [("uT", c, g0 + q) for q in range(gn)], writes=[("uT", c, g0 + q) for q in range(gn)])

    def z_proj(l, i):
        for c in range(4):
            wt, wkey = wload_chunk(l, [(ZO + i * 512 + c * 128, 128)])

            def ev(p, pkey, s0, n, c=c):
                ti = nxt("tnh", 2)
                t = tnh[ti]
                S.op("act", lambda e: e.activation(out=t[:, 0:n], in_=p[:, 0:n], func=AF.Tanh, scale=0.5),
                     reads=[pkey], writes=[("tnh", ti)])
                S.op("dve", lambda e: e.scalar_tensor_tensor(out=uT[:, c, s0:s0 + n], in0=t[:, 0:n], scalar=1.0,
                                                             in1=p[:, 0:n], op0=ALU.add, op1=ALU.mult),
                     reads=[("tnh", ti), pkey], writes=[("uT", c, b) for b in blocks_of(s0, n)])
            proj_fm(wt, wkey, 128, ev)

    def yT_ap(m, s0, n):
        if m < 4:
            return qk[m][:, s0:s0 + n]
        if m == 4:
            return S1b[:, s0:s0 + n]
        if m == 5:
            return S1b[:, L + s0:L + s0 + n]
        if m == 6:
            return S2b[:, s0:s0 + n]
        return S2b[:, L + s0:L + s0 + n]

    def yT_keys(m, s0, n):
        c0 = (s0 // 512) * 512
        if m < 4:
            return [("qk", m, b) for b in blocks_of(s0, n)] + [("qk", m, "aug")], []
        fine = {4: ("S1", "cqn0", c0), 5: ("S1", "cqn1", c0), 6: ("S2", "ckvn", c0), 7: ("S2", "krope", c0)}[m]
        return [fine], ["S1" if m < 6 else "S2"]

    def load_wbr(l, i):
        src = w_br[l, i].rearrange("(c p) n -> p c n", p=128)
        S.dma("pool", lambda e: e.dma_start(out=wbig[:, 0:4, :], in_=src), writes=[("wbig", 0)])

    def branch_final(l, i, first):
        its = [(m, s0, n) for m in range(8) for (s0, n) in TCH]
        yolds = {}

        def issue_yold(k):
            if first or k >= len(its):
                return
            m, s0, n = its[k]
            yi = nxt("yold", 2)
            yo = yold[yi]
            S.dma("sp", (lambda yo, m, s0, n: lambda e: e.dma_start(out=yo[:, 0:n], in_=ys[m, :, s0:s0 + n]))(yo, m, s0, n),
                  reads=[("ys", m, s0)], writes=[("yold", yi)])
            yolds[k] = (yo, yi)

        issue_yold(0)
        wt = wkey = None
        for k, (m, s0, n) in enumerate(its):
            if s0 == 0:
                wt, wkey = wload_chunk(l, [(GO + i * 1024 + m * 128, 128)])
            issue_yold(k + 1)
            p, pkey = st_next()
            hkeys = [("hT", b) for b in blocks_of(s0, n)]
            for kk in range(8):
                mm(p[:, 0:n], wt[:, kk, :], hT[:, kk, s0:s0 + n], kk == 0, kk == 7, [wkey] + hkeys, [pkey])
            ti = nxt("tnh", 2)
            t = tnh[ti]
            S.op("act", (lambda t, p, n: lambda e: e.activation(out=t[:, 0:n], in_=p[:, 0:n], func=AF.Tanh,
                                                                scale=0.5))(t, p, n),
                 reads=[pkey], writes=[("tnh", ti)])
            p2, pkey2 = st_next()
            for c in range(4):
                mm(p2[:, 0:n], wbig[:, c, m * 128:(m + 1) * 128], uT[:, c, s0:s0 + n], c == 0, c == 3,
                   [("wbig", 0)] + [("uT", c, b) for b in blocks_of(s0, n)], [pkey2])
            fi = nxt("tmpf", 2)
            tf = tmpf[fi]
            S.op("dve", (lambda tf, t, p2, n: lambda e: e.scalar_tensor_tensor(
                out=tf[:, 0:n], in0=t[:, 0:n], scalar=1.0, in1=p2[:, 0:n], op0=ALU.add, op1=ALU.mult))(tf, t, p2, n),
                reads=[("tnh", ti), pkey2], writes=[("tmpf", fi)])
            ykey = ("ys", m, s0)
            ydst = ys[m, :, s0:s0 + n]
            if i == 2 and not first:
                yo, yi = yolds.pop(k)
                wk_, rk_ = yT_keys(m, s0, n)
                yap = yT_ap(m, s0, n)
                S.op("dve", (lambda tf, yo, n, yap: lambda e: e.tensor_tensor(out=yap, in0=tf[:, 0:n],
                                                                              in1=yo[:, 0:n], op=ALU.add))(tf, yo, n, yap),
                     reads=[("tmpf", fi), ("yold", yi)] + rk_, writes=wk_)
                continue
            if not first:
                yo, yi = yolds.pop(k)
                S.op("dve", (lambda tf, yo, n: lambda e: e.tensor_tensor(out=tf[:, 0:n], in0=tf[:, 0:n],
                                                                         in1=yo[:, 0:n], op=ALU.add))(tf, yo, n),
                     reads=[("tmpf", fi), ("yold", yi)], writes=[("tmpf", fi)])
            S.dma("sp", (lambda tf, ydst, n: lambda e: e.dma_start(out=ydst, in_=tf[:, 0:n]))(tf, ydst, n),
                  reads=[("tmpf", fi)], writes=[ykey])

    def load_wout_hi(l):
        src = w_out[l, 512:1024, :].rearrange("(c p) n -> p c n", p=128)
        S.dma("pool", lambda e: e.dma_start(out=wbig[:, 4:8, :], in_=src), writes=[("wbig", 1)])

    wlo = S3b[:, 0:4096].rearrange("p (c n) -> p c n", c=4)

    def load_wout_lo(l):
        src = w_out[l, 0:512, :].rearrange("(c p) n -> p c n", p=128)
        S.dma("pool", lambda e: e.dma_start(out=wlo, in_=src), writes=["S3"])

    def wout_phase(l, sidx):
        last = (l == NLAYER - 1)
        gi = (l + 1) % 2
        load_gb(gi, final_g[0] if last else norm_g[l + 1])
        xsrc = xin[sidx] if l == 0 else xs
        pending = []
        xbuf = {}

        def issue_xload(b):
            if b >= NB:
                return
            xi = nxt("xt", NXT)
            x = xt[xi]
            xkey = ("xt", xi)
            S.dma("sp", (lambda x, b: lambda e: e.dma_start(out=x[:], in_=xsrc[b * 128:(b + 1) * 128, :]))(x, b),
                  reads=[("xs", b)] if l > 0 else [], writes=[xkey])
            xbuf[b] = (x, xkey)

        issue_xload(0)
        for b in range(NB):
            issue_xload(b + 1)
            x, xkey = xbuf.pop(b)
            for half in range(2):
                p, pkey = st_next()
                order = [4, 5, 6, 7, 0, 1, 2, 3]
                for oi, m in enumerate(order):
                    wk_, rk_ = yT_keys(m, b * 128, 128)
                    if m < 4:
                        rhs, wkeys = wlo[:, m, half * 512:(half + 1) * 512], ["S3"]
                    else:
                        rhs, wkeys = wbig[:, m, half * 512:(half + 1) * 512], [("wbig", 1)]
                    mm(p[:, :], yT_ap(m, b * 128, 128), rhs, oi == 0, oi == 7, wk_ + rk_ + wkeys, [pkey])
                S.op("dve", (lambda x, p, half: lambda e: e.scalar_tensor_tensor(
                    out=x[:, half * 512:(half + 1) * 512], in0=p[:, :], scalar=0.25,
                    in1=x[:, half * 512:(half + 1) * 512], op0=ALU.mult, op1=ALU.add))(x, p, half),
                    reads=[pkey, xkey], writes=[xkey])
            if not last:
                S.dma("sp", (lambda x, b: lambda e: e.dma_start(out=xs[b * 128:(b + 1) * 128, :], in_=x[:]))(x, b),
                      reads=[xkey], writes=[("xs", b)])
            norm_stats(x, xkey, b, gbuf[gi], ("gb", gi), last, sidx)
            pending.append(b)
            if len(pending) > 2 and not last:
                norm_transposes(pending.pop(0))
        while pending and not last:
            norm_transposes(pending.pop(0))

    def first_norm(sidx):
        load_gb(0, norm_g[0])
        for b in range(NB):
            xi = nxt("xt", NXT)
            x = xt[xi]
            xkey = ("xt", xi)
            S.dma("sp", (lambda x, b: lambda e: e.dma_start(out=x[:], in_=xin[sidx, b * 128:(b + 1) * 128, :]))(x, b),
                  writes=[xkey])
            norm_block(x, xkey, b, gbuf[0], ("gb", 0), False, sidx)

    S1K = ["S1"]
    S2K = ["S2"]
    S3K = ["S3"]

    def evac_pair(p, pkey, s0, n, t0, k0, t1, k1, rows=64):
        copy_op("act", t0[0:rows, s0:s0 + n], p[0:rows, 0:n], [pkey], [(k0[0], k0[1], b) for b in blocks_of(s0, n)])
        copy_op("dve", t1[0:rows, s0:s0 + n], p[64:64 + rows, 0:n], [pkey],
                [(k1[0], k1[1], b) for b in blocks_of(s0, n)])

    def v_proj_tm(wt, wkey, lhs_fn, lhs_keys_fn, nk):
        for b in range(NB):
            p, pkey = st_next()
            for k in range(nk):
                mm(p[:, 0:128], lhs_fn(k, b), wt(k), k == 0, k == nk - 1, [wkey] + lhs_keys_fn(b), [pkey])
            copy_op(evac_eng(), vp[:, b, :, 0:64], p[:, 0:128].rearrange("p (j d) -> p j d", j=2), [pkey],
                    [("vp", b)])

    def mixer_A(l, first=True):
        for _i in range(2):
            S.op("pool", (lambda _i: lambda e: e.memset(qk[_i][64:128, :], 0.0))(_i), writes=[("qk", _i, "aug")])
        z_proj(l, 0)
        for c in range(4):
            g = c // 2
            wq_, wqk = wload_chunk(l, [(AQ + c * 128, 128)])
            proj_fm(wq_, wqk, 128, lambda p, pkey, s0, n: evac_pair(p, pkey, s0, n, qk[0], ("qk", 0), qk[1], ("qk", 1)))
            wk_, wkk = wload_chunk(l, [(AK + g * 64, 64), (AK + g * 64, 64)])
            proj_fm(wk_, wkk, 128, lambda p, pkey, s0, n: evac_pair(p, pkey, s0, n, qk[2], ("qk", 2), qk[3], ("qk", 3)))
            wv_, wvk = wload_chunk(l, [(AV + g * 64, 64), (AV + g * 64, 64)])
            v_proj_tm(lambda k: wv_[:, k, :], wvk, lambda k, b: hT[:, k, b * 128:(b + 1) * 128],
                      lambda b: [("hT", b)], 8)
            for j in range(2):
                attn_swa(j, 2 * c + j, l)
            finish_pair(c)
            if c == 1:
                load_wbr(l, 0)
        branch_final(l, 0, first)

    def fox_gates(l):
        src = w_in[l, :, BFO:BFO + 8].rearrange("(kc p) n -> p kc n", p=128)
        with nc.allow_non_contiguous_dma(reason="8-col gate weights"):
            S.dma("pool", lambda e: e.dma_start(out=wbf[:], in_=src), writes=["wbf"])
        for (s0, n) in TCH:
            p, pkey = st_next()
            for k in range(8):
                mm(p[0:8, 0:n], wbf[:, k, :], hT[:, k, s0:s0 + n], k == 0, k == 7,
                   ["wbf"] + [("hT", b) for b in blocks_of(s0, n)], [pkey])
            S.op("act", (lambda p, s0, n: lambda e: e.activation(out=S1[0:8, s0:s0 + n], in_=p[0:8, 0:n], func=AF.Exp,
                                                                 bias=negbf[:, l:l + 1], scale=-1.0))(p, s0, n),
                 reads=[pkey, "negbf"], writes=S1K)
        S.op("act", lambda e: e.activation(out=S1[0:8, :], in_=S1[0:8, :], func=AF.Ln, bias=one_c[0:8, :], scale=1.0),
             reads=S1K + ["one_c"], writes=S1K)
        S.op("dve", lambda e: e.tensor_tensor_scan(out=S2[0:8, :], data0=S1[0:8, :], data1=S1[0:8, :], initial=0.0,
                                                   op0=ALU.add, op1=ALU.max), reads=S1K, writes=S2K)
        fq = S3b[0:8, 0:2 * L].rearrange("p (t l) -> p t l", t=2)
        S.op("dve", lambda e: e.tensor_scalar(out=fq[:, 0, :], in0=S2[0:8, :], scalar1=-8.0, scalar2=None,
                                              op0=ALU.mult), reads=S2K, writes=S3K)
        S.op("dve", lambda e: e.scalar_tensor_tensor(out=fq[:, 1, :], in0=S2[0:8, :], scalar=-8.0, in1=fq[:, 0, :],
                                                     op0=ALU.mult, op1=ALU.subtract), reads=S2K + S3K, writes=S3K)
        return fq

    def fox_gates_T():
        for b in range(NB):
            S.op("pe", (lambda b: lambda e: e.transpose(out=trf[:, b * 8:(b + 1) * 8], in_=S2[0:8, b * 128:(b + 1) * 128],
                                                        identity=identf[0:8, 0:8]))(b),
                 reads=S2K + ["identf"], writes=TRBK)
        S.op("dve", lambda e: e.tensor_copy(out=GT[:], in_=trf[:, 0:NB * 8]), reads=TRBK, writes=["GT"])

    def mixer_B(l, first=False):
        fq = fox_gates(l)
        for _i in range(4):
            S.op("pool", (lambda _i: lambda e: e.memset(qk[_i][64:128, :], 0.0))(_i), writes=[("qk", _i, "aug")])
        z_proj(l, 1)
        fox_gates_T()
        for c in range(4):
            wq_, wqk = wload_chunk(l, [(BQ + c * 128, 128)])
            proj_fm(wq_, wqk, 128, lambda p, pkey, s0, n: evac_pair(p, pkey, s0, n, qk[0], ("qk", 0), qk[1], ("qk", 1)))
            wk_, wkk = wload_chunk(l, [(BK + c * 128, 128)])
            proj_fm(wk_, wkk, 128, lambda p, pkey, s0, n: evac_pair(p, pkey, s0, n, qk[2], ("qk", 2), qk[3], ("qk", 3)))
            wv_, wvk = wload_chunk(l, [(BV + c * 128, 128)])
            v_proj_tm(lambda k: wv_[:, k, :], wvk, lambda k, b: hT[:, k, b * 128:(b + 1) * 128],
                      lambda b: [("hT", b)], 8)
            for j in range(2):
                h = 2 * c + j
                for t_ in range(2):
                    S.dma("sp", (lambda j, h, t_: lambda e: e.dma_start(out=qk[j][64 + t_:65 + t_, :],
                                                                        in_=fq[h:h + 1, t_, :]))(j, h, t_),
                          reads=S3K, writes=[("qk", j, "aug")])
                S.op("pool", (lambda j: lambda e: e.memset(qk[2 + j][64:66, :], 1.0))(j), writes=[("qk", 2 + j, "aug")])
            for j in range(2):
                h = 2 * c + j
                attn_dense(j, 128, 0.125,
                           lambda kb, h=h: (GT[:, kb * 8 + h:kb * 8 + h + 1], ["GT"]))
            finish_pair(c)
            if c == 1:
                load_wbr(l, 1)
        branch_final(l, 1, first)

    def mixer_C(l, first=False):
        z_proj(l, 2)
        S.dma("pool", lambda e: e.dma_start(out=wuq_t[:, :, 0:768], in_=w_uq[l].rearrange("(c p) n -> p c n", p=128)),
              writes=["wuq"])
        S.dma("pool", lambda e: e.dma_start(out=wukv_t[:], in_=w_ukv[l]), writes=["wukv"])
        uqv = w_uq[l].rearrange("(c p) (h r) -> p c h r", p=128, r=96)
        with nc.allow_non_contiguous_dma(reason="rope column permutation"):
            for c2 in range(2):
                S.dma("pool", (lambda c2: lambda e: e.dma_start(out=wuqrot[:, c2, :, 64:80], in_=uqv[:, c2, :, 80:96]))(c2),
                      writes=["wuqrot"])
                S.dma("pool", (lambda c2: lambda e: e.dma_start(out=wuqrot[:, c2, :, 80:96], in_=uqv[:, c2, :, 64:80]))(c2),
                      writes=["wuqrot"])
            krv = w_in[l].rearrange("(kc p) n -> p kc n", p=128)
            S.dma("pool", lambda e: e.dma_start(out=wkr[:, :, 0, 64:96], in_=krv[:, :, CKR:CKR + 32]), writes=["wkr"])
            S.dma("pool", lambda e: e.dma_start(out=wkr[:, :, 1, 64:80], in_=krv[:, :, CKR + 16:CKR + 32]),
                  writes=["wkr"])
            S.dma("pool", lambda e: e.dma_start(out=wkr[:, :, 1, 80:96], in_=krv[:, :, CKR:CKR + 16]), writes=["wkr"])
        cqn = S1b[:, 0:2 * L].rearrange("p (c l) -> p c l", c=2)
        ckvn = S2b[:, 0:L]
        krope = S2b[:, L:2 * L]
        wcq = [wload_chunk(l, [(CQ + i * 128, 128)]) for i in range(2)]
        wckv = wload_chunk(l, [(CKV, 128)])
        for (s0, n) in TCH:
            hkeys = [("hT", b) for b in blocks_of(s0, n)]
            for grp, wl, nfeat in (("q", wcq, 256), ("kv", [wckv], 128)):
                raws = []
                for gi_, (wt, wkey) in enumerate(wl):
                    p, pkey = st_next()
                    for k in range(8):
                        mm(p[:, 0:n], wt[:, k, :], hT[:, k, s0:s0 + n], k == 0, k == 7, [wkey] + hkeys, [pkey])
                    ri = gi_ if grp == "q" else 2
                    raw = S3[:, ri * 512:ri * 512 + n]
                    rkey = ("S3", "raw%d" % ri)
                    S.op("act", (lambda raw, p, n: lambda e: e.copy(out=raw, in_=p[:, 0:n]))(raw, p, n),
                         reads=[pkey, "S3"], writes=[rkey])
                    raws.append((raw, rkey, gi_))
                si = nxt("st", 2)
                sps = stp[si]
                skey = ("st", si)
                for idx, (raw, rkey, gi_) in enumerate(raws):
                    sq = S3[:, 1536:1536 + n]
                    S.op("act", (lambda sq, raw: lambda e: e.activation(out=sq, in_=raw, func=AF.Square))(sq, raw),
                         reads=[rkey, "S3"], writes=[("S3", "sq")])
                    mm(sps[:, 0:n], onesf[:], sq, idx == 0, idx == len(raws) - 1, [("S3", "sq"), "onesf"], [skey])
                S.op("act", (lambda sps, n, nfeat: lambda e: e.activation(
                    out=rstdb[:, 0:n], in_=sps[:, 0:n], func=AF.Ln, bias=eps_c[:], scale=1.0 / nfeat))(sps, n, nfeat),
                    reads=[skey, "eps_c"], writes=["rstdb"])
                S.op("act", (lambda n: lambda e: e.activation(out=rstdb[:, 0:n], in_=rstdb[:, 0:n], func=AF.Exp,
                                                              scale=-0.5))(n),
                     reads=["rstdb"], writes=["rstdb"])
                for (raw, rkey, gi_) in raws:
                    if grp == "q":
                        dst, dkey, gsc = cqn[:, gi_, s0:s0 + n], ("S1", "cqn%d" % gi_), gq[:, l, gi_:gi_ + 1]
                        kk_ = [dkey, ]
                        rd = ["S1"]
                    else:
                        dst, dkey, gsc = ckvn[:, s0:s0 + n], ("S2", "ckvn"), gkv[:, l:l + 1]
                        rd = ["S2"]
                    S.op("dve", (lambda dst, raw, gsc, n: lambda e: e.scalar_tensor_tensor(
                        out=dst, in0=raw, scalar=gsc, in1=rstdb[:, 0:n], op0=ALU.mult, op1=ALU.mult))(dst, raw, gsc, n),
                        reads=[rkey, "rstdb", "gq", "gkv", "S3"] + rd, writes=[(dkey[0], dkey[1], s0)])
            pa, pa_i = st_next()
            for k in range(8):
                mm(pa[:, 0:n], wkr[:, k, 0, :], hT[:, k, s0:s0 + n], k == 0, k == 7, ["wkr"] + hkeys, [pa_i])
            pb, pb_i = st_next()
            for k in range(8):
                mm(pb[:, 0:n], wkr[:, k, 1, :], hT[:, k, s0:s0 + n], k == 0, k == 7, ["wkr"] + hkeys, [pb_i])
            rope_combine(pa, pa_i, pb, pb_i, krope, s0, n, [("S2", "krope", s0)], ["S2"])
        for c in range(4):
            for j in range(2):
                h = 2 * c + j
                for (s0, n) in TCH:
                    ckeys = [("S1", "cqn0", s0), ("S1", "cqn1", s0), "S1"]
                    p1, p1i = st_next()
                    for k in range(2):
                        mm(p1[:, 0:n], wuq_t[:, k, h * 96:h * 96 + 128], cqn[:, k, s0:s0 + n], k == 0, k == 1,
                           ["wuq"] + ckeys, [p1i])
                    p2, p2i = st_next()
                    for k in range(2):
                        mm(p2[:, 0:n], wuqrot[:, k, h, :], cqn[:, k, s0:s0 + n], k == 0, k == 1,
                           ["wuqrot"] + ckeys, [p2i])
                    copy_op("act", qk[j][0:64, s0:s0 + n], p1[0:64, 0:n], [p1i],
                            [("qk", j, b) for b in blocks_of(s0, n)])
                    rope_combine(p1, p1i, p2, p2i, qk[j], s0, n,
                                 [("qk", j, "aug")], [])
                    p3, p3i = st_next()
                    mm(p3[:, 0:n], wukv_t[:, h * 128:h * 128 + 128], ckvn[:, s0:s0 + n], True, True,
                       ["wukv", ("S2", "ckvn", s0), "S2"], [p3i])
                    copy_op(evac_eng(), qk[2 + j][0:64, s0:s0 + n], p3[0:64, 0:n], [p3i],
                            [("qk", 2 + j, b) for b in blocks_of(s0, n)])
                S.dma("sp", (lambda j: lambda e: e.dma_start(out=qk[2 + j][64:96, :], in_=krope[64:96, :]))(j),
                      reads=[("S2", "krope", s0) for (s0, n) in TCH] + ["S2"], writes=[("qk", 2 + j, "aug")])
            vview = wukv_t[:].rearrange("p (h t d) -> p h t d", h=8, t=2)[:, 2 * c:2 * c + 2, 1, :]
            v_proj_tm(lambda k: vview, "wukv", lambda k, b: ckvn[:, b * 128:(b + 1) * 128],
                      lambda b: [("S2", "ckvn", s0) for (s0, n) in TCH if s0 <= b * 128 < s0 + n] + ["S2"], 1)
            for j in range(2):
                attn_dense(j, 128, 96 ** -0.5, lambda kb: (None, []))
            finish_pair(c)
            if c == 1:
                load_wbr(l, 2)
            if c == 2:
                load_wout_hi(l)
                load_wout_lo(l)
        branch_final(l, 2, first)

    def rope_combine(pa, pakey, pb, pbkey, dst_tile, s0, n, wkeys, rkeys):
        r1 = nxt("tmpf", 2)
        t1 = tmpf[r1]
        S.op("dve", lambda e: e.tensor_tensor(out=t1[64:96, 0:n], in0=pa[64:96, 0:n], in1=ropeT[64:96, 0, s0:s0 + n],
                                              op=ALU.mult), reads=[pakey, "ropeT"], writes=[("tmpf", r1)])
        r2 = nxt("tmpf", 2)
        t2 = tmpf[r2]
        S.op("dve", lambda e: e.tensor_tensor(out=t2[64:96, 0:n], in0=pb[64:96, 0:n], in1=ropeT[64:96, 1, s0:s0 + n],
                                              op=ALU.mult), reads=[pbkey, "ropeT"], writes=[("tmpf", r2)])
        S.op("pool", lambda e: e.tensor_tensor(out=dst_tile[64:96, s0:s0 + n], in0=t1[64:96, 0:n], in1=t2[64:96, 0:n],
                                               op=ALU.add), reads=[("tmpf", r1), ("tmpf", r2)] + rkeys, writes=wkeys)

    for sidx in range(NSEQ):
        first_norm(sidx)
        for l in range(NLAYER):
            mixer_A(l, True)
            mixer_B(l, False)
            mixer_C(l, False)
            wout_phase(l, sidx)

    with nc.allow_non_contiguous_dma(reason='small strided parameter / permuted weight loads'):
        S.emit()
    st.close()
    return nc, S


_CACHE = {}


def kernel(x, meta_tokens, norm_g, w_in, b_f, sinks, q_norm_g, kv_norm_g, w_uq, w_ukv, w_br, w_out, final_norm_g):
    x = np.asarray(x, np.float32)
    B, SEQ, _ = x.shape
    NCORE = 8
    NSEQ = B // NCORE
    NB = SEQ // 128 + 1
    L = NB * 128
    key = (NSEQ, NB)
    if key not in _CACHE:
        _CACHE[key] = build(NSEQ, DEPTH, NB)
    nc, _ = _CACHE[key]
    consts = make_consts(NB)
    xp = np.zeros((B, L, D), np.float32)
    xp[:, 128 - N_META:128, :] = np.asarray(meta_tokens, np.float32)[None]
    xp[:, 128:, :] = x
    shared = {
        "norm_g": np.asarray(norm_g, np.float32), "w_in": np.asarray(w_in, np.float32),
        "b_f": np.asarray(b_f, np.float32), "sinks": np.asarray(sinks, np.float32),
        "q_norm_g": np.asarray(q_norm_g, np.float32), "kv_norm_g": np.asarray(kv_norm_g, np.float32),
        "w_uq": np.asarray(w_uq, np.float32), "w_ukv": np.asarray(w_ukv, np.float32),
        "w_br": np.asarray(w_br, np.float32), "w_out": np.asarray(w_out, np.float32),
        "final_norm_g": np.asarray(final_norm_g, np.float32).reshape(1, D),
    }
    shared.update(consts)
    in_maps = []
    for c in range(NCORE):
        m = dict(shared)
        m["xin"] = np.ascontiguousarray(xp[c * NSEQ:(c + 1) * NSEQ])
        in_maps.append(m)
    res = run_bass_kernel_spmd(nc, in_maps, core_ids=list(range(NCORE)))
    outs = [np.asarray(r["out"], np.float32) for r in res.results]
    return np.concatenate(outs, axis=0)
```

```python
import contextlib
import math
import numpy as np
import concourse.bass as bass
import concourse.mybir as mybir
from concourse.bass_utils import run_bass_kernel_spmd

F32 = mybir.dt.float32
BF16 = mybir.dt.bfloat16
AF = mybir.ActivationFunctionType
ALU = mybir.AluOpType

D = 1024
DEPTH = 4
N_META = 16
HD = 64
D_IN = 7336
AQ, AK, AV, BQ, BK, BV, BFO, CQ, CKV, CKR, ZO, GO = 0, 512, 640, 768, 1280, 1792, 2304, 2312, 2568, 2696, 2728, 4264
EPS = 1e-6
MASKV = -240000.0
ENGINES = ("pe", "act", "dve", "pool", "sp")


class Op:
    __slots__ = ("eng", "fn", "is_dma", "deps", "needs_inc", "tok", "ring")

    def __init__(self, eng, fn, is_dma):
        self.eng = eng
        self.fn = fn
        self.is_dma = is_dma
        self.deps = []
        self.needs_inc = False
        self.tok = None
        self.ring = None


class Sched:
    def __init__(self, nc):
        self.nc = nc
        self.ops = {e: [] for e in ENGINES}
        self.last_w = {}
        self.readers = {}
        self.dma_ring = {"sp": 24, "pool": 12, "act": 4}

    def _add(self, eng, fn, reads, writes, is_dma):
        op = Op(eng, fn, is_dma)
        deps = op.deps
        lw = self.last_w
        rd = self.readers
        for r in reads:
            w = lw.get(r)
            if w is not None:
                deps.append((w, 0))
        for k in writes:
            w = lw.get(k)
            if w is not None:
                deps.append((w, 1))
            lst = rd.get(k)
            if lst:
                for x in lst:
                    deps.append((x, 2))
        for r in reads:
            lst = rd.get(r)
            if lst is None:
                rd[r] = [op]
            else:
                lst.append(op)
        for k in writes:
            lw[k] = op
            rd[k] = []
        self.ops[eng].append(op)
        return op

    def op(self, eng, fn, reads=(), writes=()):
        return self._add(eng, fn, reads, writes, False)

    def dma(self, eng, fn, reads=(), writes=()):
        return self._add(eng, fn, reads, writes, True)

    def emit(self):
        nc = self.nc
        for e in ENGINES:
            for op in self.ops[e]:
                kept = []
                seen = set()
                for (p, kind) in op.deps:
                    if p is op or id(p) in seen:
                        continue
                    if (not p.is_dma) and (not op.is_dma) and p.eng == op.eng:
                        if p.eng == "pe":
                            continue
                    seen.add(id(p))
                    p.needs_inc = True
                    kept.append(p)
                op.deps = kept
        stack = contextlib.ExitStack()
        eng_sem = {e: stack.enter_context(nc.semaphore("c_" + e)) for e in ENGINES}
        ring_sems = {e: [stack.enter_context(nc.semaphore("d_%s%d" % (e, i))) for i in range(n)]
                     for e, n in self.dma_ring.items()}
        final_vals = {}
        for e in ENGINES:
            cnt = 0
            dcnt = 0
            for op in self.ops[e]:
                if op.is_dma:
                    n = self.dma_ring[e]
                    slot, k = dcnt % n, dcnt // n
                    op.ring = (ring_sems[e][slot], 16 * k)
                    op.tok = (ring_sems[e][slot], 16 * (k + 1))
                    final_vals[id(op.tok[0])] = op.tok
                    dcnt += 1
                elif op.needs_inc:
                    cnt += 1
                    op.tok = (eng_sem[e], cnt)
        self.stats = {e: len(self.ops[e]) for e in ENGINES}
        ops = self.ops

        def body(e):
            def run(engh):
                waited = {}
                for op in ops[e]:
                    need = {}
                    if op.is_dma and op.ring[1] > 0:
                        need[id(op.ring[0])] = op.ring
                    for p in op.deps:
                        s, v = p.tok
                        cur = need.get(id(s))
                        if cur is None or cur[1] < v:
                            need[id(s)] = (s, v)
                    for sid, (s, v) in need.items():
                        if waited.get(sid, 0) >= v:
                            continue
                        engh.wait_ge(s, v)
                        waited[sid] = v
                    ins = op.fn(engh)
                    if op.is_dma:
                        ins.then_inc(op.tok[0], 16)
                    elif op.needs_inc:
                        ins.then_inc(op.tok[0], 1)
                if e == "sp":
                    for (s, v) in final_vals.values():
                        if waited.get(id(s), 0) < v:
                            engh.wait_ge(s, v)
                            waited[id(s)] = v
            return run

        with nc.Block() as block:
            block.tensor(body("pe"))
            block.scalar(body("act"))
            block.vector(body("dve"))
            block.gpsimd(body("pool"))
            block.sync(body("sp"))
        stack.close()


def make_consts(NB):
    L = NB * 128
    pad = 128 - N_META
    ident = np.eye(128, dtype=np.float32)
    kk = np.arange(128)[:, None]
    qq = np.arange(128)[None, :]
    cmask = np.where(kk <= qq, 0.0, MASKV).astype(np.float32)
    slopes = 2.0 ** (-8.0 * (np.arange(8) + 1.0) / 8)
    abias = np.zeros((128, 8, 2, 128), np.float32)
    for h in range(8):
        dprev = 128 + qq - kk
        dcur = qq - kk
        abias[:, h, 0, :] = np.where(dprev < 128, -8.0 * slopes[h] * dprev, MASKV)
        abias[:, h, 1, :] = np.where(dcur >= 0, -8.0 * slopes[h] * dcur, MASKV)
    pos = (np.arange(L) - pad).astype(np.float32)
    inv = (10000.0 ** (-np.arange(16, dtype=np.float32) / 16)).astype(np.float32)
    ang = pos[None, :] * inv[:, None]
    cos = np.cos(ang).astype(np.float32)
    sin = np.sin(ang).astype(np.float32)
    rope = np.zeros((128, 2, L), np.float32)
    rope[64:80, 0] = cos
    rope[80:96, 0] = cos
    rope[64:80, 1] = -sin
    rope[80:96, 1] = sin
    return {"c_ident": ident, "c_cmask": cmask, "c_abias": abias, "c_rope": rope}


def build(NSEQ, NLAYER, NB):
    L = NB * 128
    TCH = [(s, min(512, L - s)) for s in range(0, L, 512)]
    nc = bass.Bass("TRN2", target_bir_lowering=False)

    def din(name, shape):
        return nc.dram_tensor(name, list(shape), F32, kind="ExternalInput").ap()

    xin = din("xin", [NSEQ, L, D])
    norm_g = din("norm_g", [DEPTH, D])
    w_in = din("w_in", [DEPTH, D, D_IN])
    b_f = din("b_f", [DEPTH, 8])
    sinks = din("sinks", [DEPTH, 8])
    q_norm_g = din("q_norm_g", [DEPTH, 256])
    kv_norm_g = din("kv_norm_g", [DEPTH, 128])
    w_uq = din("w_uq", [DEPTH, 256, 768])
    w_ukv = din("w_ukv", [DEPTH, 128, 1024])
    w_br = din("w_br", [DEPTH, 3, 512, D])
    w_out = din("w_out", [DEPTH, D, D])
    final_g = din("final_norm_g", [1, D])
    c_ident = din("c_ident", [128, 128])
    c_cmask = din("c_cmask", [128, 128])
    c_abias = din("c_abias", [128, 8, 2, 128])
    c_rope = din("c_rope", [128, 2, L])
    out = nc.dram_tensor("out", [NSEQ, L - 128, D], F32, kind="ExternalOutput").ap()
    xs = nc.dram_tensor("xs_scr", [L, D], F32, kind="Internal").ap()
    ys = nc.dram_tensor("ys_scr", [8, 128, L], F32, kind="Internal").ap()

    st = contextlib.ExitStack()

    def sb(name, shape, dt):
        return st.enter_context(nc.sbuf_tensor(name, list(shape), dt))

    def pst(name, shape, dt):
        return st.enter_context(nc.psum_tensor(name, list(shape), dt))

    hT = sb("hT", [128, 8, L], BF16)
    uT = sb("uT", [128, 4, L], BF16)
    qk = [sb("qk%d" % i, [128, L], BF16) for i in range(4)]
    vp = sb("vp", [128, NB, 2, 65], BF16)
    opair = sb("opair", [128, NB, 128], BF16)
    NPT = 4
    PT = [sb("pt%d" % i, [128, 512], BF16) for i in range(NPT)]
    NWR = 6
    wring = [sb("wr%d" % i, [128, 8, 128], BF16) for i in range(NWR)]
    wbig = sb("wbig", [128, 8, 1024], BF16)
    wuq_t = sb("wuq", [128, 2, 800], BF16)
    wuqrot = sb("wuqrot", [128, 2, 8, 128], BF16)
    wukv_t = sb("wukv", [128, 1024], BF16)
    wkr = sb("wkr", [128, 8, 2, 128], BF16)
    wbf = sb("wbf", [128, 8, 8], BF16)
    S1 = sb("S1", [128, L], F32)
    S2 = sb("S2", [128, L], F32)
    S3 = sb("S3", [128, max(L, 2048)], F32)
    S1b = S1.bitcast(BF16)
    S2b = S2.bitcast(BF16)
    S3b = S3.bitcast(BF16)
    rstdb = sb("rstdb", [128, 512], F32)
    ropeT = sb("ropeT", [128, 2, L], BF16)
    abias = sb("abias", [128, 8, 2, 128], BF16)
    ident = sb("ident", [128, 128], BF16)
    identf = sb("identf", [128, 128], F32)
    onesf = sb("onesf", [128, 128], F32)
    cmask = sb("cmask", [128, 128], BF16)
    gbuf = [sb("gb%d" % i, [128, D], F32) for i in range(2)]
    negbf = sb("negbf", [8, DEPTH], F32)
    esink = sb("esink", [128, DEPTH * 8], F32)
    gq = sb("gq", [128, DEPTH, 2], F32)
    gkv = sb("gkv", [128, DEPTH], F32)
    one_c = sb("one_c", [128, 1], F32)
    NXT = 3
    xt = [sb("xt%d" % i, [128, D], F32) for i in range(NXT)]
    hb = [sb("hb%d" % i, [128, D], BF16) for i in range(3)]
    st_ss = [sb("ss%d" % i, [128, 1], F32) for i in range(3)]
    st_ms = [sb("ms%d" % i, [128, 1], F32) for i in range(3)]
    st_rs = [sb("rs%d" % i, [128, 1], F32) for i in range(3)]
    tnh = [sb("tnh%d" % i, [128, 512], BF16) for i in range(2)]
    tmpf = [sb("tmpf%d" % i, [128, 512], F32) for i in range(2)]
    yold = [sb("yold%d" % i, [128, 512], F32) for i in range(2)]
    GT = sb("GT", [128, NB * 8], F32)
    den = [sb("den%d" % i, [128, 1], F32) for i in range(4)]
    den8 = [sb("den8_%d" % i, [128, 8], F32) for i in range(4)]
    eps_c = sb("eps_c", [128, 1], F32)
    osb = [sb("osb%d" % i, [128, 455], F32) for i in range(3)]

    pj = [pst("pj%d" % i, [128, 512], F32) for i in range(2)]
    stp = [pst("st%d" % i, [128, 512], F32) for i in range(2)]
    ob = [pst("ob%d" % i, [128, 512], F32) for i in range(3)]
    trb = pst("trb", [128, 1024], BF16)
    trf = trb.bitcast(F32)

    S = Sched(nc)
    cnt = {"pj": 0, "st": 0, "pt": 0, "wr": 0, "tnh": 0, "tmpf": 0, "yold": 0, "yblk": 0, "xt": 0, "den": 0,
           "rtm": 0, "evac": 0, "trs": 0, "st4": 0}

    def nxt(name, n):
        v = cnt[name] % n
        cnt[name] += 1
        return v

    def evac_eng():
        cnt["evac"] += 1
        return "act" if cnt["evac"] % 2 == 0 else "dve"

    def copy_op(eng, out_ap, in_ap, reads, writes):
        if eng == "act":
            S.op("act", lambda e: e.copy(out=out_ap, in_=in_ap), reads, writes)
        else:
            S.op("dve", lambda e: e.tensor_copy(out=out_ap, in_=in_ap), reads, writes)

    def mm(out_ap, lhsT, rhs, start, stop, reads, writes):
        S.op("pe", lambda e: e.matmul(out_ap, lhsT=lhsT, rhs=rhs, start=start, stop=stop, skip_group_check=True),
             reads, writes)

    TRBK = ["trb"]

    def blocks_of(s0, n):
        return range(s0 // 128, (s0 + n) // 128)

    S.dma("pool", lambda e: e.dma_start(out=ident[:], in_=c_ident), writes=["ident"])
    S.dma("pool", lambda e: e.dma_start(out=cmask[:], in_=c_cmask), writes=["cmask"])
    S.dma("pool", lambda e: e.dma_start(out=abias[:], in_=c_abias), writes=["abias"])
    S.dma("pool", lambda e: e.dma_start(out=ropeT[:], in_=c_rope), writes=["ropeT"])
    S.dma("sp", lambda e: e.dma_start(out=identf[:], in_=c_ident), writes=["identf"])
    S.op("pool", lambda e: e.memset(onesf[:], 1.0), writes=["onesf"])
    S.op("pool", lambda e: e.memset(one_c[:], 1.0), writes=["one_c"])
    S.op("pool", lambda e: e.memset(eps_c[:], EPS), writes=["eps_c"])
    S.op("pool", lambda e: e.memset(vp[:, :, :, 64:65], 1.0), writes=["vp_ones"])
    S.op("pool", lambda e: e.memset(vp[0:128 - N_META, 0:1, :, 64:65], 0.0), reads=[], writes=["vp_ones"])
    for _i in range(4):
        S.op("pool", (lambda _i: lambda e: e.memset(qk[_i][:], 0.0))(_i),
             writes=[("qk", _i, b) for b in range(NB)] + [("qk", _i, "aug")])
    S.op("pool", lambda e: e.memset(wuqrot[:], 0.0), writes=["wuqrot"])
    S.op("pool", lambda e: e.memset(wuq_t[:], 0.0), writes=["wuq"])
    S.op("pool", lambda e: e.memset(wkr[:], 0.0), writes=["wkr"])
    with nc.allow_non_contiguous_dma(reason="tiny param loads"):
        S.dma("sp", lambda e: e.dma_start(out=negbf[:], in_=b_f.rearrange("l h -> h l")), writes=["negbf"])
        S.dma("sp", lambda e: e.dma_start(out=gq[:], in_=q_norm_g.rearrange("l (c p) -> p l c", p=128)),
              writes=["gq"])
        S.dma("sp", lambda e: e.dma_start(out=gkv[:], in_=kv_norm_g.rearrange("l p -> p l")), writes=["gkv"])
    S.dma("sp", lambda e: e.dma_start(out=esink[:], in_=sinks.rearrange("l h -> (l h)").partition_broadcast(128)),
          writes=["esink"])
    S.op("dve", lambda e: e.tensor_scalar(out=negbf[:], in0=negbf[:], scalar1=-1.0, scalar2=None, op0=ALU.mult),
         reads=["negbf"], writes=["negbf"])
    S.op("act", lambda e: e.activation(out=esink[:], in_=esink[:], func=AF.Exp), reads=["esink"], writes=["esink"])

    def layer_specs(l):
        sp = []
        for c in range(4):
            sp.append((l, [(ZO + 0 * 512 + c * 128, 128)]))
        for c in range(4):
            g = c // 2
            sp.append((l, [(AQ + c * 128, 128)]))
            sp.append((l, [(AK + g * 64, 64), (AK + g * 64, 64)]))
            sp.append((l, [(AV + g * 64, 64), (AV + g * 64, 64)]))
        for m in range(8):
            sp.append((l, [(GO + 0 * 1024 + m * 128, 128)]))
        for c in range(4):
            sp.append((l, [(ZO + 1 * 512 + c * 128, 128)]))
        for c in range(4):
            sp.append((l, [(BQ + c * 128, 128)]))
            sp.append((l, [(BK + c * 128, 128)]))
            sp.append((l, [(BV + c * 128, 128)]))
        for m in range(8):
            sp.append((l, [(GO + 1 * 1024 + m * 128, 128)]))
        for c in range(4):
            sp.append((l, [(ZO + 2 * 512 + c * 128, 128)]))
        for i in range(2):
            sp.append((l, [(CQ + i * 128, 128)]))
        sp.append((l, [(CKV, 128)]))
        for m in range(8):
            sp.append((l, [(GO + 2 * 1024 + m * 128, 128)]))
        return sp

    WSPECS = []
    for _s in range(NSEQ):
        for _l in range(NLAYER):
            WSPECS.extend(layer_specs(_l))
    wstate = {"issued": 0, "used": 0}
    WPF = NWR - 3

    def _issue_chunk(idx):
        l, col_runs = WSPECS[idx]
        slot = idx % NWR
        t = wring[slot]
        key = ("wr", slot)
        o = 0
        for (c0, n) in col_runs:
            src = w_in[l, :, c0:c0 + n].rearrange("(kc p) n -> p kc n", p=128)
            dst = t[:, :, o:o + n]
            S.dma("pool", (lambda dst, src: lambda e: e.dma_start(out=dst, in_=src))(dst, src), writes=[key])
            o += n

    def wload_chunk(l, col_runs):
        idx = wstate["used"]
        assert WSPECS[idx] == (l, col_runs), (idx, WSPECS[idx], l, col_runs)
        while wstate["issued"] < min(len(WSPECS), idx + 1 + WPF):
            _issue_chunk(wstate["issued"])
            wstate["issued"] += 1
        wstate["used"] += 1
        slot = idx % NWR
        return wring[slot], ("wr", slot)

    def proj_fm(wt, wkey, M, evac_fn, extra_reads=()):
        for (s0, n) in TCH:
            p, pkey = st_next()
            hkeys = [("hT", b) for b in blocks_of(s0, n)]
            for k in range(8):
                mm(p[0:M, 0:n], wt[:, k, 0:M], hT[:, k, s0:s0 + n], k == 0, k == 7,
                   [wkey] + hkeys + list(extra_reads), [pkey])
            evac_fn(p, pkey, s0, n)

    def norm_stats(xtile, xkey, b, gtile, gkey, last, sidx):
        i = b % 3
        hbt = hb[i]
        S.op("act", lambda e: e.activation(out=hbt[:], in_=xtile[:], func=AF.Square, accum_out=st_ss[i][:]),
             reads=[xkey], writes=[("hb", i), ("ss", i)])
        S.op("act", lambda e: e.activation(out=st_ms[i][:], in_=st_ss[i][:], func=AF.Ln, bias=eps_c[:], scale=1.0 / D),
             reads=[("ss", i), "eps_c"], writes=[("ms", i)])
        S.op("act", lambda e: e.activation(out=st_rs[i][:], in_=st_ms[i][:], func=AF.Exp, scale=-0.5),
             reads=[("ms", i)], writes=[("rs", i)])
        if last:
            if b == 0:
                return
            S.op("dve", lambda e: e.scalar_tensor_tensor(out=xtile[:], in0=xtile[:], scalar=st_rs[i][:], in1=gtile[:],
                                                         op0=ALU.mult, op1=ALU.mult),
                 reads=[xkey, ("rs", i), gkey], writes=[xkey])
            S.dma("sp", lambda e: e.dma_start(out=out[sidx, (b - 1) * 128:b * 128, :], in_=xtile[:]), reads=[xkey])
            return
        S.op("dve", lambda e: e.scalar_tensor_tensor(out=hbt[:], in0=xtile[:], scalar=st_rs[i][:], in1=gtile[:],
                                                     op0=ALU.mult, op1=ALU.mult),
             reads=[xkey, ("rs", i), gkey], writes=[("hb", i)])

    def norm_transposes(b):
        i = b % 3
        hbt = hb[i]
        for c in range(8):
            S.op("pe", (lambda c: lambda e: e.transpose(out=trb[:, c * 128:(c + 1) * 128],
                                                        in_=hbt[:, c * 128:(c + 1) * 128], identity=ident[:]))(c),
                 reads=[("hb", i), "ident"], writes=TRBK)
        copy_op(evac_eng(), hT[:, :, b * 128:(b + 1) * 128], trb[:].rearrange("p (c t) -> p c t", c=8),
                TRBK, [("hT", b)])

    def norm_block(xtile, xkey, b, gtile, gkey, last, sidx):
        norm_stats(xtile, xkey, b, gtile, gkey, last, sidx)
        if not last:
            norm_transposes(b)

    def load_gb(which, src_row):
        S.dma("sp", lambda e: e.dma_start(out=gbuf[which][:], in_=src_row.partition_broadcast(128)),
              writes=[("gb", which)])

    SKEW = 3

    def st_next():
        i = nxt("st4", 4)
        return [(stp[0], ("st", 0)), (stp[1], ("st", 1)), (pj[0], ("pj", 0)), (pj[1], ("pj", 1))][i]

    def run_pipeline(stages, qk_stage, pv_stage):
        pend = []
        for stg in stages:
            pend.append(qk_stage(stg))
            if len(pend) > SKEW:
                pv_stage(*pend.pop(0))
        while pend:
            pv_stage(*pend.pop(0))

    def o_ap(qb):
        return ob[qb // 7][:, (qb % 7) * 65:(qb % 7) * 65 + 65], ("ob", qb // 7)

    def normalize(oap, okey, qb, j, sink_ap=None):
        di = nxt("den", 4)
        d = den[di]
        if sink_ap is None:
            S.op("dve", lambda e: e.tensor_scalar(out=d[:], in0=oap[:, 64:65], scalar1=1e-30, scalar2=None,
                                                  op0=ALU.max), reads=[okey], writes=[("den", di)])
        else:
            S.op("dve", lambda e: e.tensor_scalar(out=d[:], in0=oap[:, 64:65], scalar1=sink_ap, scalar2=None,
                                                  op0=ALU.add), reads=[okey, "esink"], writes=[("den", di)])
        S.op("dve", lambda e: e.reciprocal(out=d[:], in_=d[:]), reads=[("den", di)], writes=[("den", di)])
        S.op("dve", lambda e: e.tensor_scalar(out=opair[:, qb, j * 64:(j + 1) * 64], in0=oap[:, 0:64],
                                              scalar1=d[:], scalar2=None, op0=ALU.mult),
             reads=[okey, ("den", di)], writes=[("opair", qb, j)])

    def attn_dense(j, Kd, scale, bias_fn):
        qt, kt = qk[j], qk[2 + j]
        qkeys = lambda s0, n: [("qk", j, b) for b in blocks_of(s0, n)] + [("qk", j, "aug")]
        stages = []
        for kb in range(NB):
            q0 = kb * 128
            for ci, s0 in enumerate(range(q0, L, 512)):
                stages.append((kb, ci, s0, min(512, L - s0)))

        def qk_stage(stg):
            kb, ci, s0, n = stg
            q0 = kb * 128
            sp_, skey = st_next()
            mm(sp_[:, 0:n], kt[0:Kd, q0:q0 + 128], qt[0:Kd, s0:s0 + n], True, ci != 0,
               [("qk", 2 + j, kb), ("qk", 2 + j, "aug")] + qkeys(s0, n), [skey])
            if ci == 0:
                mm(sp_[:, 0:128], ident[:], cmask[:], False, True, ["ident", "cmask"], [skey])
            return stg, sp_, skey

        def pv_stage(stg, sp_, skey):
            kb, ci, s0, n = stg
            pi = nxt("pt", NPT)
            pt = PT[pi]
            pkey = ("pt", pi)
            bias_ap, bias_keys = bias_fn(kb)
            if bias_ap is None:
                S.op("act", (lambda pt, sp_, n: lambda e: e.activation(out=pt[:, 0:n], in_=sp_[:, 0:n],
                                                                       func=AF.Exp, scale=scale))(pt, sp_, n),
                     reads=[skey], writes=[pkey])
            else:
                S.op("act", (lambda pt, sp_, n, bias_ap: lambda e: e.activation(
                    out=pt[:, 0:n], in_=sp_[:, 0:n], func=AF.Exp, bias=bias_ap, scale=scale))(pt, sp_, n, bias_ap),
                    reads=[skey] + bias_keys, writes=[pkey])
            for qi, qb in enumerate(blocks_of(s0, n)):
                oap, okey = o_ap(qb)
                mm(oap, pt[:, qi * 128:(qi + 1) * 128], vp[:, kb, j, :], kb == 0 and qb % 7 == 0, kb == qb,
                   [pkey, ("vp", kb), "vp_ones"], [okey])

        run_pipeline(stages, qk_stage, pv_stage)
        dense_finish(j)

    def dense_finish(j):
        for bank in range((NB + 6) // 7):
            nq = min(7, NB - bank * 7)
            ncol = nq * 65
            copy_op("dve", osb[bank][:, 0:ncol], ob[bank][:, 0:ncol], [("ob", bank)], [("osb", bank)])
            ov = osb[bank][:, 0:ncol].rearrange("p (q d) -> p q d", d=65)
            di = nxt("den", 4)
            d = den8[di]
            S.op("dve", (lambda d, ov, nq: lambda e: e.tensor_scalar(out=d[:, 0:nq], in0=ov[:, :, 64], scalar1=1e-30,
                                                                    scalar2=None, op0=ALU.max))(d, ov, nq),
                 reads=[("osb", bank)], writes=[("den8", di)])
            S.op("dve", (lambda d, nq: lambda e: e.reciprocal(out=d[:, 0:nq], in_=d[:, 0:nq]))(d, nq),
                 reads=[("den8", di)], writes=[("den8", di)])
            q0 = bank * 7
            S.op("dve", (lambda d, ov, nq, q0: lambda e: e.tensor_tensor(
                out=opair[:, q0:q0 + nq, j * 64:(j + 1) * 64], in0=ov[:, :, 0:64],
                in1=d[:, 0:nq].unsqueeze(2).broadcast_to([128, nq, 64]), op=ALU.mult))(d, ov, nq, q0),
                reads=[("osb", bank), ("den8", di)], writes=[("opair", q0 + q, j) for q in range(nq)])

    def attn_swa(j, h, l):
        qt, kt = qk[j], qk[2 + j]
        sink_ap = esink[:, l * 8 + h:l * 8 + h + 1]

        def qk_stage(qb):
            sp_, skey = st_next()
            parts = []
            if qb >= 1:
                parts.append((qb - 1, 0))
            parts.append((qb, 1))
            for pi_, (kb, which) in enumerate(parts):
                col = pi_ * 128
                mm(sp_[:, col:col + 128], kt[:, kb * 128:(kb + 1) * 128], qt[:, qb * 128:(qb + 1) * 128],
                   True, False, [("qk", 2 + j, kb), ("qk", 2 + j, "aug"), ("qk", j, qb), ("qk", j, "aug")], [skey])
                mm(sp_[:, col:col + 128], ident[:], abias[:, h, which, :], False, True, ["ident", "abias"], [skey])
            return qb, parts, sp_, skey

        def pv_stage(qb, parts, sp_, skey):
            n = 128 * len(parts)
            pi = nxt("pt", NPT)
            pt = PT[pi]
            pkey = ("pt", pi)
            S.op("act", (lambda pt, sp_, n: lambda e: e.activation(out=pt[:, 0:n], in_=sp_[:, 0:n], func=AF.Exp,
                                                                   scale=0.125))(pt, sp_, n),
                 reads=[skey], writes=[pkey])
            oap, okey = o_ap(qb)
            for pi_, (kb, which) in enumerate(parts):
                mm(oap, pt[:, pi_ * 128:(pi_ + 1) * 128], vp[:, kb, j, :], pi_ == 0 and qb % 7 == 0,
                   pi_ == len(parts) - 1, [pkey, ("vp", kb), "vp_ones"], [okey])
            if qb % 7 == 6 or qb == NB - 1:
                bank = qb // 7
                q0 = bank * 7
                nq = qb - q0 + 1
                ov = ob[bank][:, 0:nq * 65].rearrange("p (q d) -> p q d", d=65)
                di = nxt("den", 4)
                d = den8[di]
                S.op("dve", (lambda d, ov, nq: lambda e: e.tensor_scalar(out=d[:, 0:nq], in0=ov[:, :, 64], scalar1=sink_ap,
                                                                        scalar2=None, op0=ALU.add))(d, ov, nq),
                     reads=[("ob", bank), "esink"], writes=[("den8", di)])
                S.op("dve", (lambda d, nq: lambda e: e.reciprocal(out=d[:, 0:nq], in_=d[:, 0:nq]))(d, nq),
                     reads=[("den8", di)], writes=[("den8", di)])
                S.op("dve", (lambda d, ov, nq, q0: lambda e: e.tensor_tensor(
                    out=opair[:, q0:q0 + nq, j * 64:(j + 1) * 64], in0=ov[:, :, 0:64],
                    in1=d[:, 0:nq].unsqueeze(2).broadcast_to([128, nq, 64]), op=ALU.mult))(d, ov, nq, q0),
                    reads=[("ob", bank), ("den8", di)], writes=[("opair", q0 + q, j) for q in range(nq)])

        run_pipeline(list(range(NB)), qk_stage, pv_stage)

    def finish_pair(c):
        banks = [(trb, "trb"), (pj[0].bitcast(BF16), ("pj", 0)), (pj[1].bitcast(BF16), ("pj", 1))]
        for g0 in range(0, NB, 8):
            gn = min(8, NB - g0)
            bi = nxt("trs", 3)
            bt, bkey = banks[bi]
            for qi in range(gn):
                qb = g0 + qi
                S.op("pe", (lambda qb, qi, bt: lambda e: e.transpose(out=bt[:, qi * 128:(qi + 1) * 128],
                                                                   in_=opair[:, qb, :], identity=ident[:]))(qb, qi, bt),
                     reads=[("opair", qb, 0), ("opair", qb, 1), "ident"], writes=[bkey])
            S.op("dve", (lambda g0, gn, bt: lambda e: e.tensor_tensor(
                out=uT[:, c, g0 * 128:(g0 + gn) * 128], in0=bt[:, 0:gn * 128],
                in1=uT[:, c, g0 * 128:(g0 + gn) * 128], op=ALU.mult))(g0, gn, bt),
                reads=[bkey] + [("uT", c, g0 + q) for q in range(gn)], writes=[("uT", c, g0 + q) for q in range(gn)])

    def z_proj(l, i):
        for c in range(4):
            wt, wkey = wload_chunk(l, [(ZO + i * 512 + c * 128, 128)])

            def ev(p, pkey, s0, n, c=c):
                ti = nxt("tnh", 2)
                t = tnh[ti]
                S.op("act", lambda e: e.activation(out=t[:, 0:n], in_=p[:, 0:n], func=AF.Tanh, scale=0.5),
                     reads=[pkey], writes=[("tnh", ti)])
                S.op("dve", lambda e: e.scalar_tensor_tensor(out=uT[:, c, s0:s0 + n], in0=t[:, 0:n], scalar=1.0,
                                                             in1=p[:, 0:n], op0=ALU.add, op1=ALU.mult),
                     reads=[("tnh", ti), pkey], writes=[("uT", c, b) for b in blocks_of(s0, n)])
            proj_fm(wt, wkey, 128, ev)

    def yT_ap(m, s0, n):
        if m < 4:
            return qk[m][:, s0:s0 + n]
        if m == 4:
            return S1b[:, s0:s0 + n]
        if m == 5:
            return S1b[:, L + s0:L + s0 + n]
        if m == 6:
            return S2b[:, s0:s0 + n]
        return S2b[:, L + s0:L + s0 + n]

    def yT_keys(m, s0, n):
        c0 = (s0 // 512) * 512
        if m < 4:
            return [("qk", m, b) for b in blocks_of(s0, n)] + [("qk", m, "aug")], []
        fine = {4: ("S1", "cqn0", c0), 5: ("S1", "cqn1", c0), 6: ("S2", "ckvn", c0), 7: ("S2", "krope", c0)}[m]
        return [fine], ["S1" if m < 6 else "S2"]

    def load_wbr(l, i):
        src = w_br[l, i].rearrange("(c p) n -> p c n", p=128)
        S.dma("pool", lambda e: e.dma_start(out=wbig[:, 0:4, :], in_=src), writes=[("wbig", 0)])

    def branch_final(l, i, first):
        its = [(m, s0, n) for m in range(8) for (s0, n) in TCH]
        yolds = {}

        def issue_yold(k):
            if first or k >= len(its):
                return
            m, s0, n = its[k]
            yi = nxt("yold", 2)
            yo = yold[yi]
            S.dma("sp", (lambda yo, m, s0, n: lambda e: e.dma_start(out=yo[:, 0:n], in_=ys[m, :, s0:s0 + n]))(yo, m, s0, n),
                  reads=[("ys", m, s0)], writes=[("yold", yi)])
            yolds[k] = (yo, yi)

        issue_yold(0)
        wt = wkey = None
        for k, (m, s0, n) in enumerate(its):
            if s0 == 0:
                wt, wkey = wload_chunk(l, [(GO + i * 1024 + m * 128, 128)])
            issue_yold(k + 1)
            p, pkey = st_next()
            hkeys = [("hT", b) for b in blocks_of(s0, n)]
            for kk in range(8):
                mm(p[:, 0:n], wt[:, kk, :], hT[:, kk, s0:s0 + n], kk == 0, kk == 7, [wkey] + hkeys, [pkey])
            ti = nxt("tnh", 2)
            t = tnh[ti]
            S.op("act", (lambda t, p, n: lambda e: e.activation(out=t[:, 0:n], in_=p[:, 0:n], func=AF.Tanh,
                                                                scale=0.5))(t, p, n),
                 reads=[pkey], writes=[("tnh", ti)])
            p2, pkey2 = st_next()
            for c in range(4):
                mm(p2[:, 0:n], wbig[:, c, m * 128:(m + 1) * 128], uT[:, c, s0:s0 + n], c == 0, c == 3,
                   [("wbig", 0)] + [("uT", c, b) for b in blocks_of(s0, n)], [pkey2])
            fi = nxt("tmpf", 2)
            tf = tmpf[fi]
            S.op("dve", (lambda tf, t, p2, n: lambda e: e.scalar_tensor_tensor(
                out=tf[:, 0:n], in0=t[:, 0:n], scalar=1.0, in1=p2[:, 0:n], op0=ALU.add, op1=ALU.mult))(tf, t, p2, n),
                reads=[("tnh", ti), pkey2], writes=[("tmpf", fi)])
            ykey = ("ys", m, s0)
            ydst = ys[m, :, s0:s0 + n]
            if i == 2 and not first:
                yo, yi = yolds.pop(k)
                wk_, rk_ = yT_keys(m, s0, n)
                yap = yT_ap(m, s0, n)
                S.op("dve", (lambda tf, yo, n, yap: lambda e: e.tensor_tensor(out=yap, in0=tf[:, 0:n],
                                                                              in1=yo[:, 0:n], op=ALU.add))(tf, yo, n, yap),
                     reads=[("tmpf", fi), ("yold", yi)] + rk_, writes=wk_)
                continue
            if not first:
                yo, yi = yolds.pop(k)
                S.op("dve", (lambda tf, yo, n: lambda e: e.tensor_tensor(out=tf[:, 0:n], in0=tf[:, 0:n],
                                                                         in1=yo[:, 0:n], op=ALU.add))(tf, yo, n),
                     reads=[("tmpf", fi), ("yold", yi)], writes=[("tmpf", fi)])
            S.dma("sp", (lambda tf, ydst, n: lambda e: e.dma_start(out=ydst, in_=tf[:, 0:n]))(tf, ydst, n),
                  reads=[("tmpf", fi)], writes=[ykey])

    def load_wout_hi(l):
        src = w_out[l, 512:1024, :].rearrange("(c p) n -> p c n", p=128)
        S.dma("pool", lambda e: e.dma_start(out=wbig[:, 4:8, :], in_=src), writes=[("wbig", 1)])

    wlo = S3b[:, 0:4096].rearrange("p (c n) -> p c n", c=4)

    def load_wout_lo(l):
        src = w_out[l, 0:512, :].rearrange("(c p) n -> p c n", p=128)
        S.dma("pool", lambda e: e.dma_start(out=wlo, in_=src), writes=["S3"])

    def wout_phase(l, sidx):
        last = (l == NLAYER - 1)
        gi = (l + 1) % 2
        load_gb(gi, final_g[0] if last else norm_g[l + 1])
        xsrc = xin[sidx] if l == 0 else xs
        pending = []
        xbuf = {}

        def issue_xload(b):
            if b >= NB:
                return
            xi = nxt("xt", NXT)
            x = xt[xi]
            xkey = ("xt", xi)
            S.dma("sp", (lambda x, b: lambda e: e.dma_start(out=x[:], in_=xsrc[b * 128:(b + 1) * 128, :]))(x, b),
                  reads=[("xs", b)] if l > 0 else [], writes=[xkey])
            xbuf[b] = (x, xkey)

        issue_xload(0)
        for b in range(NB):
            issue_xload(b + 1)
            x, xkey = xbuf.pop(b)
            for half in range(2):
                p, pkey = st_next()
                order = [4, 5, 6, 7, 0, 1, 2, 3]
                for oi, m in enumerate(order):
                    wk_, rk_ = yT_keys(m, b * 128, 128)
                    if m < 4:
                        rhs, wkeys = wlo[:, m, half * 512:(half + 1) * 512], ["S3"]
                    else:
                        rhs, wkeys = wbig[:, m, half * 512:(half + 1) * 512], [("wbig", 1)]
                    mm(p[:, :], yT_ap(m, b * 128, 128), rhs, oi == 0, oi == 7, wk_ + rk_ + wkeys, [pkey])
                S.op("dve", (lambda x, p, half: lambda e: e.scalar_tensor_tensor(
                    out=x[:, half * 512:(half + 1) * 512], in0=p[:, :], scalar=0.25,
                    in1=x[:, half * 512:(half + 1) * 512], op0=ALU.mult, op1=ALU.add))(x, p, half),
                    reads=[pkey, xkey], writes=[xkey])
            if not last:
                S.dma("sp", (lambda x, b: lambda e: e.dma_start(out=xs[b * 128:(b + 1) * 128, :], in_=x[:]))(x, b),
                      reads=[xkey], writes=[("xs", b)])
            norm_stats(x, xkey, b, gbuf[gi], ("gb", gi), last, sidx)
            pending.append(b)
            if len(pending) > 2 and not last:
                norm_transposes(pending.pop(0))
        while pending and not last:
            norm_transposes(pending.pop(0))

    def first_norm(sidx):
        load_gb(0, norm_g[0])
        for b in range(NB):
            xi = nxt("xt", NXT)
            x = xt[xi]
            xkey = ("xt", xi)
            S.dma("sp", (lambda x, b: lambda e: e.dma_start(out=x[:], in_=xin[sidx, b * 128:(b + 1) * 128, :]))(x, b),
                  writes=[xkey])
            norm_block(x, xkey, b, gbuf[0], ("gb", 0), False, sidx)

    S1K = ["S1"]
    S2K = ["S2"]
    S3K = ["S3"]

    def evac_pair(p, pkey, s0, n, t0, k0, t1, k1, rows=64):
        copy_op("act", t0[0:rows, s0:s0 + n], p[0:rows, 0:n], [pkey], [(k0[0], k0[1], b) for b in blocks_of(s0, n)])
        copy_op("dve", t1[0:rows, s0:s0 + n], p[64:64 + rows, 0:n], [pkey],
                [(k1[0], k1[1], b) for b in blocks_of(s0, n)])

    def v_proj_tm(wt, wkey, lhs_fn, lhs_keys_fn, nk):
        for b in range(NB):
            p, pkey = st_next()
            for k in range(nk):
                mm(p[:, 0:128], lhs_fn(k, b), wt(k), k == 0, k == nk - 1, [wkey] + lhs_keys_fn(b), [pkey])
            copy_op(evac_eng(), vp[:, b, :, 0:64], p[:, 0:128].rearrange("p (j d) -> p j d", j=2), [pkey],
                    [("vp", b)])

    def mixer_A(l, first=True):
        for _i in range(2):
            S.op("pool", (lambda _i: lambda e: e.memset(qk[_i][64:128, :], 0.0))(_i), writes=[("qk", _i, "aug")])
        z_proj(l, 0)
        for c in range(4):
            g = c // 2
            wq_, wqk = wload_chunk(l, [(AQ + c * 128, 128)])
            proj_fm(wq_, wqk, 128, lambda p, pkey, s0, n: evac_pair(p, pkey, s0, n, qk[0], ("qk", 0), qk[1], ("qk", 1)))
            wk_, wkk = wload_chunk(l, [(AK + g * 64, 64), (AK + g * 64, 64)])
            proj_fm(wk_, wkk, 128, lambda p, pkey, s0, n: evac_pair(p, pkey, s0, n, qk[2], ("qk", 2), qk[3], ("qk", 3)))
            wv_, wvk = wload_chunk(l, [(AV + g * 64, 64), (AV + g * 64, 64)])
            v_proj_tm(lambda k: wv_[:, k, :], wvk, lambda k, b: hT[:, k, b * 128:(b + 1) * 128],
                      lambda b: [("hT", b)], 8)
            for j in range(2):
                attn_swa(j, 2 * c + j, l)
            finish_pair(c)
            if c == 1:
                load_wbr(l, 0)
        branch_final(l, 0, first)

    def fox_gates(l):
        src = w_in[l, :, BFO:BFO + 8].rearrange("(kc p) n -> p kc n", p=128)
        with nc.allow_non_contiguous_dma(reason="8-col gate weights"):
            S.dma("pool", lambda e: e.dma_start(out=wbf[:], in_=src), writes=["wbf"])
        for (s0, n) in TCH:
            p, pkey = st_next()
            for k in range(8):
                mm(p[0:8, 0:n], wbf[:, k, :], hT[:, k, s0:s0 + n], k == 0, k == 7,
                   ["wbf"] + [("hT", b) for b in blocks_of(s0, n)], [pkey])
            S.op("act", (lambda p, s0, n: lambda e: e.activation(out=S1[0:8, s0:s0 + n], in_=p[0:8, 0:n], func=AF.Exp,
                                                                 bias=negbf[:, l:l + 1], scale=-1.0))(p, s0, n),
                 reads=[pkey, "negbf"], writes=S1K)
        S.op("act", lambda e: e.activation(out=S1[0:8, :], in_=S1[0:8, :], func=AF.Ln, bias=one_c[0:8, :], scale=1.0),
             reads=S1K + ["one_c"], writes=S1K)
        S.op("dve", lambda e: e.tensor_tensor_scan(out=S2[0:8, :], data0=S1[0:8, :], data1=S1[0:8, :], initial=0.0,
                                                   op0=ALU.add, op1=ALU.max), reads=S1K, writes=S2K)
        fq = S3b[0:8, 0:2 * L].rearrange("p (t l) -> p t l", t=2)
        S.op("dve", lambda e: e.tensor_scalar(out=fq[:, 0, :], in0=S2[0:8, :], scalar1=-8.0, scalar2=None,
                                              op0=ALU.mult), reads=S2K, writes=S3K)
        S.op("dve", lambda e: e.scalar_tensor_tensor(out=fq[:, 1, :], in0=S2[0:8, :], scalar=-8.0, in1=fq[:, 0, :],
                                                     op0=ALU.mult, op1=ALU.subtract), reads=S2K + S3K, writes=S3K)
        return fq

    def fox_gates_T():
        for b in range(NB):
            S.op("pe", (lambda b: lambda e: e.transpose(out=trf[:, b * 8:(b + 1) * 8], in_=S2[0:8, b * 128:(b + 1) * 128],
                                                        identity=identf[0:8, 0:8]))(b),
                 reads=S2K + ["identf"], writes=TRBK)
        S.op("dve", lambda e: e.tensor_copy(out=GT[:], in_=trf[:, 0:NB * 8]), reads=TRBK, writes=["GT"])

    def mixer_B(l, first=False):
        fq = fox_gates(l)
        for _i in range(4):
            S.op("pool", (lambda _i: lambda e: e.memset(qk[_i][64:128, :], 0.0))(_i), writes=[("qk", _i, "aug")])
        z_proj(l, 1)
        fox_gates_T()
        for c in range(4):
            wq_, wqk = wload_chunk(l, [(BQ + c * 128, 128)])
            proj_fm(wq_, wqk, 128, lambda p, pkey, s0, n: evac_pair(p, pkey, s0, n, qk[0], ("qk", 0), qk[1], ("qk", 1)))
            wk_, wkk = wload_chunk(l, [(BK + c * 128, 128)])
            proj_fm(wk_, wkk, 128, lambda p, pkey, s0, n: evac_pair(p, pkey, s0, n, qk[2], ("qk", 2), qk[3], ("qk", 3)))
            wv_, wvk = wload_chunk(l, [(BV + c * 128, 128)])
            v_proj_tm(lambda k: wv_[:, k, :], wvk, lambda k, b: hT[:, k, b * 128:(b + 1) * 128],
                      lambda b: [("hT", b)], 8)
            for j in range(2):
                h = 2 * c + j
                for t_ in range(2):
                    S.dma("sp", (lambda j, h, t_: lambda e: e.dma_start(out=qk[j][64 + t_:65 + t_, :],
                                                                        in_=fq[h:h + 1, t_, :]))(j, h, t_),
                          reads=S3K, writes=[("qk", j, "aug")])
                S.op("pool", (lambda j: lambda e: e.memset(qk[2 + j][64:66, :], 1.0))(j), writes=[("qk", 2 + j, "aug")])
            for j in range(2):
                h = 2 * c + j
                attn_dense(j, 128, 0.125,
                           lambda kb, h=h: (GT[:, kb * 8 + h:kb * 8 + h + 1], ["GT"]))
            finish_pair(c)
            if c == 1:
                load_wbr(l, 1)
        branch_final(l, 1, first)

    def mixer_C(l, first=False):
        z_proj(l, 2)
        S.dma("pool", lambda e: e.dma_start(out=wuq_t[:, :, 0:768], in_=w_uq[l].rearrange("(c p) n -> p c n", p=128)),
              writes=["wuq"])
        S.dma("pool", lambda e: e.dma_start(out=wukv_t[:], in_=w_ukv[l]), writes=["wukv"])
        uqv = w_uq[l].rearrange("(c p) (h r) -> p c h r", p=128, r=96)
        with nc.allow_non_contiguous_dma(reason="rope column permutation"):
            for c2 in range(2):
                S.dma("pool", (lambda c2: lambda e: e.dma_start(out=wuqrot[:, c2, :, 64:80], in_=uqv[:, c2, :, 80:96]))(c2),
                      writes=["wuqrot"])
                S.dma("pool", (lambda c2: lambda e: e.dma_start(out=wuqrot[:, c2, :, 80:96], in_=uqv[:, c2, :, 64:80]))(c2),
                      writes=["wuqrot"])
            krv = w_in[l].rearrange("(kc p) n -> p kc n", p=128)
            S.dma("pool", lambda e: e.dma_start(out=wkr[:, :, 0, 64:96], in_=krv[:, :, CKR:CKR + 32]), writes=["wkr"])
            S.dma("pool", lambda e: e.dma_start(out=wkr[:, :, 1, 64:80], in_=krv[:, :, CKR + 16:CKR + 32]),
                  writes=["wkr"])
            S.dma("pool", lambda e: e.dma_start(out=wkr[:, :, 1, 80:96], in_=krv[:, :, CKR:CKR + 16]), writes=["wkr"])
        cqn = S1b[:, 0:2 * L].rearrange("p (c l) -> p c l", c=2)
        ckvn = S2b[:, 0:L]
        krope = S2b[:, L:2 * L]
        wcq = [wload_chunk(l, [(CQ + i * 128, 128)]) for i in range(2)]
        wckv = wload_chunk(l, [(CKV, 128)])
        for (s0, n) in TCH:
            hkeys = [("hT", b) for b in blocks_of(s0, n)]
            for grp, wl, nfeat in (("q", wcq, 256), ("kv", [wckv], 128)):
                raws = []
                for gi_, (wt, wkey) in enumerate(wl):
                    p, pkey = st_next()
                    for k in range(8):
                        mm(p[:, 0:n], wt[:, k, :], hT[:, k, s0:s0 + n], k == 0, k == 7, [wkey] + hkeys, [pkey])
                    ri = gi_ if grp == "q" else 2
                    raw = S3[:, ri * 512:ri * 512 + n]
                    rkey = ("S3", "raw%d" % ri)
                    S.op("act", (lambda raw, p, n: lambda e: e.copy(out=raw, in_=p[:, 0:n]))(raw, p, n),
                         reads=[pkey, "S3"], writes=[rkey])
                    raws.append((raw, rkey, gi_))
                si = nxt("st", 2)
                sps = stp[si]
                skey = ("st", si)
                for idx, (raw, rkey, gi_) in enumerate(raws):
                    sq = S3[:, 1536:1536 + n]
                    S.op("act", (lambda sq, raw: lambda e: e.activation(out=sq, in_=raw, func=AF.Square))(sq, raw),
                         reads=[rkey, "S3"], writes=[("S3", "sq")])
                    mm(sps[:, 0:n], onesf[:], sq, idx == 0, idx == len(raws) - 1, [("S3", "sq"), "onesf"], [skey])
                S.op("act", (lambda sps, n, nfeat: lambda e: e.activation(
                    out=rstdb[:, 0:n], in_=sps[:, 0:n], func=AF.Ln, bias=eps_c[:], scale=1.0 / nfeat))(sps, n, nfeat),
                    reads=[skey, "eps_c"], writes=["rstdb"])
                S.op("act", (lambda n: lambda e: e.activation(out=rstdb[:, 0:n], in_=rstdb[:, 0:n], func=AF.Exp,
                                                              scale=-0.5))(n),
                     reads=["rstdb"], writes=["rstdb"])
                for (raw, rkey, gi_) in raws:
                    if grp == "q":
                        dst, dkey, gsc = cqn[:, gi_, s0:s0 + n], ("S1", "cqn%d" % gi_), gq[:, l, gi_:gi_ + 1]
                        kk_ = [dkey, ]
                        rd = ["S1"]
                    else:
                        dst, dkey, gsc = ckvn[:, s0:s0 + n], ("S2", "ckvn"), gkv[:, l:l + 1]
                        rd = ["S2"]
                    S.op("dve", (lambda dst, raw, gsc, n: lambda e: e.scalar_tensor_tensor(
                        out=dst, in0=raw, scalar=gsc, in1=rstdb[:, 0:n], op0=ALU.mult, op1=ALU.mult))(dst, raw, gsc, n),
                        reads=[rkey, "rstdb", "gq", "gkv", "S3"] + rd, writes=[(dkey[0], dkey[1], s0)])
            pa, pa_i = st_next()
            for k in range(8):
                mm(pa[:, 0:n], wkr[:, k, 0, :], hT[:, k, s0:s0 + n], k == 0, k == 7, ["wkr"] + hkeys, [pa_i])
            pb, pb_i = st_next()
            for k in range(8):
                mm(pb[:, 0:n], wkr[:, k, 1, :], hT[:, k, s0:s0 + n], k == 0, k == 7, ["wkr"] + hkeys, [pb_i])
            rope_combine(pa, pa_i, pb, pb_i, krope, s0, n, [("S2", "krope", s0)], ["S2"])
        for c in range(4):
            for j in range(2):
                h = 2 * c + j
                for (s0, n) in TCH:
                    ckeys = [("S1", "cqn0", s0), ("S1", "cqn1", s0), "S1"]
                    p1, p1i = st_next()
                    for k in range(2):
                        mm(p1[:, 0:n], wuq_t[:, k, h * 96:h * 96 + 128], cqn[:, k, s0:s0 + n], k == 0, k == 1,
                           ["wuq"] + ckeys, [p1i])
                    p2, p2i = st_next()
                    for k in range(2):
                        mm(p2[:, 0:n], wuqrot[:, k, h, :], cqn[:, k, s0:s0 + n], k == 0, k == 1,
                           ["wuqrot"] + ckeys, [p2i])
                    copy_op("act", qk[j][0:64, s0:s0 + n], p1[0:64, 0:n], [p1i],
                            [("qk", j, b) for b in blocks_of(s0, n)])
                    rope_combine(p1, p1i, p2, p2i, qk[j], s0, n,
                                 [("qk", j, "aug")], [])
                    p3, p3i = st_next()
                    mm(p3[:, 0:n], wukv_t[:, h * 128:h * 128 + 128], ckvn[:, s0:s0 + n], True, True,
                       ["wukv", ("S2", "ckvn", s0), "S2"], [p3i])
                    copy_op(evac_eng(), qk[2 + j][0:64, s0:s0 + n], p3[0:64, 0:n], [p3i],
                            [("qk", 2 + j, b) for b in blocks_of(s0, n)])
                S.dma("sp", (lambda j: lambda e: e.dma_start(out=qk[2 + j][64:96, :], in_=krope[64:96, :]))(j),
                      reads=[("S2", "krope", s0) for (s0, n) in TCH] + ["S2"], writes=[("qk", 2 + j, "aug")])
            vview = wukv_t[:].rearrange("p (h t d) -> p h t d", h=8, t=2)[:, 2 * c:2 * c + 2, 1, :]
            v_proj_tm(lambda k: vview, "wukv", lambda k, b: ckvn[:, b * 128:(b + 1) * 128],
                      lambda b: [("S2", "ckvn", s0) for (s0, n) in TCH if s0 <= b * 128 < s0 + n] + ["S2"], 1)
            for j in range(2):
                attn_dense(j, 128, 96 ** -0.5, lambda kb: (None, []))
            finish_pair(c)
            if c == 1:
                load_wbr(l, 2)
            if c == 2:
                load_wout_hi(l)
                load_wout_lo(l)
        branch_final(l, 2, first)

    def rope_combine(pa, pakey, pb, pbkey, dst_tile, s0, n, wkeys, rkeys):
        r1 = nxt("tmpf", 2)
        t1 = tmpf[r1]
        S.op("dve", lambda e: e.tensor_tensor(out=t1[64:96, 0:n], in0=pa[64:96, 0:n], in1=ropeT[64:96, 0, s0:s0 + n],
                                              op=ALU.mult), reads=[pakey, "ropeT"], writes=[("tmpf", r1)])
        r2 = nxt("tmpf", 2)
        t2 = tmpf[r2]
        S.op("dve", lambda e: e.tensor_tensor(out=t2[64:96, 0:n], in0=pb[64:96, 0:n], in1=ropeT[64:96, 1, s0:s0 + n],
                                              op=ALU.mult), reads=[pbkey, "ropeT"], writes=[("tmpf", r2)])
        S.op("dve", lambda e: e.tensor_tensor(out=dst_tile[64:96, s0:s0 + n], in0=t1[64:96, 0:n], in1=t2[64:96, 0:n],
                                              op=ALU.add), reads=[("tmpf", r1), ("tmpf", r2)] + rkeys, writes=wkeys)

    for sidx in range(NSEQ):
        first_norm(sidx)
        for l in range(NLAYER):
            mixer_A(l, True)
            mixer_B(l, False)
            mixer_C(l, False)
            wout_phase(l, sidx)

    with nc.allow_non_contiguous_dma(reason='small strided parameter / permuted weight loads'):
        S.emit()
    st.close()
    return nc, S


_CACHE = {}


def kernel(x, meta_tokens, norm_g, w_in, b_f, sinks, q_norm_g, kv_norm_g, w_uq, w_ukv, w_br, w_out, final_norm_g):
    x = np.asarray(x, np.float32)
    B, SEQ, _ = x.shape
    NCORE = 8
    NSEQ = B // NCORE
    NB = SEQ // 128 + 1
    L = NB * 128
    key = (NSEQ, NB)
    if key not in _CACHE:
        _CACHE[key] = build(NSEQ, DEPTH, NB)
    nc, _ = _CACHE[key]
    consts = make_consts(NB)
    xp = np.zeros((B, L, D), np.float32)
    xp[:, 128 - N_META:128, :] = np.asarray(meta_tokens, np.float32)[None]
    xp[:, 128:, :] = x
    shared = {
        "norm_g": np.asarray(norm_g, np.float32), "w_in": np.asarray(w_in, np.float32),
        "b_f": np.asarray(b_f, np.float32), "sinks": np.asarray(sinks, np.float32),
        "q_norm_g": np.asarray(q_norm_g, np.float32), "kv_norm_g": np.asarray(kv_norm_g, np.float32),
        "w_uq": np.asarray(w_uq, np.float32), "w_ukv": np.asarray(w_ukv, np.float32),
        "w_br": np.asarray(w_br, np.float32), "w_out": np.asarray(w_out, np.float32),
        "final_norm_g": np.asarray(final_norm_g, np.float32).reshape(1, D),
    }
    shared.update(consts)
    in_maps = []
    for c in range(NCORE):
        m = dict(shared)
        m["xin"] = np.ascontiguousarray(xp[c * NSEQ:(c + 1) * NSEQ])
        in_maps.append(m)
    res = run_bass_kernel_spmd(nc, in_maps, core_ids=list(range(NCORE)))
    outs = [np.asarray(r["out"], np.float32) for r in res.results]
    return np.concatenate(outs, axis=0)
```
